# Optimizing a Trainium2 kernel written in Bass

```python
import jax, jax.numpy as jnp
from jax import lax
import numpy as np

D_MODEL = 1024
BATCH = 8
SEQ = 4096
DEPTH = 1

CHUNK = 64
D_RNN = 1024
RNN_BLOCKS = 16
RNN_BLOCK_DIM = D_RNN // RNN_BLOCKS
CONV_WIDTH = 4
LRU_C = 8.0
N_HEADS = 16
N_KV_HEADS = 4
HEAD_DIM = 64
D_ATTN = N_HEADS * HEAD_DIM
D_KV = N_KV_HEADS * HEAD_DIM
IDX_HEADS = 8
IDX_DIM = 64
TOPK_MAX = 256
D_FF = 4 * D_MODEL
N_BRANCHES = 2
EPS = 1e-6
SPLITS = (D_RNN, D_RNN, D_ATTN, D_KV, D_KV, IDX_HEADS * IDX_DIM, IDX_DIM, IDX_HEADS, N_BRANCHES * D_MODEL)
D_IN = D_RNN + D_RNN + D_ATTN + D_KV + D_KV + IDX_HEADS * IDX_DIM + IDX_DIM + IDX_HEADS + N_BRANCHES * D_MODEL

kernel_name = "hybrid_rglru_dsa_block"


def rms_norm(x, w):
    xf = x.astype(jnp.float32)
    y = xf * lax.rsqrt(jnp.mean(xf * xf, axis=-1, keepdims=True) + EPS)
    return (y * w.astype(jnp.float32)).astype(x.dtype)


def causal_depthwise_conv(x, w, b):
    T = x.shape[1]
    xp = jnp.pad(x, ((0, 0), (CONV_WIDTH - 1, 0), (0, 0)))
    out = b
    for j in range(CONV_WIDTH):
        out = out + xp[:, j:j + T] * w[j]
    return out


def block_diag_linear(x, w, b):
    xb = x.reshape(x.shape[:-1] + (RNN_BLOCKS, RNN_BLOCK_DIM))
    y = jnp.einsum('btnd,nde->btne', xb, w)
    return y.reshape(x.shape) + b


def rg_lru(x, wa, ba, wx, bx, lam):
    r = jax.nn.sigmoid(block_diag_linear(x, wa, ba).astype(jnp.float32))
    i = jax.nn.sigmoid(block_diag_linear(x, wx, bx).astype(jnp.float32))
    log_a = -LRU_C * r * jax.nn.softplus(-lam.astype(jnp.float32))
    a = jnp.exp(log_a)
    mult = jnp.sqrt(-jnp.expm1(2.0 * log_a))
    u = mult * (i * x.astype(jnp.float32))

    def combine(c1, c2):
        a1, b1 = c1
        a2, b2 = c2
        return a1 * a2, a2 * b1 + b2

    _, h = lax.associative_scan(combine, (a, u), axis=1)
    return h.astype(x.dtype)


def dsa_attention(q, k, v, iq, ik, iw):
    B, T = q.shape[0], q.shape[1]
    n_chunks = T // CHUNK
    top_k = min(TOPK_MAX, T // 4)
    group = N_HEADS // N_KV_HEADS
    scale = HEAD_DIM ** -0.5
    idx_scale = (IDX_DIM ** -0.5) * (IDX_HEADS ** -0.5)
    key_pos = jnp.arange(T)

    def one_chunk(c):
        start = c * CHUNK
        qc = lax.dynamic_slice_in_dim(q, start, CHUNK, axis=1)
        iqc = lax.dynamic_slice_in_dim(iq, start, CHUNK, axis=1)
        iwc = lax.dynamic_slice_in_dim(iw, start, CHUNK, axis=1)
        limit = start + CHUNK
        visible = key_pos < limit
        rel = jax.nn.relu(jnp.einsum('bqhd,bsd->bqhs', iqc, ik).astype(jnp.float32))
        score = jnp.einsum('bqhs,bqh->bqs', rel, iwc.astype(jnp.float32)) * idx_scale
        score = jnp.where(visible, score, -jnp.inf)
        _, idx = lax.top_k(score, top_k)
        valid = idx < limit
        k_sel = jax.vmap(lambda kb, ib: kb[ib])(k, idx)
        v_sel = jax.vmap(lambda vb, ib: vb[ib])(v, idx)
        qg = qc.reshape(B, CHUNK, N_KV_HEADS, group, HEAD_DIM)
        s = jnp.einsum('bqgrd,bqkgd->bqgrk', qg, k_sel).astype(jnp.float32) * scale
        s = jnp.where(valid[:, :, None, None, :], s, -jnp.inf)
        p = jax.nn.softmax(s, axis=-1).astype(v.dtype)
        o = jnp.einsum('bqgrk,bqkgd->bqgrd', p, v_sel)
        return o.reshape(B, CHUNK, D_ATTN)

    out = lax.map(one_chunk, jnp.arange(n_chunks))
    return out.transpose(1, 0, 2, 3).reshape(B, T, D_ATTN)


def hybrid_layer(x, norm1_w, w_in, conv_w, conv_b, rg_wa, rg_ba, rg_wx, rg_bx, rg_lambda,
                 q_norm_w, k_norm_w, idx_k_norm_w, w_o_rnn, w_o_attn, gate_b, w_out,
                 norm2_w, w_ff_in, w_ff_out):
    B, T, _ = x.shape
    h = rms_norm(x, norm1_w)
    u = h @ w_in
    offsets = np.cumsum(SPLITS)[:-1].tolist()
    rnn_x, rnn_gate, q, k, v, iq, ik, iw, g = jnp.split(u, offsets, axis=-1)

    xc = causal_depthwise_conv(rnn_x, conv_w, conv_b)
    y_rnn = rg_lru(xc, rg_wa, rg_ba, rg_wx, rg_bx, rg_lambda) * jax.nn.gelu(rnn_gate)

    q = rms_norm(q.reshape(B, T, N_HEADS, HEAD_DIM), q_norm_w)
    k = rms_norm(k.reshape(B, T, N_KV_HEADS, HEAD_DIM), k_norm_w)
    v = v.reshape(B, T, N_KV_HEADS, HEAD_DIM)
    iq = iq.reshape(B, T, IDX_HEADS, IDX_DIM)
    ik = rms_norm(ik, idx_k_norm_w)
    y_attn = dsa_attention(q, k, v, iq, ik, iw)

    gates = jax.nn.sigmoid((g + gate_b.reshape(-1)).astype(jnp.float32)).astype(x.dtype)
    gates = gates.reshape(B, T, N_BRANCHES, D_MODEL)
    merged = gates[:, :, 0] * (y_rnn @ w_o_rnn) + gates[:, :, 1] * (y_attn @ w_o_attn)
    x = x + merged @ w_out

    h2 = rms_norm(x, norm2_w)
    x = x + jnp.square(jax.nn.relu(h2 @ w_ff_in)) @ w_ff_out
    return x


def setup_inputs(seed: int = 0) -> dict:
    key = jax.random.key(seed)
    ks = jax.random.split(key, 24)
    f32 = jnp.float32

    def nrm(k, shape, scale):
        return jax.random.normal(k, shape, f32) * scale

    a0 = jax.random.uniform(ks[9], (DEPTH, D_RNN), f32, 0.9, 0.999) ** (1.0 / LRU_C)
    rg_lambda = jnp.log(a0) - jnp.log1p(-a0)
    return {
        "x": nrm(ks[0], (BATCH, SEQ, D_MODEL), 1.0),
        "norm1_w": 1.0 + nrm(ks[1], (DEPTH, D_MODEL), 0.02),
        "w_in": nrm(ks[2], (DEPTH, D_MODEL, D_IN), D_MODEL ** -0.5),
        "conv_w": nrm(ks[3], (DEPTH, CONV_WIDTH, D_RNN), CONV_WIDTH ** -0.5),
        "conv_b": nrm(ks[4], (DEPTH, D_RNN), 0.02),
        "rg_wa": nrm(ks[5], (DEPTH, RNN_BLOCKS, RNN_BLOCK_DIM, RNN_BLOCK_DIM), RNN_BLOCK_DIM ** -0.5),
        "rg_ba": nrm(ks[6], (DEPTH, D_RNN), 0.02),
        "rg_wx": nrm(ks[7], (DEPTH, RNN_BLOCKS, RNN_BLOCK_DIM, RNN_BLOCK_DIM), RNN_BLOCK_DIM ** -0.5),
        "rg_bx": nrm(ks[8], (DEPTH, D_RNN), 0.02),
        "rg_lambda": rg_lambda,
        "q_norm_w": 1.0 + nrm(ks[10], (DEPTH, HEAD_DIM), 0.02),
        "k_norm_w": 1.0 + nrm(ks[11], (DEPTH, HEAD_DIM), 0.02),
        "idx_k_norm_w": 1.0 + nrm(ks[12], (DEPTH, IDX_DIM), 0.02),
        "w_o_rnn": nrm(ks[13], (DEPTH, D_RNN, D_MODEL), D_RNN ** -0.5),
        "w_o_attn": nrm(ks[14], (DEPTH, D_ATTN, D_MODEL), D_ATTN ** -0.5),
        "gate_b": nrm(ks[15], (DEPTH, N_BRANCHES, D_MODEL), 0.02),
        "w_out": nrm(ks[16], (DEPTH, D_MODEL, D_MODEL), D_MODEL ** -0.5),
        "norm2_w": 1.0 + nrm(ks[17], (DEPTH, D_MODEL), 0.02),
        "w_ff_in": nrm(ks[18], (DEPTH, D_MODEL, D_FF), D_MODEL ** -0.5),
        "w_ff_out": nrm(ks[19], (DEPTH, D_FF, D_MODEL), D_FF ** -0.5),
    }


def reference(x, norm1_w, w_in, conv_w, conv_b, rg_wa, rg_ba, rg_wx, rg_bx, rg_lambda,
              q_norm_w, k_norm_w, idx_k_norm_w, w_o_rnn, w_o_attn, gate_b, w_out,
              norm2_w, w_ff_in, w_ff_out):
    for l in range(DEPTH):
        x = hybrid_layer(x, norm1_w[l], w_in[l], conv_w[l], conv_b[l], rg_wa[l], rg_ba[l],
                         rg_wx[l], rg_bx[l], rg_lambda[l], q_norm_w[l], k_norm_w[l],
                         idx_k_norm_w[l], w_o_rnn[l], w_o_attn[l], gate_b[l], w_out[l],
                         norm2_w[l], w_ff_in[l], w_ff_out[l])
    return x
```

```python
import types
import numpy as np
from contextlib import ExitStack
import concourse.bass as bass
import concourse.mybir as mybir
from concourse.bass_utils import run_bass_kernel_spmd

F32 = mybir.dt.float32
BF16 = mybir.dt.bfloat16
AF = mybir.ActivationFunctionType
ALU = mybir.AluOpType
AX = mybir.AxisListType

ENGS = ("tensor", "vector", "scalar", "gpsimd", "sync")
DMA_POOL = 20
D = 1024
NIT = 16
TOPK = 256
NEG = -1024.0
EPS = 1e-6


class Op:
    __slots__ = ("eng", "fn", "reads", "writes", "dma", "deps", "signal", "ev", "idx")

    def __init__(self, eng, fn, reads, writes, dma):
        self.eng = eng
        self.fn = fn
        self.reads = reads
        self.writes = writes
        self.dma = dma
        self.deps = []
        self.signal = False
        self.ev = None


def _freeze(fn):
    if fn.__closure__ is None:
        return fn
    cells = []
    for c in fn.__closure__:
        try:
            cells.append(types.CellType(c.cell_contents))
        except ValueError:
            cells.append(c)
    return types.FunctionType(fn.__code__, fn.__globals__, fn.__name__, fn.__defaults__, tuple(cells))


class Prog:
    def __init__(self, nc, stack):
        self.nc = nc
        self.stack = stack
        self.ops = []

    def op(self, eng, fn, reads=(), writes=(), dma=False):
        reads = tuple(reads)
        writes = tuple(writes) + tuple(r for r in reads if r.startswith("@"))
        reads = tuple(r for r in reads if not r.startswith("@"))
        o = Op(eng, _freeze(fn), reads + ("PHASE",), writes, dma)
        o.idx = len(self.ops)
        self.ops.append(o)
        return o

    def pe(self, fn, reads=(), writes=()):
        return self.op("tensor", fn, reads, writes)

    def dve(self, fn, reads=(), writes=()):
        return self.op("vector", fn, reads, writes)

    def act(self, fn, reads=(), writes=()):
        return self.op("scalar", fn, reads, writes)

    def pool(self, fn, reads=(), writes=()):
        return self.op("gpsimd", fn, reads, writes)

    def dma(self, out, in_, reads=(), writes=(), eng="sync"):
        return self.op(eng, lambda e: e.dma_start(out=out, in_=in_), reads, writes, dma=True)

    def barrier(self, scratch):
        o = Op("vector", lambda e: e.memset(scratch, 0.0), (), ("PHASE",), False)
        o.idx = len(self.ops)
        self.ops.append(o)

    def emit(self):
        nc = self.nc
        last_writer = {}
        readers = {}
        for o in self.ops:
            deps = set()
            for r in o.reads:
                w = last_writer.get(r)
                if w is not None:
                    deps.add(w.idx)
            for wtok in o.writes:
                w = last_writer.get(wtok)
                if w is not None:
                    deps.add(w.idx)
                for rd in readers.get(wtok, ()):
                    deps.add(rd.idx)
            deps.discard(o.idx)
            final = []
            for d in sorted(deps):
                p = self.ops[d]
                if (not p.dma) and (not o.dma) and p.eng == o.eng and o.eng == "tensor":
                    raw = any(last_writer.get(r) is p for r in o.reads)
                    if not raw:
                        continue
                final.append(p)
                p.signal = True
            o.deps = final
            for r in o.reads:
                lst = readers.setdefault(r, [])
                if not o.dma:
                    lst[:] = [x for x in lst if x.dma or x.eng != o.eng]
                lst.append(o)
            for wtok in o.writes:
                last_writer[wtok] = o
                readers[wtok] = []

        sems = {}
        counts = {}
        for e in ENGS:
            sems[e] = self.stack.enter_context(nc.semaphore("s_" + e))
            counts[e] = 0
        dpool = {}
        dcount = {}
        for q in ("sync", "gpsimd"):
            dpool[q] = [self.stack.enter_context(nc.semaphore("d_%s_%d" % (q, i))) for i in range(DMA_POOL)]
            dcount[q] = 0
        seen = {e: {} for e in ENGS}

        def wait(engname, sem, val):
            key = id(sem)
            if seen[engname].get(key, 0) >= val:
                return
            seen[engname][key] = val
            getattr(nc, engname).wait_ge(sem, val)

        for o in self.ops:
            eng = getattr(nc, o.eng)
            for p in o.deps:
                sem, val = p.ev
                wait(o.eng, sem, val)
            if o.dma:
                n = dcount[o.eng]
                dcount[o.eng] += 1
                sem = dpool[o.eng][n % DMA_POOL]
                prev = 16 * (n // DMA_POOL)
                if prev > 0:
                    wait(o.eng, sem, prev)
                inst = o.fn(eng)
                inst.then_inc(sem, 16)
                o.ev = (sem, prev + 16)
            else:
                inst = o.fn(eng)
                if o.signal:
                    counts[o.eng] += 1
                    inst.then_inc(sems[o.eng], 1)
                    o.ev = (sems[o.eng], counts[o.eng])
        for q in dpool:
            n = dcount[q]
            for i in range(min(n, DMA_POOL)):
                k = n - 1 - i
                sem = dpool[q][k % DMA_POOL]
                wait("sync", sem, 16 * (k // DMA_POOL + 1))
        self.counts = counts
        self.dcount = dcount


def chunk_list():
    ch = []
    for i in range(8):
        ch.append(("q", i))
    for i in range(2):
        ch.append(("k", i))
    for i in range(2):
        ch.append(("ks", i))
    for i in range(4):
        ch.append(("iq", i))
    for i in range(16):
        ch.append(("g", i))
    ch.append(("ik", 0))
    ch.append(("pad", 0))
    for c in range(8):
        ch.append(("rx", c))
        ch.append(("gate", c))
    return ch


V_N1, V_N2, V_CW, V_CB, V_BA, V_BX, V_LAM, V_QW, V_KW, V_IKW, V_GB = 0, 8, 16, 48, 56, 64, 72, 80, 81, 82, 83
NV = 100


def prep_shared(inp):
    w_in = np.asarray(inp["w_in"][0], np.float32)
    cols = []
    for kind, i in chunk_list():
        if kind == "q":
            cols.append(np.arange(2048 + 128 * i, 2048 + 128 * (i + 1)))
        elif kind == "k":
            cols.append(np.arange(3072 + 128 * i, 3072 + 128 * (i + 1)))
        elif kind == "ks":
            b = 3072 + 128 * i
            cols.append(np.concatenate([np.arange(b + 64, b + 128), np.arange(b, b + 64)]))
        elif kind in ("ik", "pad"):
            cols.append(np.concatenate([np.arange(4096, 4160), np.arange(4096, 4160)]))
        elif kind == "iq":
            cols.append(np.arange(3584 + 128 * i, 3584 + 128 * (i + 1)))
        elif kind == "g":
            cols.append(np.arange(4168 + 128 * i, 4168 + 128 * (i + 1)))
        elif kind == "rx":
            cols.append(np.arange(128 * i, 128 * (i + 1)))
        elif kind == "gate":
            cols.append(np.arange(1024 + 128 * i, 1024 + 128 * (i + 1)))
    cols = np.concatenate(cols)
    w_fm = np.ascontiguousarray(w_in[:, cols])
    w_tm = np.ascontiguousarray(np.concatenate([w_in[:, 3328:3584], w_in[:, 4160:4168]], axis=1))

    vec = np.zeros((128, NV), np.float32)

    def colmajor(v):
        return np.asarray(v, np.float32).reshape(8, 128).T

    vec[:, V_N1:V_N1 + 8] = colmajor(inp["norm1_w"][0])
    vec[:, V_N2:V_N2 + 8] = colmajor(inp["norm2_w"][0])
    for j in range(4):
        vec[:, V_CW + 8 * j:V_CW + 8 * j + 8] = colmajor(inp["conv_w"][0][j])
    vec[:, V_CB:V_CB + 8] = colmajor(inp["conv_b"][0])
    vec[:, V_BA:V_BA + 8] = colmajor(inp["rg_ba"][0])
    vec[:, V_BX:V_BX + 8] = colmajor(inp["rg_bx"][0])
    vec[:, V_LAM:V_LAM + 8] = colmajor(inp["rg_lambda"][0])
    vec[:, V_QW] = np.tile(np.asarray(inp["q_norm_w"][0], np.float32), 2)
    vec[:, V_KW] = np.tile(np.asarray(inp["k_norm_w"][0], np.float32), 2)
    vec[:, V_IKW] = np.tile(np.asarray(inp["idx_k_norm_w"][0], np.float32), 2)
    vec[:, V_GB:V_GB + 16] = np.asarray(inp["gate_b"][0], np.float32).reshape(16, 128).T

    def blockdiag(w):
        w = np.asarray(w, np.float32)
        o = np.zeros((128, 8, 128), np.float32)
        for c in range(8):
            o[0:64, c, 0:64] = w[2 * c]
            o[64:128, c, 64:128] = w[2 * c + 1]
        return o

    ident = np.eye(128, dtype=np.float32)
    onesblk = np.zeros((128, 128), np.float32)
    onesblk[0:64, 0:64] = 1.0
    onesblk[64:128, 64:128] = 1.0
    ptab = np.zeros((128, 2 * NIT), np.float32)
    for k in range(NIT):
        ptab[:, k] = 2.0 ** (-(k + 2))
        ptab[:, NIT + k] = 2.0 ** (-(k + 1))
    import ml_dtypes
    bf = ml_dtypes.bfloat16
    shared = {
        "w_fm": w_fm,
        "w_tm": w_tm,
        "vec": vec,
        "wa_bd": blockdiag(inp["rg_wa"][0]),
        "wx_bd": blockdiag(inp["rg_wx"][0]),
        "w_o_rnn": np.ascontiguousarray(np.asarray(inp["w_o_rnn"][0], np.float32)),
        "w_o_attn": np.ascontiguousarray(np.asarray(inp["w_o_attn"][0], np.float32)),
        "w_out": np.ascontiguousarray(np.asarray(inp["w_out"][0], np.float32)),
        "w_ff_in": np.ascontiguousarray(np.asarray(inp["w_ff_in"][0], np.float32)),
        "w_ff_out": np.ascontiguousarray(np.asarray(inp["w_ff_out"][0], np.float32)),
        "ident_bf": ident.astype(bf),
        "e4_bf": np.tile(ident, (1, 4)).astype(bf),
        "onesblk": onesblk,
        "ptab": ptab,
    }
    return shared


def build(T, dbg=(), stop=None):
    NT = T // 512
    NP = T // 128
    nc = bass.Bass("TRN2", target_bir_lowering=False)
    es = ExitStack()
    P = Prog(nc, es)

    def din(name, shape, dt=F32):
        return nc.dram_tensor(name, list(shape), dt, kind="ExternalInput").ap()

    x_d = din("x", [T, D])
    wfm_d = din("w_fm", [D, 6400])
    wtm_d = din("w_tm", [D, 264])
    vec_d = din("vec", [128, NV])
    wabd_d = din("wa_bd", [128, 8, 128])
    wxbd_d = din("wx_bd", [128, 8, 128])
    wor_d = din("w_o_rnn", [D, D])
    woa_d = din("w_o_attn", [D, D])
    wout_d = din("w_out", [D, D])
    w1_d = din("w_ff_in", [D, 4 * D])
    w2_d = din("w_ff_out", [4 * D, D])
    ident_d = din("ident_bf", [128, 128], BF16)
    e4_d = din("e4_bf", [128, 512], BF16)
    onesblk_d = din("onesblk", [128, 128])
    ptab_d = din("ptab", [128, 2 * NIT])
    out_d = nc.dram_tensor("out", [T, D], F32, kind="ExternalOutput").ap()

    qT_d = nc.dram_tensor("qT_s", [8, 128, T], BF16).ap()
    iqT_d = nc.dram_tensor("iqT_s", [4, 128, T], BF16).ap()
    gT_d = nc.dram_tensor("gT_s", [16, 128, T], BF16).ap()
    yrT_d = nc.dram_tensor("yrT_s", [8, 128, T], BF16).ap()
    yaT_d = nc.dram_tensor("yaT_s", [8, 128, T], BF16).ap()
    x1_d = nc.dram_tensor("x1_s", [T, D], F32).ap()
    dbg_out = {}

    def sb(name, shape, dt, stack=None, side=None):
        if side is not None:
            return (stack or es).enter_context(nc.sbuf_tensor("s_" + name, list(shape), dt, side=side))
        return (stack or es).enter_context(nc.sbuf_tensor("s_" + name, list(shape), dt))

    def ps(name, shape, dt, stack):
        return stack.enter_context(nc.psum_tensor("p_" + name.replace("@", ""), list(shape), dt))

    vec = sb("vec", [128, NV], F32)
    ident = sb("ident", [128, 128], BF16)
    e4 = sb("e4", [128, 512], BF16)
    onesblk = sb("onesblk", [128, 128], F32)
    ptab = sb("ptab", [128, 2 * NIT], F32)
    dvec = sb("dvec", [128, 32], F32)
    P.dma(vec[:], vec_d, writes=["vec"])
    P.dma(ident[:], ident_d, writes=["ident"])
    P.dma(e4[:], e4_d, writes=["e4"])
    P.dma(onesblk[:], onesblk_d, writes=["onesblk"])
    P.dma(ptab[:], ptab_d, writes=["ptab"])
    P.dve(lambda e: e.tensor_scalar(out=dvec[:, 0:8], in0=vec[:, V_BA:V_BA + 8], scalar1=0.5, scalar2=None,
                                    op0=ALU.mult), reads=["vec"], writes=["dv0"])
    P.dve(lambda e: e.tensor_scalar(out=dvec[:, 8:16], in0=vec[:, V_BX:V_BX + 8], scalar1=0.5, scalar2=None,
                                    op0=ALU.mult), reads=["vec"], writes=["dv1"])
    P.act(lambda e: e.activation(out=dvec[:, 24:32], in_=vec[:, V_LAM:V_LAM + 8], func=AF.Exp, scale=-1.0),
          reads=["vec"], writes=["dv3"])
    P.act(lambda e: e.activation(out=dvec[:, 24:32], in_=dvec[:, 24:32], func=AF.Ln, bias=1.0),
          reads=["dv3"], writes=["dv3"])
    P.dve(lambda e: e.tensor_scalar(out=dvec[:, 16:24], in0=dvec[:, 24:32], scalar1=-4.0, scalar2=None,
                                    op0=ALU.mult), reads=["dv3"], writes=["dv2"])
    DV = ["dv0", "dv1", "dv2"]
    bscr = sb("bscr", [128, 8], F32)

    def norm_transpose(S, src_tile, src_tok, nsub, dst_fn, wcol0, tp, tag, cnt):
        ssq = S["ssq"][cnt % 2]
        junk = S["junk"]
        hb = S["hb"][cnt % 2]
        tk = "%s%d" % (tag, cnt % 2)
        for j in range(nsub):
            P.act(lambda e, j=j: e.activation(out=junk[:], in_=src_tile[:, j, :], func=AF.Square,
                                              accum_out=ssq[:, j:j + 1]),
                  reads=[src_tok], writes=["junkA", tk + "ssq%d" % j])
        P.act(lambda e: e.activation(out=ssq[:, 4:4 + nsub], in_=ssq[:, 0:nsub], func=AF.Sqrt, scale=1.0 / D,
                                     bias=EPS),
              reads=[tk + "ssq%d" % j for j in range(nsub)], writes=[tk + "sd"])
        P.dve(lambda e: e.reciprocal(out=ssq[:, 8:8 + nsub], in_=ssq[:, 4:4 + nsub]),
              reads=[tk + "sd"], writes=[tk + "rs"])
        for j in range(nsub):
            P.dve(lambda e, j=j: e.tensor_scalar(out=hb[:, j, :], in0=src_tile[:, j, :],
                                                 scalar1=ssq[:, 8 + j:9 + j], scalar2=None, op0=ALU.mult),
                  reads=[src_tok, tk + "rs"], writes=[tk + "hb%d" % j])
        nb = tp.shape[1]
        for kc in range(8):
            bank = kc % nb
            ptok = "@tp%s%d" % (tag, bank)
            for j in range(nsub):
                P.pe(lambda e, j=j, kc=kc, bank=bank: e.transpose(
                    out=tp[:, bank, j * 128:(j + 1) * 128],
                    in_=hb[:, j, kc * 128:(kc + 1) * 128], identity=ident[:]),
                    reads=[tk + "hb%d" % j, "ident"], writes=[ptok])
            dst, dtok = dst_fn(kc)
            src = tp[:, bank, 0:nsub * 128]
            if kc % 2 == 0:
                P.act(lambda e, dst=dst, src=src, kc=kc: e.activation(
                    out=dst, in_=src, func=AF.Copy, scale=vec[:, wcol0 + kc: wcol0 + kc + 1]),
                    reads=[ptok, "vec"], writes=[dtok])
            else:
                P.dve(lambda e, dst=dst, src=src, kc=kc: e.tensor_scalar(
                    out=dst, in0=src, scalar1=vec[:, wcol0 + kc: wcol0 + kc + 1], scalar2=None, op0=ALU.mult),
                    reads=[ptok, "vec"], writes=[dtok])

    att = ExitStack()
    attR = ExitStack()
    KT = sb("KT", [128, 2, T], BF16, attR, side="right")
    KTs = sb("KTs", [128, 2, T], BF16, attR, side="right")
    Vp = sb("Vp", [128, NP, 4, 65], BF16, att)
    ikT = sb("ikT", [128, T], BF16, attR, side="right")
    iw = sb("iw", [128, NP, 8], F32, att)
    absw = sb("absw", [128, NP, 8], F32, att)
    sgnw = sb("sgnw", [128, NP, 8], F32, att)

    pab = ExitStack()
    hT = sb("hT", [128, 8, T], BF16, pab)
    with ExitStack() as pa:
        S = {
            "ssq": [sb("ssq%d" % i, [128, 12], F32, pa) for i in range(2)],
            "junk": sb("junkA", [128, 1024], BF16, pa),
            "hb": [sb("hb%d" % i, [128, 4, 1024], BF16, pa) for i in range(2)],
        }
        xs = [sb("xs%d" % i, [128, 4, 1024], F32, pa) for i in range(2)]
        tp = ps("tpA", [128, 4, 1024], BF16, pa)
        for g4 in range(NT):
            xt = xs[g4 % 2]
            xtok = "xs%d" % (g4 % 2)
            P.dma(xt[:], x_d[g4 * 512:(g4 + 1) * 512, :].rearrange("(j p) d -> p j d", p=128), writes=[xtok])
            norm_transpose(S, xt, xtok, 4,
                           lambda kc, g4=g4: (hT[:, kc, g4 * 512:(g4 + 1) * 512], "hT%d_%d" % (kc, g4)),
                           V_N1, tp, "A", g4)
    P.barrier(bscr[:, 0:1])
    if stop == "A":
        d_ = nc.dram_tensor("dbg_hT", [128, 8, T], BF16, kind="ExternalOutput").ap()
        P.dma(d_, hT[:], reads=["hT%d_%d" % (kc, g4) for kc in range(8) for g4 in range(NT)], writes=["dbg_hT"])
        P.emit()
        return nc, P
    HT_ALL = lambda tt: ["hT%d_%d" % (kc, tt) for kc in range(8)]

    chunks = chunk_list()
    NCH = len(chunks)
    with ExitStack() as pb:
        wbuf = [sb("wb%d" % i, [128, 8, 256], BF16, pb) for i in range(2)]
        wtm = sb("wtm", [128, 8, 264], BF16, pb)
        wabd = sb("wabd", [128, 8, 128], BF16, pb)
        wxbd = sb("wxbd", [128, 8, 128], BF16, pb)
        P.dma(wtm[:], wtm_d.rearrange("(kc p) c -> p kc c", p=128), writes=["wtm"], eng="gpsimd")
        P.dma(wabd[:], wabd_d, writes=["wabd"], eng="gpsimd")
        P.dma(wxbd[:], wxbd_d, writes=["wxbd"], eng="gpsimd")
        pacc = [ps("@pacc%d" % i, [128, 512], F32, pb) for i in range(3)]
        state = {"acc": 0, "hn": 0, "ba": 0, "g": 0}

        def load_w(grp):
            c0 = grp * 256
            wb = wbuf[grp % 2]
            P.dma(wb[:], wfm_d[:, c0:c0 + 256].rearrange("(kc p) c -> p kc c", p=128),
                  writes=["wb%d" % (grp % 2)], eng="gpsimd")

        def project(ci, tt):
            grp, loc = ci // 2, ci % 2
            wb = wbuf[grp % 2]
            a = state["acc"] % 3
            state["acc"] += 1
            pt = pacc[a]
            for kc in range(8):
                P.pe(lambda e: e.matmul(
                    pt[:], lhsT=wb[:, kc, loc * 128:(loc + 1) * 128], rhs=hT[:, kc, tt * 512:(tt + 1) * 512],
                    start=(kc == 0), stop=(kc == 7)),
                    reads=["wb%d" % (grp % 2), "hT%d_%d" % (kc, tt)], writes=["@pacc%d" % a])
            return pt, "@pacc%d" % a

        with ExitStack() as pb1:
            pst = [ps("@pst%d" % i, [128, 512], F32, pb1) for i in range(2)]
            hn0 = {k: sb("hn_%s" % k, [128, 512], F32, pb1) for k in ("sq", "qs", "sd")}
            hst = [sb("hst%d" % i, [128, 512], BF16, pb1) for i in range(2)]
            gst = [sb("gst%d" % i, [128, 512], BF16, pb1) for i in range(2)]

            def headnorm(pt, ptok, wcol, dst, dtok):
                i = state["hn"] % 2
                state["hn"] += 1
                B = hn0
                t = "hn0"
                P.act(lambda e: e.activation(out=B["sq"][:], in_=pt[:], func=AF.Square), reads=[ptok], writes=[t + "sq"])
                P.dve(lambda e: e.tensor_copy(out=B["qs"][:], in_=pt[:]), reads=[ptok], writes=[t + "qs"])
                P.pe(lambda e: e.matmul(pst[i][:], lhsT=onesblk[:], rhs=B["sq"][:], start=True, stop=True),
                     reads=[t + "sq", "onesblk"], writes=["@pst%d" % i])
                P.act(lambda e: e.activation(out=B["sd"][:], in_=pst[i][:], func=AF.Ln, scale=1.0 / 64, bias=EPS),
                      reads=["@pst%d" % i], writes=[t + "sd"])
                P.act(lambda e: e.activation(out=B["sd"][:], in_=B["sd"][:], func=AF.Exp, scale=-0.5),
                      reads=[t + "sd"], writes=[t + "sd"])
                P.dve(lambda e: e.scalar_tensor_tensor(out=dst, in0=B["qs"][:], scalar=vec[:, wcol:wcol + 1],
                                                       in1=B["sd"][:], op0=ALU.mult, op1=ALU.mult),
                      reads=[t + "qs", t + "sd", "vec"], writes=[dtok])

            for ci, (kind, idx) in enumerate(chunks):
                if kind in ("rx", "gate"):
                    break
                if ci % 2 == 0:
                    load_w(ci // 2)
                if kind == "pad":
                    continue
                for tt in range(NT):
                    cs = slice(tt * 512, (tt + 1) * 512)
                    pt, ptok = project(ci, tt)
                    if kind == "q":
                        i = state["hn"] % 2
                        headnorm(pt, ptok, V_QW, hst[i][:], "hst%d" % i)
                        P.dma(qT_d[idx, :, cs], hst[i][:], reads=["hst%d" % i], writes=["qT_d%d" % tt])
                    elif kind == "k":
                        headnorm(pt, ptok, V_KW, KT[:, idx, cs], "KT%d" % tt)
                    elif kind == "ks":
                        headnorm(pt, ptok, V_KW, KTs[:, idx, cs], "KTs%d" % tt)
                    elif kind == "ik":
                        headnorm(pt, ptok, V_IKW, ikT[:, cs], "ikT%d" % tt)
                    elif kind == "iq":
                        i = state["g"] % 2
                        state["g"] += 1
                        P.act(lambda e: e.activation(out=gst[i][:], in_=pt[:], func=AF.Copy),
                              reads=[ptok], writes=["gst%d" % i])
                        P.dma(iqT_d[idx, :, cs], gst[i][:], reads=["gst%d" % i], writes=["iqT_d%d" % tt])
                    elif kind == "g":
                        i = state["g"] % 2
                        state["g"] += 1
                        P.act(lambda e: e.activation(
                            out=gst[i][:], in_=pt[:], func=AF.Sigmoid, bias=vec[:, V_GB + idx:V_GB + idx + 1]),
                            reads=[ptok, "vec"], writes=["gst%d" % i])
                        P.dma(gT_d[idx, :, cs], gst[i][:], reads=["gst%d" % i], writes=["gT_d%d" % tt])
                if kind == "iq" and idx == 3:
                    for tk in range(NP):
                        a = state["acc"] % 3
                        state["acc"] += 1
                        pt = pacc[a]
                        for kc in range(8):
                            P.pe(lambda e: e.matmul(
                                pt[:, 0:264], lhsT=hT[:, kc, tk * 128:(tk + 1) * 128], rhs=wtm[:, kc, :],
                                start=(kc == 0), stop=(kc == 7)),
                                reads=["wtm", "hT%d_%d" % (kc, tk // 4)], writes=["@pacc%d" % a])
                        P.act(lambda e: e.activation(
                            out=Vp[:, tk, :, 0:64], in_=pt[:, 0:256].rearrange("p (g d) -> p g d", d=64), func=AF.Copy),
                            reads=["@pacc%d" % a], writes=["Vp%d" % tk])
                        P.dve(lambda e: e.tensor_copy(out=iw[:, tk, :], in_=pt[:, 256:264]),
                              reads=["@pacc%d" % a], writes=["iw%d" % tk])
                    P.dve(lambda e: e.memset(Vp[:, :, :, 64:65], 1.0), writes=["Vp1"])
                    P.act(lambda e: e.activation(out=absw[:], in_=iw[:], func=AF.Abs),
                          reads=["iw%d" % tk for tk in range(NP)], writes=["absw"])
                    P.act(lambda e: e.activation(out=sgnw[:], in_=iw[:], func=AF.Sign),
                          reads=["iw%d" % tk for tk in range(NP)], writes=["sgnw"])
        P.barrier(bscr[:, 4:5])

        with ExitStack() as pb2:
            pbd = [ps("@pbd%d" % i, [128, 512], F32, pb2) for i in range(4)]
            BA = []
            for i in range(4):
                d_ = {k: sb("ba_%s%d" % (k, i), [128, 512], F32, pb2) for k in ("xc", "tha", "thi", "aa", "hs", "gl")}
                d_["xr"] = sb("ba_xr%d" % i, [128, 515], F32, pb2)
                d_["xcb"] = sb("ba_xcb%d" % i, [128, 512], BF16, pb2)
                d_["yb"] = sb("ba_yb%d" % i, [128, 512], BF16, pb2)
                BA.append(d_)
            CI0 = [ci for ci, (kind, idx) in enumerate(chunks) if kind == "rx"][0]
            work = [(c, pr) for c in range(8) for pr in range(NT // 2)]
            prev = {"set": None, "hs": None}

            def front(wi):
                c, pr = work[wi]
                ci_rx = CI0 + 2 * c
                if pr == 0:
                    load_w(ci_rx // 2)
                for X in range(2):
                    tt = 2 * pr + X
                    si = 2 * (wi % 2) + X
                    B = BA[si]
                    t = "ba%d" % si
                    pt, ptok = project(ci_rx, tt)
                    P.act(lambda e: e.activation(out=B["xr"][:, 3:515], in_=pt[:], func=AF.Copy),
                          reads=[ptok], writes=[t + "xr"])
                    if tt == 0:
                        P.pool(lambda e: e.memset(B["xr"][:, 0:3], 0.0), writes=[t + "xh"])
                    else:
                        pB = BA[prev["set"]]
                        P.pool(lambda e: e.tensor_copy(out=B["xr"][:, 0:3], in_=pB["xr"][:, 512:515]),
                               reads=["ba%dxr" % prev["set"]], writes=[t + "xh"])
                    prev["set"] = si
                    rd = [t + "xr", t + "xh", "vec"]
                    P.dve(lambda e: e.tensor_scalar(
                        out=B["xc"][:], in0=B["xr"][:, 0:512], scalar1=vec[:, V_CW + c:V_CW + c + 1],
                        scalar2=vec[:, V_CB + c:V_CB + c + 1], op0=ALU.mult, op1=ALU.add),
                        reads=rd, writes=[t + "xc"])
                    for j in range(1, 4):
                        P.dve(lambda e: e.scalar_tensor_tensor(
                            out=B["xc"][:], in0=B["xr"][:, j:j + 512],
                            scalar=vec[:, V_CW + 8 * j + c:V_CW + 8 * j + c + 1], in1=B["xc"][:],
                            op0=ALU.mult, op1=ALU.add),
                            reads=rd + [t + "xc"], writes=[t + "xc"])
                    P.pool(lambda e: e.tensor_copy(out=B["xcb"][:], in_=B["xc"][:]), reads=[t + "xc"],
                           writes=[t + "xcb"])
                    ptg, ptokg = project(ci_rx + 1, tt)
                    P.act(lambda e: e.activation(out=B["gl"][:], in_=ptg[:], func=AF.Gelu_apprx_tanh),
                          reads=[ptokg], writes=[t + "gl"])

            def back(wi):
                c, pr = work[wi]
                sets = [2 * (wi % 2), 2 * (wi % 2) + 1]
                for X in range(2):
                    B = BA[sets[X]]
                    t = "ba%d" % sets[X]
                    P.pe(lambda e: e.matmul(pbd[2 * X][:], lhsT=wabd[:, c, :], rhs=B["xcb"][:], start=True, stop=True),
                         reads=[t + "xcb", "wabd"], writes=["@pbd%d" % (2 * X)])
                    P.pe(lambda e: e.matmul(pbd[2 * X + 1][:], lhsT=wxbd[:, c, :], rhs=B["xcb"][:], start=True, stop=True),
                         reads=[t + "xcb", "wxbd"], writes=["@pbd%d" % (2 * X + 1)])
                for X in range(2):
                    B = BA[sets[X]]
                    t = "ba%d" % sets[X]
                    P.act(lambda e: e.activation(out=B["tha"][:], in_=pbd[2 * X][:], func=AF.Tanh, scale=0.5,
                                                 bias=dvec[:, c:c + 1]),
                          reads=["@pbd%d" % (2 * X)] + DV, writes=[t + "tha"])
                    P.act(lambda e: e.activation(out=B["thi"][:], in_=pbd[2 * X + 1][:], func=AF.Tanh, scale=0.5,
                                                 bias=dvec[:, 8 + c:9 + c]),
                          reads=["@pbd%d" % (2 * X + 1)] + DV, writes=[t + "thi"])
                for X in range(2):
                    B = BA[sets[X]]
                    t = "ba%d" % sets[X]
                    P.act(lambda e: e.activation(out=B["aa"][:], in_=B["tha"][:], func=AF.Exp,
                                                 scale=dvec[:, 16 + c:17 + c], bias=dvec[:, 16 + c:17 + c]),
                          reads=[t + "tha"] + DV, writes=[t + "aa"])
                for X in range(2):
                    B = BA[sets[X]]
                    t = "ba%d" % sets[X]
                    P.act(lambda e: e.activation(out=B["tha"][:], in_=B["aa"][:], func=AF.Square),
                          reads=[t + "aa"], writes=[t + "tha"])
                for X in range(2):
                    B = BA[sets[X]]
                    t = "ba%d" % sets[X]
                    P.act(lambda e: e.activation(out=B["tha"][:], in_=B["tha"][:], func=AF.Sqrt, scale=-0.25, bias=0.25),
                          reads=[t + "tha"], writes=[t + "tha"])
                for X in range(2):
                    tt = 2 * pr + X
                    B = BA[sets[X]]
                    t = "ba%d" % sets[X]
                    P.dve(lambda e: e.scalar_tensor_tensor(out=B["thi"][:], in0=B["thi"][:], scalar=1.0, in1=B["xc"][:],
                                                           op0=ALU.add, op1=ALU.mult),
                          reads=[t + "thi", t + "xc"], writes=[t + "thi"])
                    P.dve(lambda e: e.tensor_tensor(out=B["thi"][:], in0=B["thi"][:], in1=B["tha"][:], op=ALU.mult),
                          reads=[t + "thi", t + "tha"], writes=[t + "thi"])
                    if tt == 0:
                        init, rdi = 0.0, []
                    else:
                        init, rdi = prev["hs"]
                    P.dve(lambda e: e.tensor_tensor_scan(out=B["hs"][:], data0=B["aa"][:], data1=B["thi"][:],
                                                         initial=init, op0=ALU.mult, op1=ALU.add),
                          reads=[t + "aa", t + "thi"] + rdi, writes=[t + "hs"])
                    prev["hs"] = (B["hs"][:, 511:512], [t + "hs"])
                    P.dve(lambda e: e.tensor_tensor(out=B["yb"][:], in0=B["hs"][:], in1=B["gl"][:], op=ALU.mult),
                          reads=[t + "hs", t + "gl"], writes=[t + "yb"])
                    P.dma(yrT_d[c, :, tt * 512:(tt + 1) * 512], B["yb"][:], reads=[t + "yb"], writes=["yrT_d%d" % tt])

            front(0)
            for wi in range(len(work)):
                if wi + 1 < len(work):
                    front(wi + 1)
                back(wi)
    pab.close()
    P.barrier(bscr[:, 1:2])

    if "B" in dbg:
        for name, t_, shp, dt in (("KT", KT, [128, 2, T], BF16), ("KTs", KTs, [128, 2, T], BF16),
                                  ("Vp", Vp, [128, NP, 4, 65], BF16), ("ikT", ikT, [128, T], BF16),
                                  ("iw", iw, [128, NP, 8], F32)):
            d_ = nc.dram_tensor("dbg_" + name, shp, dt, kind="ExternalOutput").ap()
            rd = {"KT": ["KT%d" % i for i in range(NT)], "KTs": ["KTs%d" % i for i in range(NT)],
                  "Vp": ["Vp%d" % i for i in range(NP)] + ["Vp1"], "ikT": ["ikT%d" % i for i in range(NT)],
                  "iw": ["iw%d" % i for i in range(NP)]}[name]
            P.dma(d_, t_[:], reads=rd, writes=["dbg_" + name])
        for name, src, n in (("qT", qT_d, 8), ("iqT", iqT_d, 4), ("gT", gT_d, 16), ("yrT", yrT_d, 8)):
            d_ = nc.dram_tensor("dbg_" + name, [n, 128, T], BF16, kind="ExternalOutput").ap()
            P.dma(d_, src, reads=["%s_d%d" % (name, i) for i in range(NT)], writes=["dbg_" + name])

    if stop == "B":
        P.emit()
        return nc, P
    QT_RD = ["qT_d%d" % i for i in range(NT)]
    IQ_RD = ["iqT_d%d" % i for i in range(NT)]
    LA = 2
    with ExitStack() as pt_:
        KZ = sb("KZ", [128, 8, T], BF16, pt_)
        ikZ = sb("ikZ", [128, 2, T], BF16, pt_)
        P.pool(lambda e: e.memset(KZ[:], 0.0), writes=["KZ0"])
        P.pool(lambda e: e.memset(ikZ[:], 0.0), writes=["ikZ0"])
        KALL = ["KT%d" % i for i in range(NT)] + ["KTs%d" % i for i in range(NT)]
        ci_ = 0
        for g in range(4):
            for hp in range(2):
                src_t = KT if (g % 2) == hp else KTs
                src = src_t[hp * 64:(hp + 1) * 64, g // 2, :]
                dst = KZ[hp * 64:(hp + 1) * 64, g * 2 + hp, :]
                if ci_ % 2 == 0:
                    P.dve(lambda e: e.tensor_copy(out=dst, in_=src), reads=KALL + ["KZ0"], writes=["KZ_%d" % ci_])
                else:
                    P.act(lambda e: e.activation(out=dst, in_=src, func=AF.Copy), reads=KALL + ["KZ0"],
                          writes=["KZ_%d" % ci_])
                ci_ += 1
        for hp in range(2):
            P.dve(lambda e: e.tensor_copy(out=ikZ[hp * 64:(hp + 1) * 64, hp, :], in_=ikT[hp * 64:(hp + 1) * 64, :]),
                  reads=["ikT%d" % i for i in range(NT)] + ["ikZ0"], writes=["ikZ_%d" % hp])
        KZ_RD = ["KZ0"] + ["KZ_%d" % i for i in range(8)]
        IKZ_RD = ["ikZ0", "ikZ_0", "ikZ_1"]
        P.barrier(bscr[:, 5:6])
        attR.close()
        NSC = 2
        sc = [sb("sc%d" % i, [128, T], F32, pt_) for i in range(NSC)]
        negm = [sb("negm%d" % i, [128, T], BF16, pt_) for i in range(NSC)]
        junkc = sb("junkc", [128, T], BF16, pt_)
        qp = [sb("qp%d" % i, [128, 8, 128], BF16, pt_) for i in range(2)]
        iqp = [sb("iqp%d" % i, [128, 4, 128], BF16, pt_) for i in range(NSC)]
        NRB = 10
        Rb = [sb("Rb%d" % i, [128, 512], BF16, pt_) for i in range(NRB)]
        dsg = [sb("dsg%d" % i, [128, 8, 128], BF16, pt_) for i in range(NSC)]
        NPTB = 6
        ptb = [sb("ptb%d" % i, [128, 512], BF16, pt_) for i in range(NPTB)]
        bs = [sb("bs%d" % i, [128, 8 + 2 * NIT], F32, pt_) for i in range(NSC)]
        rec = sb("rec", [128, 16], F32, pt_)
        yst = [sb("yst%d" % i, [128, 16, 64], BF16, pt_) for i in range(2)]
        accs = [sb("accs%d" % i, [128, 3, 512], F32, pt_) for i in range(2)]
        yT = [sb("yT%d" % i, [128, 8, 128], BF16, pt_) for i in range(2)]
        thc = sb("thc", [128, 1], F32, pt_)
        P.dve(lambda e: e.memset(thc[:], -1e29), writes=["thc"])
        st = [ps("@st%d" % i, [128, 512], F32, pt_) for i in range(2)]
        psc = ps("@psc", [128, 512], F32, pt_)
        acc = ps("acc", [128, 3, 512], F32, pt_)
        pmixs = [ps("pmix%d" % i, [128, 512], F32, pt_) for i in range(2)]
        pmix = pmixs[0]
        ptr = pmix[:].bitcast(BF16)

        cnt = {"R": 0, "pt": 0, "st": 0, "px": 0}

        def indexer_units(m):
            V = 128 * (m + 1)
            sbi = m % NSC
            units = []
            pend = {"f": None}

            def load():
                P.dma(iqp[sbi][:], iqT_d[:, :, m * 128:(m + 1) * 128].rearrange("c p t -> p c t"),
                      reads=IQ_RD, writes=["iqp%d" % sbi])
                for h in range(8):
                    P.pool(lambda e: e.tensor_scalar(out=dsg[sbi][:, h, :], in0=ident[:], scalar1=sgnw[:, m, h:h + 1],
                                                     scalar2=None, op0=ALU.mult),
                           reads=["ident", "sgnw"], writes=["dsg%d_%d" % (sbi, h)])
            units.append(load)
            nkb = (V + 511) // 512
            for kb in range(nkb):
                ncol = min(512, V - 512 * kb)
                for h in range(8):
                    def unit(kb=kb, ncol=ncol, h=h):
                        hp = h % 2
                        r = cnt["R"] % NRB
                        cnt["R"] += 1
                        px = cnt["px"] % 2
                        cnt["px"] += 1
                        pm = pmixs[px]
                        pmt = "@pmix%d" % px
                        P.pe(lambda e: e.matmul(pm[:, 0:ncol], lhsT=iqp[sbi][:, h // 2, :],
                                                rhs=ikZ[:, hp, kb * 512:kb * 512 + ncol],
                                                start=True, stop=True),
                             reads=["iqp%d" % sbi] + IKZ_RD, writes=[pmt])
                        P.act(lambda e: e.activation(out=Rb[r][:, 0:ncol], in_=pm[:, 0:ncol], func=AF.Relu,
                                                     scale=absw[:, m, h:h + 1]),
                              reads=[pmt, "absw"], writes=["Rb%d" % r])
                        if pend["f"] is not None:
                            pend["f"]()

                        def dsum():
                            P.pe(lambda e: e.matmul(psc[:, 0:ncol], lhsT=dsg[sbi][:, h, :], rhs=Rb[r][:, 0:ncol],
                                                    start=(h == 0), stop=(h == 7)),
                                 reads=["Rb%d" % r, "dsg%d_%d" % (sbi, h)], writes=["@psc"])
                            if h == 7:
                                P.act(lambda e: e.activation(out=sc[sbi][:, kb * 512:kb * 512 + ncol],
                                                             in_=psc[:, 0:ncol], func=AF.Copy),
                                      reads=["@psc"], writes=["sc%d_%d" % (sbi, kb)])
                        pend["f"] = dsum
                    units.append(unit)

            def flush():
                if pend["f"] is not None:
                    pend["f"]()
                    pend["f"] = None
            units.append(flush)
            return units

        def threshold_steps(m):
            V = 128 * (m + 1)
            sbi = m % NSC
            nkb = (V + 511) // 512
            SCT = ["sc%d_%d" % (sbi, kb) for kb in range(nkb)]
            b = bs[sbi]
            bt = "bs%d" % sbi
            s = sc[sbi]
            SCT2 = SCT + [bt + "ms"]
            steps = []

            def init():
                if m >= 2:
                    P.dve(lambda e: e.tensor_reduce(out=b[:, 0:1], in_=s[:, 0:V], axis=AX.X, op=ALU.max),
                          reads=SCT, writes=[bt + "mx"])
                    P.dve(lambda e: e.tensor_reduce(out=b[:, 1:2], in_=s[:, 0:V], axis=AX.X, op=ALU.min),
                          reads=SCT, writes=[bt + "mn"])
                P.dve(lambda e: e.memset(s[0:64, V - 64:V], -1e30), reads=SCT + [bt + "mx", bt + "mn"],
                      writes=[bt + "ms"])
                if m >= 2:
                    P.dve(lambda e: e.tensor_scalar(out=b[:, 2:3], in0=b[:, 0:1], scalar1=b[:, 1:2], scalar2=0.5,
                                                    op0=ALU.add, op1=ALU.mult),
                          reads=[bt + "mx", bt + "mn"], writes=[bt + "th"])
                    P.dve(lambda e: e.tensor_tensor(out=b[:, 3:4], in0=b[:, 0:1], in1=b[:, 1:2], op=ALU.subtract),
                          reads=[bt + "mx", bt + "mn"], writes=[bt + "rg"])
                    P.dve(lambda e: e.tensor_scalar(out=b[:, 8:8 + 2 * NIT], in0=ptab[:], scalar1=b[:, 3:4],
                                                    scalar2=None, op0=ALU.mult),
                          reads=[bt + "rg", "ptab"], writes=[bt + "tab"])
            steps.append(init)
            if m >= 2:
                for k in range(NIT):
                    def it(k=k):
                        P.dve(lambda e: e.tensor_scalar(out=junkc[:, 0:V], in0=s[:, 0:V], scalar1=b[:, 2:3],
                                                        scalar2=None, op0=ALU.is_ge, op1=ALU.add,
                                                        accum_out=b[:, 4:5]),
                              reads=SCT2 + [bt + "th"], writes=["junkc", bt + "cnt"])
                        P.dve(lambda e: e.tensor_scalar(out=b[:, 5:6], in0=b[:, 4:5], scalar1=TOPK - 0.5,
                                                        scalar2=b[:, 8 + NIT + k:9 + NIT + k],
                                                        op0=ALU.is_ge, op1=ALU.mult),
                              reads=[bt + "cnt", bt + "tab"], writes=[bt + "d"])
                        P.dve(lambda e: e.scalar_tensor_tensor(out=b[:, 2:3], in0=b[:, 2:3],
                                                               scalar=b[:, 8 + k:9 + k], in1=b[:, 5:6],
                                                               op0=ALU.subtract, op1=ALU.add),
                              reads=[bt + "th", bt + "d", bt + "tab"], writes=[bt + "th"])
                    steps.append(it)
                thap, thtok = b[:, 2:3], bt + "th"
            else:
                thap, thtok = thc[:], "thc"

            def fin():
                P.dve(lambda e: e.tensor_scalar(out=negm[sbi][:, 0:V], in0=s[:, 0:V], scalar1=thap, scalar2=NEG,
                                                op0=ALU.is_lt, op1=ALU.mult),
                      reads=SCT2 + [thtok], writes=["negm%d" % sbi])
            steps.append(fin)
            return steps

        def merge(a, b_):
            out_ = []
            ia = ib = 0
            na, nb_ = len(a), len(b_)
            while ia < na or ib < nb_:
                if ib >= nb_ or (ia < na and ia * nb_ <= ib * na):
                    out_.append(a[ia])
                    ia += 1
                else:
                    out_.append(b_[ib])
                    ib += 1
            return out_

        def attention(m, inter, deferred=None):
            sbi = m % NSC
            qb = m % 2
            if m == 0:
                P.dma(qp[0][:], qT_d[:, :, 0:128].rearrange("c p t -> p c t"), reads=QT_RD, writes=["qp0"])
            if m + 1 < NP:
                P.dma(qp[1 - qb][:], qT_d[:, :, (m + 1) * 128:(m + 2) * 128].rearrange("c p t -> p c t"),
                      reads=QT_RD, writes=["qp%d" % (1 - qb)])
            started = set()
            units = [(j, b2, hp) for j in range(m + 1) for b2 in range(2) for hp in range(2)]
            nun = len(units)
            ii = 0
            pending = None

            def emit_pv(u, pb_):
                j, b2, hp = u
                for gl in range(2):
                    for rr in range(2):
                        g = 2 * b2 + gl
                        head = 4 * g + hp + 2 * rr
                        bank = head // 7
                        off = (head % 7) * 65
                        first = (j == 0) and (bank not in started)
                        started.add(bank)
                        P.pe(lambda e: e.matmul(acc[:, bank, off:off + 65],
                                                lhsT=ptb[pb_][:, gl * 256 + rr * 128: gl * 256 + (rr + 1) * 128],
                                                rhs=Vp[:, j, g, :], start=first, stop=(j == m), skip_group_check=True),
                             reads=["ptb%d" % pb_, "Vp%d" % j, "Vp1"], writes=["@acc"])

            for ui, (j, b2, hp) in enumerate(units):
                k = cnt["st"] % 2
                cnt["st"] += 1
                stt = "@st%d" % k
                for gl in range(2):
                    g = 2 * b2 + gl
                    P.pe(lambda e: e.matmul(
                        st[k][:, gl * 256:(gl + 1) * 256],
                        lhsT=KZ[:, g * 2 + hp, j * 128:(j + 1) * 128],
                        rhs=qp[qb][:, 2 * g:2 * g + 2, :],
                        start=(gl == 0), stop=False, skip_group_check=True),
                        reads=["qp%d" % qb] + KZ_RD, writes=[stt])
                P.pe(lambda e: e.matmul(
                    st[k][:], lhsT=negm[sbi][:, j * 128:(j + 1) * 128], rhs=e4[:],
                    start=False, stop=True, skip_group_check=True),
                    reads=["negm%d" % sbi, "e4"], writes=[stt])
                pb_ = cnt["pt"] % NPTB
                cnt["pt"] += 1
                P.act(lambda e: e.activation(out=ptb[pb_][:], in_=st[k][:], func=AF.Exp, scale=0.125),
                      reads=[stt], writes=["ptb%d" % pb_])
                if pending is not None:
                    emit_pv(*pending)
                pending = ((j, b2, hp), pb_)
                if deferred is not None and ui == min(6, nun - 1):
                    deferred()
                    deferred = None
                tgt = (len(inter) * (ui + 1) + nun - 1) // nun
                while ii < tgt and ii < len(inter):
                    inter[ii]()
                    ii += 1
            emit_pv(*pending)
            while ii < len(inter):
                inter[ii]()
                ii += 1
            ab = m % 2
            for bank in range(3):
                ncw = 65 * (7 if bank < 2 else 2)
                P.act(lambda e: e.activation(out=accs[ab][:, bank, 0:ncw], in_=acc[:, bank, 0:ncw], func=AF.Copy),
                      reads=["@acc"], writes=["accs%d_%d" % (ab, bank)])
            for bank in range(3):
                nh = 7 if bank < 2 else 2
                v3 = accs[ab][:, bank, 0:nh * 65].rearrange("p (h d) -> p h d", d=65)
                P.dve(lambda e: e.reciprocal(out=rec[:, bank * 7:bank * 7 + nh], in_=v3[:, :, 64]),
                      reads=["accs%d_%d" % (ab, bank)], writes=["rec%d" % bank])
                P.dve(lambda e: e.tensor_tensor(
                    out=yst[ab][:, bank * 7:bank * 7 + nh, :], in0=v3[:, :, 0:64],
                    in1=rec[:, bank * 7:bank * 7 + nh].unsqueeze(2).to_broadcast([128, nh, 64]), op=ALU.mult),
                    reads=["accs%d_%d" % (ab, bank), "rec%d" % bank], writes=["yst%d_%d" % (ab, bank)])

            def finish():
                for kc in range(8):
                    P.pe(lambda e: e.transpose(out=ptr[:, kc * 128:(kc + 1) * 128],
                                               in_=yst[ab][:, 2 * kc:2 * kc + 2, :].rearrange("p h d -> p (h d)"),
                                               identity=ident[:]),
                         reads=["yst%d_%d" % (ab, b_) for b_ in range(3)] + ["ident"], writes=["@pmix0"])
                P.act(lambda e: e.activation(out=yT[ab][:], in_=ptr.rearrange("p (c t) -> p c t", t=128), func=AF.Copy),
                      reads=["@pmix0"], writes=["yT%d" % ab])
                P.dma(yaT_d[:, :, m * 128:(m + 1) * 128].rearrange("c p t -> p c t"), yT[ab][:],
                      reads=["yT%d" % ab], writes=["yaT_d%d" % (m // 4)])
            return finish

        for u in indexer_units(0):
            u()
        for u in threshold_steps(0):
            u()
        if NP > 1:
            for u in indexer_units(1):
                u()
        fin_prev = None
        for m in range(NP):
            idx_u = indexer_units(m + 2) if m + 2 < NP else []
            thr_u = threshold_steps(m + 1) if m + 1 < NP else []
            fin_prev = attention(m, merge(idx_u, thr_u), fin_prev)
        fin_prev()
    att.close()
    P.barrier(bscr[:, 2:3])

    if "ATT" in dbg:
        d_ = nc.dram_tensor("dbg_yaT", [8, 128, T], BF16, kind="ExternalOutput").ap()
        P.dma(d_, yaT_d, reads=["yaT_d%d" % i for i in range(NT)], writes=["dbg_yaT"])

    if stop == "ATT":
        P.emit()
        return nc, P
    with ExitStack() as pc:
        wor = sb("wor", [128, 8, D], BF16, pc)
        woa = sb("woa", [128, 8, D], BF16, pc)
        wout = sb("wout", [128, 8, D], BF16, pc)
        for nm, t_, d_ in (("wor", wor, wor_d), ("woa", woa, woa_d), ("wout", wout, wout_d)):
            for kc in range(8):
                P.dma(t_[:, kc, :], d_[kc * 128:(kc + 1) * 128, :], writes=["%s%d" % (nm, kc)], eng="gpsimd")
        WOR = ["wor%d" % k for k in range(8)]
        WOA = ["woa%d" % k for k in range(8)]
        WOUT = ["wout%d" % k for k in range(8)]
        yr = [sb("yr%d" % i, [128, 8, 512], BF16, pc) for i in range(2)]
        ya = [sb("ya%d" % i, [128, 8, 512], BF16, pc) for i in range(2)]
        gt = [sb("gt%d" % i, [128, 16, 512], BF16, pc) for i in range(2)]
        xt2 = [sb("xt2%d" % i, [128, 4, D], F32, pc) for i in range(2)]
        x1t = xt2
        mg = [sb("mg%d" % i, [128, 8, 512], BF16, pc) for i in range(2)]
        tmp = [sb("tmpc%d" % i, [128, 512], F32, pc) for i in range(2)]
        tmp2 = [sb("tmpd%d" % i, [128, 512], F32, pc) for i in range(2)]
        pA = [ps("@pA%d" % i, [128, 512], F32, pc) for i in range(2)]
        pB = [ps("@pB%d" % i, [128, 512], F32, pc) for i in range(2)]
        pO = [ps("@pO%d" % i, [128, 512], F32, pc) for i in range(2)]
        k = 0
        def c_loads(tt):
            b = tt % 2
            cs = slice(tt * 512, (tt + 1) * 512)
            P.dma(yr[b][:], yrT_d[:, :, cs].rearrange("c p t -> p c t"), reads=["yrT_d%d" % tt], writes=["yr%d" % b])
            P.dma(ya[b][:], yaT_d[:, :, cs].rearrange("c p t -> p c t"), reads=["yaT_d%d" % tt], writes=["ya%d" % b])
            P.dma(gt[b][:], gT_d[:, :, cs].rearrange("c p t -> p c t"), reads=["gT_d%d" % tt], writes=["gt%d" % b])
            P.dma(xt2[b][:], x_d[cs, :].rearrange("(j p) d -> p j d", p=128),
                  writes=["xt2%d" % b] + ["x1t%d_%d" % (b, s_) for s_ in range(4)])

        c_loads(0)
        for tt in range(NT):
            b = tt % 2
            cs = slice(tt * 512, (tt + 1) * 512)
            if tt + 1 < NT:
                c_loads(tt + 1)
            for mc in range(8):
                i = k % 2
                k += 1
                for kc in range(8):
                    P.pe(lambda e, kc=kc, mc=mc, i=i, b=b: e.matmul(
                        pA[i][:], lhsT=wor[:, kc, mc * 128:(mc + 1) * 128], rhs=yr[b][:, kc, :],
                        start=(kc == 0), stop=(kc == 7)),
                        reads=["wor%d" % kc, "yr%d" % b], writes=["@pA%d" % i])
                for kc in range(8):
                    P.pe(lambda e, kc=kc, mc=mc, i=i, b=b: e.matmul(
                        pB[i][:], lhsT=woa[:, kc, mc * 128:(mc + 1) * 128], rhs=ya[b][:, kc, :],
                        start=(kc == 0), stop=(kc == 7)),
                        reads=["woa%d" % kc, "ya%d" % b], writes=["@pB%d" % i])
                P.dve(lambda e, i=i, b=b, mc=mc: e.tensor_tensor(out=tmp[i][:], in0=pA[i][:], in1=gt[b][:, mc, :],
                                                                 op=ALU.mult),
                      reads=["@pA%d" % i, "gt%d" % b], writes=["tmpc%d" % i])
                P.dve(lambda e, i=i, b=b, mc=mc: e.tensor_tensor(out=tmp2[i][:], in0=pB[i][:], in1=gt[b][:, 8 + mc, :],
                                                                 op=ALU.mult),
                      reads=["@pB%d" % i, "gt%d" % b], writes=["tmpd%d" % i])
                P.pool(lambda e, i=i, b=b, mc=mc: e.tensor_tensor(out=mg[b][:, mc, :], in0=tmp[i][:], in1=tmp2[i][:],
                                                                  op=ALU.add),
                       reads=["tmpc%d" % i, "tmpd%d" % i], writes=["mg%d_%d" % (b, mc)])
            for sub in range(4):
                for half in range(2):
                    i = k % 2
                    k += 1
                    for mc in range(8):
                        P.pe(lambda e, mc=mc, sub=sub, half=half, i=i, b=b: e.matmul(
                            pO[i][:], lhsT=mg[b][:, mc, sub * 128:(sub + 1) * 128],
                            rhs=wout[:, mc, half * 512:(half + 1) * 512], start=(mc == 0), stop=(mc == 7)),
                            reads=["mg%d_%d" % (b, mc), "wout%d" % mc], writes=["@pO%d" % i])
                    P.dve(lambda e, sub=sub, half=half, i=i, b=b: e.tensor_tensor(
                        out=x1t[b][:, sub, half * 512:(half + 1) * 512], in0=pO[i][:],
                        in1=xt2[b][:, sub, half * 512:(half + 1) * 512], op=ALU.add),
                        reads=["@pO%d" % i, "xt2%d" % b], writes=["x1t%d_%d" % (b, sub)])
            P.dma(x1_d[cs, :].rearrange("(j p) d -> p j d", p=128), x1t[b][:],
                  reads=["x1t%d_%d" % (b, s_) for s_ in range(4)], writes=["x1_d%d" % tt])

    P.barrier(bscr[:, 3:4])
    if "C" in dbg:
        d_ = nc.dram_tensor("dbg_x1", [T, D], F32, kind="ExternalOutput").ap()
        P.dma(d_, x1_d, reads=["x1_d%d" % i for i in range(NT)], writes=["dbg_x1"])

    with ExitStack() as pd:
        w1 = sb("w1", [128, 8, 4 * D], BF16, pd)
        w2 = sb("w2", [128, 32, D], BF16, pd)
        for kc in range(8):
            for q4 in range(2):
                P.dma(w1[:, kc, q4 * 2048:(q4 + 1) * 2048], w1_d[kc * 128:(kc + 1) * 128, q4 * 2048:(q4 + 1) * 2048],
                      writes=["w1_%d" % kc], eng="gpsimd")
        for f in range(32):
            P.dma(w2[:, f, :], w2_d[f * 128:(f + 1) * 128, :], writes=["w2_%d" % f], eng="gpsimd")
        S2 = {
            "ssq": [sb("ssqD%d" % i, [128, 12], F32, pd) for i in range(2)],
            "junk": sb("junkD", [128, 1024], BF16, pd),
            "hb": [sb("hbD%d" % i, [128, 2, 1024], BF16, pd) for i in range(2)],
        }
        x1s = [sb("x1s%d" % i, [128, 2, D], F32, pd) for i in range(2)]
        h2T = [sb("h2T%d" % i, [128, 8, 256], BF16, pd) for i in range(2)]
        aT = sb("aT", [128, 32, 256], BF16, pd)
        rl = [sb("rl%d" % i, [128, 256], F32, pd) for i in range(2)]
        ot = x1s
        tpD = ps("tpD", [128, 3, 1024], BF16, pd)
        pF = [ps("@pF%d" % i, [128, 512], F32, pd) for i in range(3)]
        pG = [ps("@pG%d" % i, [128, 512], F32, pd) for i in range(2)]
        k = 0
        k2 = 0
        def d_prep(t2):
            b = t2 % 2
            rs_ = slice(t2 * 256, (t2 + 1) * 256)
            P.dma(x1s[b][:], x1_d[rs_, :].rearrange("(j p) d -> p j d", p=128), reads=["x1_d%d" % (t2 // 2)],
                  writes=["x1s%d" % b, "ot%d_0" % b, "ot%d_1" % b])
            norm_transpose(S2, x1s[b], "x1s%d" % b, 2,
                           lambda kc, b=b: (h2T[b][:, kc, :], "h2T%d_%d" % (b, kc)), V_N2, tpD, "D", t2)

        d_prep(0)
        for t2 in range(T // 256):
            b = t2 % 2
            rs_ = slice(t2 * 256, (t2 + 1) * 256)
            for f in range(32):
                i = k % 3
                k += 1
                for kc in range(8):
                    P.pe(lambda e, kc=kc, f=f, i=i, b=b: e.matmul(
                        pF[i][:, 0:256], lhsT=w1[:, kc, f * 128:(f + 1) * 128], rhs=h2T[b][:, kc, :],
                        start=(kc == 0), stop=(kc == 7)),
                        reads=["w1_%d" % kc, "h2T%d_%d" % (b, kc)], writes=["@pF%d" % i])
                r_ = k % 2
                P.act(lambda e, i=i, r_=r_: e.activation(out=rl[r_][:], in_=pF[i][:, 0:256], func=AF.Relu),
                      reads=["@pF%d" % i], writes=["rl%d" % r_])
                P.pool(lambda e, f=f, r_=r_: e.tensor_tensor(out=aT[:, f, :], in0=rl[r_][:], in1=rl[r_][:], op=ALU.mult),
                       reads=["rl%d" % r_], writes=["aT%d" % f])
            if t2 + 1 < T // 256:
                d_prep(t2 + 1)
            for sub in range(2):
                for half in range(2):
                    i = k2 % 2
                    k2 += 1
                    for f in range(32):
                        P.pe(lambda e, f=f, sub=sub, half=half, i=i: e.matmul(
                            pG[i][:], lhsT=aT[:, f, sub * 128:(sub + 1) * 128],
                            rhs=w2[:, f, half * 512:(half + 1) * 512], start=(f == 0), stop=(f == 31)),
                            reads=["aT%d" % f, "w2_%d" % f], writes=["@pG%d" % i])
                    P.dve(lambda e, sub=sub, half=half, i=i, b=b: e.tensor_tensor(
                        out=ot[b][:, sub, half * 512:(half + 1) * 512], in0=pG[i][:],
                        in1=x1s[b][:, sub, half * 512:(half + 1) * 512], op=ALU.add),
                        reads=["@pG%d" % i, "x1s%d" % b], writes=["ot%d_%d" % (b, sub)])
            P.dma(out_d[rs_, :].rearrange("(j p) d -> p j d", p=128), ot[b][:],
                  reads=["ot%d_0" % b, "ot%d_1" % b], writes=["out%d" % t2])

    P.emit()
    es.close()
    return nc, P


_CACHE = {}


def kernel(**inputs):
    x = np.asarray(inputs["x"], np.float32)
    B, T, _ = x.shape
    shared = prep_shared(inputs)
    if T not in _CACHE:
        _CACHE[T] = build(T)
    nc, _ = _CACHE[T]
    in_maps = []
    for b in range(B):
        m = dict(shared)
        m["x"] = np.ascontiguousarray(x[b])
        in_maps.append(m)
    res = run_bass_kernel_spmd(nc, in_maps, core_ids=list(range(B)))
    return np.stack([np.asarray(r["out"], np.float32) for r in res.results], axis=0)
```

```python
import types
import numpy as np
from contextlib import ExitStack
import concourse.bass as bass
import concourse.mybir as mybir
from concourse.bass_utils import run_bass_kernel_spmd

F32 = mybir.dt.float32
BF16 = mybir.dt.bfloat16
AF = mybir.ActivationFunctionType
ALU = mybir.AluOpType
AX = mybir.AxisListType

ENGS = ("tensor", "vector", "scalar", "gpsimd", "sync")
DMA_POOL = 20
D = 1024
NIT = 16
TOPK = 256
NEG = -1024.0
EPS = 1e-6


class Op:
    __slots__ = ("eng", "fn", "reads", "writes", "dma", "deps", "signal", "ev", "idx")

    def __init__(self, eng, fn, reads, writes, dma):
        self.eng = eng
        self.fn = fn
        self.reads = reads
        self.writes = writes
        self.dma = dma
        self.deps = []
        self.signal = False
        self.ev = None


def _freeze(fn):
    if fn.__closure__ is None:
        return fn
    cells = []
    for c in fn.__closure__:
        try:
            cells.append(types.CellType(c.cell_contents))
        except ValueError:
            cells.append(c)
    return types.FunctionType(fn.__code__, fn.__globals__, fn.__name__, fn.__defaults__, tuple(cells))


class Prog:
    def __init__(self, nc, stack):
        self.nc = nc
        self.stack = stack
        self.ops = []

    def op(self, eng, fn, reads=(), writes=(), dma=False):
        reads = tuple(reads)
        writes = tuple(writes) + tuple(r for r in reads if r.startswith("@"))
        reads = tuple(r for r in reads if not r.startswith("@"))
        o = Op(eng, _freeze(fn), reads + ("PHASE",), writes, dma)
        o.idx = len(self.ops)
        self.ops.append(o)
        return o

    def pe(self, fn, reads=(), writes=()):
        return self.op("tensor", fn, reads, writes)

    def dve(self, fn, reads=(), writes=()):
        return self.op("vector", fn, reads, writes)

    def act(self, fn, reads=(), writes=()):
        return self.op("scalar", fn, reads, writes)

    def pool(self, fn, reads=(), writes=()):
        return self.op("gpsimd", fn, reads, writes)

    def dma(self, out, in_, reads=(), writes=(), eng="sync"):
        return self.op(eng, lambda e: e.dma_start(out=out, in_=in_), reads, writes, dma=True)

    def barrier(self, scratch):
        o = Op("vector", lambda e: e.memset(scratch, 0.0), (), ("PHASE",), False)
        o.idx = len(self.ops)
        self.ops.append(o)

    def emit(self):
        nc = self.nc
        last_writer = {}
        readers = {}
        for o in self.ops:
            deps = set()
            for r in o.reads:
                w = last_writer.get(r)
                if w is not None:
                    deps.add(w.idx)
            for wtok in o.writes:
                w = last_writer.get(wtok)
                if w is not None:
                    deps.add(w.idx)
                for rd in readers.get(wtok, ()):
                    deps.add(rd.idx)
            deps.discard(o.idx)
            final = []
            for d in sorted(deps):
                p = self.ops[d]
                if (not p.dma) and (not o.dma) and p.eng == o.eng and o.eng == "tensor":
                    raw = any(last_writer.get(r) is p for r in o.reads)
                    if not raw:
                        continue
                final.append(p)
                p.signal = True
            o.deps = final
            for r in o.reads:
                lst = readers.setdefault(r, [])
                if not o.dma:
                    lst[:] = [x for x in lst if x.dma or x.eng != o.eng]
                lst.append(o)
            for wtok in o.writes:
                last_writer[wtok] = o
                readers[wtok] = []

        sems = {}
        counts = {}
        for e in ENGS:
            sems[e] = self.stack.enter_context(nc.semaphore("s_" + e))
            counts[e] = 0
        dpool = {}
        dcount = {}
        for q in ("sync", "gpsimd"):
            dpool[q] = [self.stack.enter_context(nc.semaphore("d_%s_%d" % (q, i))) for i in range(DMA_POOL)]
            dcount[q] = 0
        seen = {e: {} for e in ENGS}

        def wait(engname, sem, val):
            key = id(sem)
            if seen[engname].get(key, 0) >= val:
                return
            seen[engname][key] = val
            getattr(nc, engname).wait_ge(sem, val)

        for o in self.ops:
            eng = getattr(nc, o.eng)
            for p in o.deps:
                sem, val = p.ev
                wait(o.eng, sem, val)
            if o.dma:
                n = dcount[o.eng]
                dcount[o.eng] += 1
                sem = dpool[o.eng][n % DMA_POOL]
                prev = 16 * (n // DMA_POOL)
                if prev > 0:
                    wait(o.eng, sem, prev)
                inst = o.fn(eng)
                inst.then_inc(sem, 16)
                o.ev = (sem, prev + 16)
            else:
                inst = o.fn(eng)
                if o.signal:
                    counts[o.eng] += 1
                    inst.then_inc(sems[o.eng], 1)
                    o.ev = (sems[o.eng], counts[o.eng])
        for q in dpool:
            n = dcount[q]
            for i in range(min(n, DMA_POOL)):
                k = n - 1 - i
                sem = dpool[q][k % DMA_POOL]
                wait("sync", sem, 16 * (k // DMA_POOL + 1))
        self.counts = counts
        self.dcount = dcount


def chunk_list():
    ch = []
    for i in range(8):
        ch.append(("q", i))
    for i in range(2):
        ch.append(("k", i))
    for i in range(2):
        ch.append(("ks", i))
    for i in range(4):
        ch.append(("iq", i))
    for i in range(16):
        ch.append(("g", i))
    ch.append(("ik", 0))
    ch.append(("pad", 0))
    for c in range(8):
        ch.append(("rx", c))
        ch.append(("gate", c))
    return ch


V_N1, V_N2, V_CW, V_CB, V_BA, V_BX, V_LAM, V_QW, V_KW, V_IKW, V_GB = 0, 8, 16, 48, 56, 64, 72, 80, 81, 82, 83
NV = 100


def prep_shared(inp):
    w_in = np.asarray(inp["w_in"][0], np.float32)
    cols = []
    for kind, i in chunk_list():
        if kind == "q":
            cols.append(np.arange(2048 + 128 * i, 2048 + 128 * (i + 1)))
        elif kind == "k":
            cols.append(np.arange(3072 + 128 * i, 3072 + 128 * (i + 1)))
        elif kind == "ks":
            b = 3072 + 128 * i
            cols.append(np.concatenate([np.arange(b + 64, b + 128), np.arange(b, b + 64)]))
        elif kind in ("ik", "pad"):
            cols.append(np.concatenate([np.arange(4096, 4160), np.arange(4096, 4160)]))
        elif kind == "iq":
            cols.append(np.arange(3584 + 128 * i, 3584 + 128 * (i + 1)))
        elif kind == "g":
            cols.append(np.arange(4168 + 128 * i, 4168 + 128 * (i + 1)))
        elif kind == "rx":
            cols.append(np.arange(128 * i, 128 * (i + 1)))
        elif kind == "gate":
            cols.append(np.arange(1024 + 128 * i, 1024 + 128 * (i + 1)))
    cols = np.concatenate(cols)
    w_fm = np.ascontiguousarray(w_in[:, cols])
    w_tm = np.ascontiguousarray(np.concatenate([w_in[:, 3328:3584], w_in[:, 4160:4168]], axis=1))

    vec = np.zeros((128, NV), np.float32)

    def colmajor(v):
        return np.asarray(v, np.float32).reshape(8, 128).T

    vec[:, V_N1:V_N1 + 8] = colmajor(inp["norm1_w"][0])
    vec[:, V_N2:V_N2 + 8] = colmajor(inp["norm2_w"][0])
    for j in range(4):
        vec[:, V_CW + 8 * j:V_CW + 8 * j + 8] = colmajor(inp["conv_w"][0][j])
    vec[:, V_CB:V_CB + 8] = colmajor(inp["conv_b"][0])
    vec[:, V_BA:V_BA + 8] = colmajor(inp["rg_ba"][0])
    vec[:, V_BX:V_BX + 8] = colmajor(inp["rg_bx"][0])
    vec[:, V_LAM:V_LAM + 8] = colmajor(inp["rg_lambda"][0])
    vec[:, V_QW] = np.tile(np.asarray(inp["q_norm_w"][0], np.float32), 2)
    vec[:, V_KW] = np.tile(np.asarray(inp["k_norm_w"][0], np.float32), 2)
    vec[:, V_IKW] = np.tile(np.asarray(inp["idx_k_norm_w"][0], np.float32), 2)
    vec[:, V_GB:V_GB + 16] = np.asarray(inp["gate_b"][0], np.float32).reshape(16, 128).T

    def blockdiag(w):
        w = np.asarray(w, np.float32)
        o = np.zeros((128, 8, 128), np.float32)
        for c in range(8):
            o[0:64, c, 0:64] = w[2 * c]
            o[64:128, c, 64:128] = w[2 * c + 1]
        return o

    ident = np.eye(128, dtype=np.float32)
    onesblk = np.zeros((128, 128), np.float32)
    onesblk[0:64, 0:64] = 1.0
    onesblk[64:128, 64:128] = 1.0
    ptab = np.zeros((128, 2 * NIT), np.float32)
    for k in range(NIT):
        ptab[:, k] = 2.0 ** (-(k + 2))
        ptab[:, NIT + k] = 2.0 ** (-(k + 1))
    import ml_dtypes
    bf = ml_dtypes.bfloat16
    shared = {
        "w_fm": w_fm,
        "w_tm": w_tm,
        "vec": vec,
        "wa_bd": blockdiag(inp["rg_wa"][0]),
        "wx_bd": blockdiag(inp["rg_wx"][0]),
        "w_o_rnn": np.ascontiguousarray(np.asarray(inp["w_o_rnn"][0], np.float32)),
        "w_o_attn": np.ascontiguousarray(np.asarray(inp["w_o_attn"][0], np.float32)),
        "w_out": np.ascontiguousarray(np.asarray(inp["w_out"][0], np.float32)),
        "w_ff_in": np.ascontiguousarray(np.asarray(inp["w_ff_in"][0], np.float32)),
        "w_ff_out": np.ascontiguousarray(np.asarray(inp["w_ff_out"][0], np.float32)),
        "ident_bf": ident.astype(bf),
        "e4_bf": np.tile(ident, (1, 4)).astype(bf),
        "onesblk": onesblk,
        "ptab": ptab,
    }
    return shared


def build(T, dbg=(), stop=None):
    NT = T // 512
    NP = T // 128
    nc = bass.Bass("TRN2", target_bir_lowering=False)
    es = ExitStack()
    P = Prog(nc, es)

    def din(name, shape, dt=F32):
        return nc.dram_tensor(name, list(shape), dt, kind="ExternalInput").ap()

    x_d = din("x", [T, D])
    wfm_d = din("w_fm", [D, 6400])
    wtm_d = din("w_tm", [D, 264])
    vec_d = din("vec", [128, NV])
    wabd_d = din("wa_bd", [128, 8, 128])
    wxbd_d = din("wx_bd", [128, 8, 128])
    wor_d = din("w_o_rnn", [D, D])
    woa_d = din("w_o_attn", [D, D])
    wout_d = din("w_out", [D, D])
    w1_d = din("w_ff_in", [D, 4 * D])
    w2_d = din("w_ff_out", [4 * D, D])
    ident_d = din("ident_bf", [128, 128], BF16)
    e4_d = din("e4_bf", [128, 512], BF16)
    onesblk_d = din("onesblk", [128, 128])
    ptab_d = din("ptab", [128, 2 * NIT])
    out_d = nc.dram_tensor("out", [T, D], F32, kind="ExternalOutput").ap()

    qT_d = nc.dram_tensor("qT_s", [8, 128, T], BF16).ap()
    iqT_d = nc.dram_tensor("iqT_s", [4, 128, T], BF16).ap()
    gT_d = nc.dram_tensor("gT_s", [16, 128, T], BF16).ap()
    yrT_d = nc.dram_tensor("yrT_s", [8, 128, T], BF16).ap()
    yaT_d = nc.dram_tensor("yaT_s", [8, 128, T], BF16).ap()
    x1_d = nc.dram_tensor("x1_s", [T, D], F32).ap()
    dbg_out = {}

    def sb(name, shape, dt, stack=None, side=None):
        if side is not None:
            return (stack or es).enter_context(nc.sbuf_tensor("s_" + name, list(shape), dt, side=side))
        return (stack or es).enter_context(nc.sbuf_tensor("s_" + name, list(shape), dt))

    def ps(name, shape, dt, stack):
        return stack.enter_context(nc.psum_tensor("p_" + name.replace("@", ""), list(shape), dt))

    vec = sb("vec", [128, NV], F32)
    ident = sb("ident", [128, 128], BF16)
    e4 = sb("e4", [128, 512], BF16)
    onesblk = sb("onesblk", [128, 128], F32)
    ptab = sb("ptab", [128, 2 * NIT], F32)
    dvec = sb("dvec", [128, 32], F32)
    P.dma(vec[:], vec_d, writes=["vec"])
    P.dma(ident[:], ident_d, writes=["ident"])
    P.dma(e4[:], e4_d, writes=["e4"])
    P.dma(onesblk[:], onesblk_d, writes=["onesblk"])
    P.dma(ptab[:], ptab_d, writes=["ptab"])
    P.dve(lambda e: e.tensor_scalar(out=dvec[:, 0:8], in0=vec[:, V_BA:V_BA + 8], scalar1=0.5, scalar2=None,
                                    op0=ALU.mult), reads=["vec"], writes=["dv0"])
    P.dve(lambda e: e.tensor_scalar(out=dvec[:, 8:16], in0=vec[:, V_BX:V_BX + 8], scalar1=0.5, scalar2=None,
                                    op0=ALU.mult), reads=["vec"], writes=["dv1"])
    P.act(lambda e: e.activation(out=dvec[:, 24:32], in_=vec[:, V_LAM:V_LAM + 8], func=AF.Exp, scale=-1.0),
          reads=["vec"], writes=["dv3"])
    P.act(lambda e: e.activation(out=dvec[:, 24:32], in_=dvec[:, 24:32], func=AF.Ln, bias=1.0),
          reads=["dv3"], writes=["dv3"])
    P.dve(lambda e: e.tensor_scalar(out=dvec[:, 16:24], in0=dvec[:, 24:32], scalar1=-4.0, scalar2=None,
                                    op0=ALU.mult), reads=["dv3"], writes=["dv2"])
    DV = ["dv0", "dv1", "dv2"]
    bscr = sb("bscr", [128, 8], F32)

    def norm_transpose(S, src_tile, src_tok, nsub, dst_fn, wcol0, tp, tag, cnt):
        ssq = S["ssq"][cnt % 2]
        junk = S["junk"]
        hb = S["hb"][cnt % 2]
        tk = "%s%d" % (tag, cnt % 2)
        for j in range(nsub):
            P.act(lambda e, j=j: e.activation(out=junk[:], in_=src_tile[:, j, :], func=AF.Square,
                                              accum_out=ssq[:, j:j + 1]),
                  reads=[src_tok], writes=["junkA", tk + "ssq%d" % j])
        P.act(lambda e: e.activation(out=ssq[:, 4:4 + nsub], in_=ssq[:, 0:nsub], func=AF.Sqrt, scale=1.0 / D,
                                     bias=EPS),
              reads=[tk + "ssq%d" % j for j in range(nsub)], writes=[tk + "sd"])
        P.dve(lambda e: e.reciprocal(out=ssq[:, 8:8 + nsub], in_=ssq[:, 4:4 + nsub]),
              reads=[tk + "sd"], writes=[tk + "rs"])
        for j in range(nsub):
            P.dve(lambda e, j=j: e.tensor_scalar(out=hb[:, j, :], in0=src_tile[:, j, :],
                                                 scalar1=ssq[:, 8 + j:9 + j], scalar2=None, op0=ALU.mult),
                  reads=[src_tok, tk + "rs"], writes=[tk + "hb%d" % j])
        nb = tp.shape[1]
        for kc in range(8):
            bank = kc % nb
            ptok = "@tp%s%d" % (tag, bank)
            for j in range(nsub):
                P.pe(lambda e, j=j, kc=kc, bank=bank: e.transpose(
                    out=tp[:, bank, j * 128:(j + 1) * 128],
                    in_=hb[:, j, kc * 128:(kc + 1) * 128], identity=ident[:]),
                    reads=[tk + "hb%d" % j, "ident"], writes=[ptok])
            dst, dtok = dst_fn(kc)
            src = tp[:, bank, 0:nsub * 128]
            if kc % 2 == 0:
                P.act(lambda e, dst=dst, src=src, kc=kc: e.activation(
                    out=dst, in_=src, func=AF.Copy, scale=vec[:, wcol0 + kc: wcol0 + kc + 1]),
                    reads=[ptok, "vec"], writes=[dtok])
            else:
                P.dve(lambda e, dst=dst, src=src, kc=kc: e.tensor_scalar(
                    out=dst, in0=src, scalar1=vec[:, wcol0 + kc: wcol0 + kc + 1], scalar2=None, op0=ALU.mult),
                    reads=[ptok, "vec"], writes=[dtok])

    att = ExitStack()
    attR = ExitStack()
    KT = sb("KT", [128, 2, T], BF16, attR, side="right")
    KTs = sb("KTs", [128, 2, T], BF16, attR, side="right")
    Vp = sb("Vp", [128, NP, 4, 65], BF16, att)
    ikT = sb("ikT", [128, T], BF16, attR, side="right")
    iw = sb("iw", [128, NP, 8], F32, att)
    absw = sb("absw", [128, NP, 8], F32, att)
    sgnw = sb("sgnw", [128, NP, 8], F32, att)

    pab = ExitStack()
    hT = sb("hT", [128, 8, T], BF16, pab)
    with ExitStack() as pa:
        S = {
            "ssq": [sb("ssq%d" % i, [128, 12], F32, pa) for i in range(2)],
            "junk": sb("junkA", [128, 1024], BF16, pa),
            "hb": [sb("hb%d" % i, [128, 4, 1024], BF16, pa) for i in range(2)],
        }
        xs = [sb("xs%d" % i, [128, 4, 1024], F32, pa) for i in range(2)]
        tp = ps("tpA", [128, 4, 1024], BF16, pa)
        for g4 in range(NT):
            xt = xs[g4 % 2]
            xtok = "xs%d" % (g4 % 2)
            P.dma(xt[:], x_d[g4 * 512:(g4 + 1) * 512, :].rearrange("(j p) d -> p j d", p=128), writes=[xtok])
            norm_transpose(S, xt, xtok, 4,
                           lambda kc, g4=g4: (hT[:, kc, g4 * 512:(g4 + 1) * 512], "hT%d_%d" % (kc, g4)),
                           V_N1, tp, "A", g4)
    P.barrier(bscr[:, 0:1])
    if stop == "A":
        d_ = nc.dram_tensor("dbg_hT", [128, 8, T], BF16, kind="ExternalOutput").ap()
        P.dma(d_, hT[:], reads=["hT%d_%d" % (kc, g4) for kc in range(8) for g4 in range(NT)], writes=["dbg_hT"])
        P.emit()
        return nc, P
    HT_ALL = lambda tt: ["hT%d_%d" % (kc, tt) for kc in range(8)]

    chunks = chunk_list()
    NCH = len(chunks)
    with ExitStack() as pb:
        wbuf = [sb("wb%d" % i, [128, 8, 256], BF16, pb) for i in range(2)]
        wtm = sb("wtm", [128, 8, 264], BF16, pb)
        wabd = sb("wabd", [128, 8, 128], BF16, pb)
        wxbd = sb("wxbd", [128, 8, 128], BF16, pb)
        P.dma(wtm[:], wtm_d.rearrange("(kc p) c -> p kc c", p=128), writes=["wtm"], eng="gpsimd")
        P.dma(wabd[:], wabd_d, writes=["wabd"], eng="gpsimd")
        P.dma(wxbd[:], wxbd_d, writes=["wxbd"], eng="gpsimd")
        pacc = [ps("@pacc%d" % i, [128, 512], F32, pb) for i in range(3)]
        state = {"acc": 0, "hn": 0, "ba": 0, "g": 0}

        def load_w(grp):
            c0 = grp * 256
            wb = wbuf[grp % 2]
            P.dma(wb[:], wfm_d[:, c0:c0 + 256].rearrange("(kc p) c -> p kc c", p=128),
                  writes=["wb%d" % (grp % 2)], eng="gpsimd")

        def project(ci, tt):
            grp, loc = ci // 2, ci % 2
            wb = wbuf[grp % 2]
            a = state["acc"] % 3
            state["acc"] += 1
            pt = pacc[a]
            for kc in range(8):
                P.pe(lambda e: e.matmul(
                    pt[:], lhsT=wb[:, kc, loc * 128:(loc + 1) * 128], rhs=hT[:, kc, tt * 512:(tt + 1) * 512],
                    start=(kc == 0), stop=(kc == 7)),
                    reads=["wb%d" % (grp % 2), "hT%d_%d" % (kc, tt)], writes=["@pacc%d" % a])
            return pt, "@pacc%d" % a

        with ExitStack() as pb1:
            pst = [ps("@pst%d" % i, [128, 512], F32, pb1) for i in range(2)]
            hn0 = {k: sb("hn_%s" % k, [128, 512], F32, pb1) for k in ("sq", "qs", "sd")}
            hst = [sb("hst%d" % i, [128, 512], BF16, pb1) for i in range(2)]
            gst = [sb("gst%d" % i, [128, 512], BF16, pb1) for i in range(2)]

            def headnorm(pt, ptok, wcol, dst, dtok):
                i = state["hn"] % 2
                state["hn"] += 1
                B = hn0
                t = "hn0"
                P.act(lambda e: e.activation(out=B["sq"][:], in_=pt[:], func=AF.Square), reads=[ptok], writes=[t + "sq"])
                P.dve(lambda e: e.tensor_copy(out=B["qs"][:], in_=pt[:]), reads=[ptok], writes=[t + "qs"])
                P.pe(lambda e: e.matmul(pst[i][:], lhsT=onesblk[:], rhs=B["sq"][:], start=True, stop=True),
                     reads=[t + "sq", "onesblk"], writes=["@pst%d" % i])
                P.act(lambda e: e.activation(out=B["sd"][:], in_=pst[i][:], func=AF.Ln, scale=1.0 / 64, bias=EPS),
                      reads=["@pst%d" % i], writes=[t + "sd"])
                P.act(lambda e: e.activation(out=B["sd"][:], in_=B["sd"][:], func=AF.Exp, scale=-0.5),
                      reads=[t + "sd"], writes=[t + "sd"])
                P.dve(lambda e: e.scalar_tensor_tensor(out=dst, in0=B["qs"][:], scalar=vec[:, wcol:wcol + 1],
                                                       in1=B["sd"][:], op0=ALU.mult, op1=ALU.mult),
                      reads=[t + "qs", t + "sd", "vec"], writes=[dtok])

            for ci, (kind, idx) in enumerate(chunks):
                if kind in ("rx", "gate"):
                    break
                if ci % 2 == 0:
                    load_w(ci // 2)
                if kind == "pad":
                    continue
                for tt in range(NT):
                    cs = slice(tt * 512, (tt + 1) * 512)
                    pt, ptok = project(ci, tt)
                    if kind == "q":
                        i = state["hn"] % 2
                        headnorm(pt, ptok, V_QW, hst[i][:], "hst%d" % i)
                        P.dma(qT_d[idx, :, cs], hst[i][:], reads=["hst%d" % i], writes=["qT_d%d" % tt])
                    elif kind == "k":
                        headnorm(pt, ptok, V_KW, KT[:, idx, cs], "KT%d" % tt)
                    elif kind == "ks":
                        headnorm(pt, ptok, V_KW, KTs[:, idx, cs], "KTs%d" % tt)
                    elif kind == "ik":
                        headnorm(pt, ptok, V_IKW, ikT[:, cs], "ikT%d" % tt)
                    elif kind == "iq":
                        i = state["g"] % 2
                        state["g"] += 1
                        P.act(lambda e: e.activation(out=gst[i][:], in_=pt[:], func=AF.Copy),
                              reads=[ptok], writes=["gst%d" % i])
                        P.dma(iqT_d[idx, :, cs], gst[i][:], reads=["gst%d" % i], writes=["iqT_d%d" % tt])
                    elif kind == "g":
                        i = state["g"] % 2
                        state["g"] += 1
                        P.act(lambda e: e.activation(
                            out=gst[i][:], in_=pt[:], func=AF.Sigmoid, bias=vec[:, V_GB + idx:V_GB + idx + 1]),
                            reads=[ptok, "vec"], writes=["gst%d" % i])
                        P.dma(gT_d[idx, :, cs], gst[i][:], reads=["gst%d" % i], writes=["gT_d%d" % tt])
                if kind == "iq" and idx == 3:
                    for tk in range(NP):
                        a = state["acc"] % 3
                        state["acc"] += 1
                        pt = pacc[a]
                        for kc in range(8):
                            P.pe(lambda e: e.matmul(
                                pt[:, 0:264], lhsT=hT[:, kc, tk * 128:(tk + 1) * 128], rhs=wtm[:, kc, :],
                                start=(kc == 0), stop=(kc == 7)),
                                reads=["wtm", "hT%d_%d" % (kc, tk // 4)], writes=["@pacc%d" % a])
                        P.act(lambda e: e.activation(
                            out=Vp[:, tk, :, 0:64], in_=pt[:, 0:256].rearrange("p (g d) -> p g d", d=64), func=AF.Copy),
                            reads=["@pacc%d" % a], writes=["Vp%d" % tk])
                        P.dve(lambda e: e.tensor_copy(out=iw[:, tk, :], in_=pt[:, 256:264]),
                              reads=["@pacc%d" % a], writes=["iw%d" % tk])
                    P.dve(lambda e: e.memset(Vp[:, :, :, 64:65], 1.0), writes=["Vp1"])
                    P.act(lambda e: e.activation(out=absw[:], in_=iw[:], func=AF.Abs),
                          reads=["iw%d" % tk for tk in range(NP)], writes=["absw"])
                    P.act(lambda e: e.activation(out=sgnw[:], in_=iw[:], func=AF.Sign),
                          reads=["iw%d" % tk for tk in range(NP)], writes=["sgnw"])
        P.barrier(bscr[:, 4:5])

        with ExitStack() as pb2:
            pbd = [ps("@pbd%d" % i, [128, 512], F32, pb2) for i in range(4)]
            BA = []
            for i in range(4):
                d_ = {k: sb("ba_%s%d" % (k, i), [128, 512], F32, pb2) for k in ("xc", "tha", "thi", "aa", "hs", "gl")}
                d_["xr"] = sb("ba_xr%d" % i, [128, 515], F32, pb2)
                d_["xcb"] = sb("ba_xcb%d" % i, [128, 512], BF16, pb2)
                d_["yb"] = sb("ba_yb%d" % i, [128, 512], BF16, pb2)
                BA.append(d_)
            CI0 = [ci for ci, (kind, idx) in enumerate(chunks) if kind == "rx"][0]
            work = [(c, pr) for c in range(8) for pr in range(NT // 2)]
            prev = {"set": None, "hs": None}

            def front(wi):
                c, pr = work[wi]
                ci_rx = CI0 + 2 * c
                if pr == 0:
                    load_w(ci_rx // 2)
                for X in range(2):
                    tt = 2 * pr + X
                    si = 2 * (wi % 2) + X
                    B = BA[si]
                    t = "ba%d" % si
                    pt, ptok = project(ci_rx, tt)
                    P.act(lambda e: e.activation(out=B["xr"][:, 3:515], in_=pt[:], func=AF.Copy),
                          reads=[ptok], writes=[t + "xr"])
                    if tt == 0:
                        P.pool(lambda e: e.memset(B["xr"][:, 0:3], 0.0), writes=[t + "xh"])
                    else:
                        pB = BA[prev["set"]]
                        P.pool(lambda e: e.tensor_copy(out=B["xr"][:, 0:3], in_=pB["xr"][:, 512:515]),
                               reads=["ba%dxr" % prev["set"]], writes=[t + "xh"])
                    prev["set"] = si
                    rd = [t + "xr", t + "xh", "vec"]
                    P.dve(lambda e: e.tensor_scalar(
                        out=B["xc"][:], in0=B["xr"][:, 0:512], scalar1=vec[:, V_CW + c:V_CW + c + 1],
                        scalar2=vec[:, V_CB + c:V_CB + c + 1], op0=ALU.mult, op1=ALU.add),
                        reads=rd, writes=[t + "xc"])
                    for j in range(1, 4):
                        P.dve(lambda e: e.scalar_tensor_tensor(
                            out=B["xc"][:], in0=B["xr"][:, j:j + 512],
                            scalar=vec[:, V_CW + 8 * j + c:V_CW + 8 * j + c + 1], in1=B["xc"][:],
                            op0=ALU.mult, op1=ALU.add),
                            reads=rd + [t + "xc"], writes=[t + "xc"])
                    P.pool(lambda e: e.tensor_copy(out=B["xcb"][:], in_=B["xc"][:]), reads=[t + "xc"],
                           writes=[t + "xcb"])
                    ptg, ptokg = project(ci_rx + 1, tt)
                    P.act(lambda e: e.activation(out=B["gl"][:], in_=ptg[:], func=AF.Gelu_apprx_tanh),
                          reads=[ptokg], writes=[t + "gl"])

            def back(wi):
                c, pr = work[wi]
                sets = [2 * (wi % 2), 2 * (wi % 2) + 1]
                for X in range(2):
                    B = BA[sets[X]]
                    t = "ba%d" % sets[X]
                    P.pe(lambda e: e.matmul(pbd[2 * X][:], lhsT=wabd[:, c, :], rhs=B["xcb"][:], start=True, stop=True),
                         reads=[t + "xcb", "wabd"], writes=["@pbd%d" % (2 * X)])
                    P.pe(lambda e: e.matmul(pbd[2 * X + 1][:], lhsT=wxbd[:, c, :], rhs=B["xcb"][:], start=True, stop=True),
                         reads=[t + "xcb", "wxbd"], writes=["@pbd%d" % (2 * X + 1)])
                for X in range(2):
                    B = BA[sets[X]]
                    t = "ba%d" % sets[X]
                    P.act(lambda e: e.activation(out=B["tha"][:], in_=pbd[2 * X][:], func=AF.Tanh, scale=0.5,
                                                 bias=dvec[:, c:c + 1]),
                          reads=["@pbd%d" % (2 * X)] + DV, writes=[t + "tha"])
                    P.act(lambda e: e.activation(out=B["thi"][:], in_=pbd[2 * X + 1][:], func=AF.Tanh, scale=0.5,
                                                 bias=dvec[:, 8 + c:9 + c]),
                          reads=["@pbd%d" % (2 * X + 1)] + DV, writes=[t + "thi"])
                for X in range(2):
                    B = BA[sets[X]]
                    t = "ba%d" % sets[X]
                    P.act(lambda e: e.activation(out=B["aa"][:], in_=B["tha"][:], func=AF.Exp,
                                                 scale=dvec[:, 16 + c:17 + c], bias=dvec[:, 16 + c:17 + c]),
                          reads=[t + "tha"] + DV, writes=[t + "aa"])
                for X in range(2):
                    B = BA[sets[X]]
                    t = "ba%d" % sets[X]
                    P.act(lambda e: e.activation(out=B["tha"][:], in_=B["aa"][:], func=AF.Square),
                          reads=[t + "aa"], writes=[t + "tha"])
                for X in range(2):
                    B = BA[sets[X]]
                    t = "ba%d" % sets[X]
                    P.act(lambda e: e.activation(out=B["tha"][:], in_=B["tha"][:], func=AF.Sqrt, scale=-0.25, bias=0.25),
                          reads=[t + "tha"], writes=[t + "tha"])
                for X in range(2):
                    tt = 2 * pr + X
                    B = BA[sets[X]]
                    t = "ba%d" % sets[X]
                    P.dve(lambda e: e.scalar_tensor_tensor(out=B["thi"][:], in0=B["thi"][:], scalar=1.0, in1=B["xc"][:],
                                                           op0=ALU.add, op1=ALU.mult),
                          reads=[t + "thi", t + "xc"], writes=[t + "thi"])
                    P.dve(lambda e: e.tensor_tensor(out=B["thi"][:], in0=B["thi"][:], in1=B["tha"][:], op=ALU.mult),
                          reads=[t + "thi", t + "tha"], writes=[t + "thi"])
                    if tt == 0:
                        init, rdi = 0.0, []
                    else:
                        init, rdi = prev["hs"]
                    P.dve(lambda e: e.tensor_tensor_scan(out=B["hs"][:], data0=B["aa"][:], data1=B["thi"][:],
                                                         initial=init, op0=ALU.mult, op1=ALU.add),
                          reads=[t + "aa", t + "thi"] + rdi, writes=[t + "hs"])
                    prev["hs"] = (B["hs"][:, 511:512], [t + "hs"])
                    P.dve(lambda e: e.tensor_tensor(out=B["yb"][:], in0=B["hs"][:], in1=B["gl"][:], op=ALU.mult),
                          reads=[t + "hs", t + "gl"], writes=[t + "yb"])
                    P.dma(yrT_d[c, :, tt * 512:(tt + 1) * 512], B["yb"][:], reads=[t + "yb"], writes=["yrT_d%d" % tt])

            front(0)
            for wi in range(len(work)):
                if wi + 1 < len(work):
                    front(wi + 1)
                back(wi)
    pab.close()
    P.barrier(bscr[:, 1:2])

    if "B" in dbg:
        for name, t_, shp, dt in (("KT", KT, [128, 2, T], BF16), ("KTs", KTs, [128, 2, T], BF16),
                                  ("Vp", Vp, [128, NP, 4, 65], BF16), ("ikT", ikT, [128, T], BF16),
                                  ("iw", iw, [128, NP, 8], F32)):
            d_ = nc.dram_tensor("dbg_" + name, shp, dt, kind="ExternalOutput").ap()
            rd = {"KT": ["KT%d" % i for i in range(NT)], "KTs": ["KTs%d" % i for i in range(NT)],
                  "Vp": ["Vp%d" % i for i in range(NP)] + ["Vp1"], "ikT": ["ikT%d" % i for i in range(NT)],
                  "iw": ["iw%d" % i for i in range(NP)]}[name]
            P.dma(d_, t_[:], reads=rd, writes=["dbg_" + name])
        for name, src, n in (("qT", qT_d, 8), ("iqT", iqT_d, 4), ("gT", gT_d, 16), ("yrT", yrT_d, 8)):
            d_ = nc.dram_tensor("dbg_" + name, [n, 128, T], BF16, kind="ExternalOutput").ap()
            P.dma(d_, src, reads=["%s_d%d" % (name, i) for i in range(NT)], writes=["dbg_" + name])

    if stop == "B":
        P.emit()
        return nc, P
    QT_RD = ["qT_d%d" % i for i in range(NT)]
    IQ_RD = ["iqT_d%d" % i for i in range(NT)]
    LA = 2
    with ExitStack() as pt_:
        KZ = sb("KZ", [128, 8, T], BF16, pt_)
        ikZ = sb("ikZ", [128, 2, T], BF16, pt_)
        P.pool(lambda e: e.memset(KZ[:], 0.0), writes=["KZ0"])
        P.pool(lambda e: e.memset(ikZ[:], 0.0), writes=["ikZ0"])
        KALL = ["KT%d" % i for i in range(NT)] + ["KTs%d" % i for i in range(NT)]
        ci_ = 0
        for g in range(4):
            for hp in range(2):
                src_t = KT if (g % 2) == hp else KTs
                src = src_t[hp * 64:(hp + 1) * 64, g // 2, :]
                dst = KZ[hp * 64:(hp + 1) * 64, g * 2 + hp, :]
                if ci_ % 2 == 0:
                    P.dve(lambda e: e.tensor_copy(out=dst, in_=src), reads=KALL + ["KZ0"], writes=["KZ_%d" % ci_])
                else:
                    P.act(lambda e: e.activation(out=dst, in_=src, func=AF.Copy), reads=KALL + ["KZ0"],
                          writes=["KZ_%d" % ci_])
                ci_ += 1
        for hp in range(2):
            P.dve(lambda e: e.tensor_copy(out=ikZ[hp * 64:(hp + 1) * 64, hp, :], in_=ikT[hp * 64:(hp + 1) * 64, :]),
                  reads=["ikT%d" % i for i in range(NT)] + ["ikZ0"], writes=["ikZ_%d" % hp])
        KZ_RD = ["KZ0"] + ["KZ_%d" % i for i in range(8)]
        IKZ_RD = ["ikZ0", "ikZ_0", "ikZ_1"]
        P.barrier(bscr[:, 5:6])
        attR.close()
        NSC = 2
        sc = [sb("sc%d" % i, [128, T], F32, pt_) for i in range(NSC)]
        negm = [sb("negm%d" % i, [128, T], BF16, pt_) for i in range(NSC)]
        junkc = sb("junkc", [128, T], BF16, pt_)
        qp = [sb("qp%d" % i, [128, 8, 128], BF16, pt_) for i in range(2)]
        iqp = [sb("iqp%d" % i, [128, 4, 128], BF16, pt_) for i in range(NSC)]
        NRB = 10
        Rb = [sb("Rb%d" % i, [128, 512], BF16, pt_) for i in range(NRB)]
        dsg = [sb("dsg%d" % i, [128, 8, 128], BF16, pt_) for i in range(NSC)]
        NPTB = 6
        ptb = [sb("ptb%d" % i, [128, 512], BF16, pt_) for i in range(NPTB)]
        bs = [sb("bs%d" % i, [128, 8 + 2 * NIT], F32, pt_) for i in range(NSC)]
        rec = sb("rec", [128, 16], F32, pt_)
        yst = [sb("yst%d" % i, [128, 16, 64], BF16, pt_) for i in range(2)]
        accs = [sb("accs%d" % i, [128, 3, 512], F32, pt_) for i in range(2)]
        yT = [sb("yT%d" % i, [128, 8, 128], BF16, pt_) for i in range(2)]
        thc = sb("thc", [128, 1], F32, pt_)
        P.dve(lambda e: e.memset(thc[:], -1e29), writes=["thc"])
        st = [ps("@st%d" % i, [128, 512], F32, pt_) for i in range(2)]
        psc = ps("@psc", [128, 512], F32, pt_)
        acc = ps("acc", [128, 3, 512], F32, pt_)
        pmixs = [ps("pmix%d" % i, [128, 512], F32, pt_) for i in range(2)]
        pmix = pmixs[0]
        ptr = pmix[:].bitcast(BF16)

        cnt = {"R": 0, "pt": 0, "st": 0, "px": 0}

        def indexer_units(m):
            V = 128 * (m + 1)
            sbi = m % NSC
            units = []
            pend = {"f": None}

            def load():
                P.dma(iqp[sbi][:], iqT_d[:, :, m * 128:(m + 1) * 128].rearrange("c p t -> p c t"),
                      reads=IQ_RD, writes=["iqp%d" % sbi])
                for h in range(8):
                    P.pool(lambda e: e.tensor_scalar(out=dsg[sbi][:, h, :], in0=ident[:], scalar1=iw[:, m, h:h + 1],
                                                     scalar2=None, op0=ALU.mult),
                           reads=["ident", "iw%d" % m], writes=["dsg%d_%d" % (sbi, h)])
            units.append(load)
            nkb = (V + 511) // 512
            for kb in range(nkb):
                ncol = min(512, V - 512 * kb)
                for h in range(8):
                    def unit(kb=kb, ncol=ncol, h=h):
                        hp = h % 2
                        r = cnt["R"] % NRB
                        cnt["R"] += 1
                        px = cnt["px"] % 2
                        cnt["px"] += 1
                        pm = pmixs[px]
                        pmt = "@pmix%d" % px
                        P.pe(lambda e: e.matmul(pm[:, 0:ncol], lhsT=iqp[sbi][:, h // 2, :],
                                                rhs=ikZ[:, hp, kb * 512:kb * 512 + ncol],
                                                start=True, stop=True),
                             reads=["iqp%d" % sbi] + IKZ_RD, writes=[pmt])
                        P.act(lambda e: e.activation(out=Rb[r][:, 0:ncol], in_=pm[:, 0:ncol], func=AF.Relu),
                              reads=[pmt], writes=["Rb%d" % r])
                        if pend["f"] is not None:
                            pend["f"]()

                        def dsum():
                            P.pe(lambda e: e.matmul(psc[:, 0:ncol], lhsT=dsg[sbi][:, h, :], rhs=Rb[r][:, 0:ncol],
                                                    start=(h == 0), stop=(h == 7)),
                                 reads=["Rb%d" % r, "dsg%d_%d" % (sbi, h)], writes=["@psc"])
                            if h == 7:
                                P.act(lambda e: e.activation(out=sc[sbi][:, kb * 512:kb * 512 + ncol],
                                                             in_=psc[:, 0:ncol], func=AF.Copy),
                                      reads=["@psc"], writes=["sc%d_%d" % (sbi, kb)])
                        pend["f"] = dsum
                    units.append(unit)

            def flush():
                if pend["f"] is not None:
                    pend["f"]()
                    pend["f"] = None
            units.append(flush)
            return units

        def threshold_steps(m):
            V = 128 * (m + 1)
            sbi = m % NSC
            nkb = (V + 511) // 512
            SCT = ["sc%d_%d" % (sbi, kb) for kb in range(nkb)]
            b = bs[sbi]
            bt = "bs%d" % sbi
            s = sc[sbi]
            SCT2 = SCT + [bt + "ms"]
            steps = []

            def init():
                if m >= 2:
                    P.dve(lambda e: e.tensor_reduce(out=b[:, 0:1], in_=s[:, 0:V], axis=AX.X, op=ALU.max),
                          reads=SCT, writes=[bt + "mx"])
                    P.dve(lambda e: e.tensor_reduce(out=b[:, 1:2], in_=s[:, 0:V], axis=AX.X, op=ALU.min),
                          reads=SCT, writes=[bt + "mn"])
                P.dve(lambda e: e.memset(s[0:64, V - 64:V], -1e30), reads=SCT + [bt + "mx", bt + "mn"],
                      writes=[bt + "ms"])
                if m >= 2:
                    P.dve(lambda e: e.tensor_scalar(out=b[:, 2:3], in0=b[:, 0:1], scalar1=b[:, 1:2], scalar2=0.5,
                                                    op0=ALU.add, op1=ALU.mult),
                          reads=[bt + "mx", bt + "mn"], writes=[bt + "th"])
                    P.dve(lambda e: e.tensor_tensor(out=b[:, 3:4], in0=b[:, 0:1], in1=b[:, 1:2], op=ALU.subtract),
                          reads=[bt + "mx", bt + "mn"], writes=[bt + "rg"])
                    P.dve(lambda e: e.tensor_scalar(out=b[:, 8:8 + 2 * NIT], in0=ptab[:], scalar1=b[:, 3:4],
                                                    scalar2=None, op0=ALU.mult),
                          reads=[bt + "rg", "ptab"], writes=[bt + "tab"])
            steps.append(init)
            if m >= 2:
                for k in range(NIT):
                    def it(k=k):
                        P.dve(lambda e: e.tensor_scalar(out=junkc[:, 0:V], in0=s[:, 0:V], scalar1=b[:, 2:3],
                                                        scalar2=None, op0=ALU.is_ge, op1=ALU.add,
                                                        accum_out=b[:, 4:5]),
                              reads=SCT2 + [bt + "th"], writes=["junkc", bt + "cnt"])
                        P.dve(lambda e: e.tensor_scalar(out=b[:, 5:6], in0=b[:, 4:5], scalar1=TOPK - 0.5,
                                                        scalar2=b[:, 8 + NIT + k:9 + NIT + k],
                                                        op0=ALU.is_ge, op1=ALU.mult),
                              reads=[bt + "cnt", bt + "tab"], writes=[bt + "d"])
                        P.dve(lambda e: e.scalar_tensor_tensor(out=b[:, 2:3], in0=b[:, 2:3],
                                                               scalar=b[:, 8 + k:9 + k], in1=b[:, 5:6],
                                                               op0=ALU.subtract, op1=ALU.add),
                              reads=[bt + "th", bt + "d", bt + "tab"], writes=[bt + "th"])
                    steps.append(it)
                thap, thtok = b[:, 2:3], bt + "th"
            else:
                thap, thtok = thc[:], "thc"

            def fin():
                P.dve(lambda e: e.tensor_scalar(out=negm[sbi][:, 0:V], in0=s[:, 0:V], scalar1=thap, scalar2=NEG,
                                                op0=ALU.is_lt, op1=ALU.mult),
                      reads=SCT2 + [thtok], writes=["negm%d" % sbi])
            steps.append(fin)
            return steps

        def merge(a, b_):
            out_ = []
            ia = ib = 0
            na, nb_ = len(a), len(b_)
            while ia < na or ib < nb_:
                if ib >= nb_ or (ia < na and ia * nb_ <= ib * na):
                    out_.append(a[ia])
                    ia += 1
                else:
                    out_.append(b_[ib])
                    ib += 1
            return out_

        def attention(m, inter, deferred=None):
            sbi = m % NSC
            qb = m % 2
            if m == 0:
                P.dma(qp[0][:], qT_d[:, :, 0:128].rearrange("c p t -> p c t"), reads=QT_RD, writes=["qp0"])
            if m + 1 < NP:
                P.dma(qp[1 - qb][:], qT_d[:, :, (m + 1) * 128:(m + 2) * 128].rearrange("c p t -> p c t"),
                      reads=QT_RD, writes=["qp%d" % (1 - qb)])
            started = set()
            units = [(j, b2, hp) for j in range(m + 1) for b2 in range(2) for hp in range(2)]
            nun = len(units)
            ii = 0
            pending = None

            def emit_pv(u, pb_):
                j, b2, hp = u
                for gl in range(2):
                    for rr in range(2):
                        g = 2 * b2 + gl
                        head = 4 * g + hp + 2 * rr
                        bank = head // 7
                        off = (head % 7) * 65
                        first = (j == 0) and (bank not in started)
                        started.add(bank)
                        P.pe(lambda e: e.matmul(acc[:, bank, off:off + 65],
                                                lhsT=ptb[pb_][:, gl * 256 + rr * 128: gl * 256 + (rr + 1) * 128],
                                                rhs=Vp[:, j, g, :], start=first, stop=(j == m), skip_group_check=True),
                             reads=["ptb%d" % pb_, "Vp%d" % j, "Vp1"], writes=["@acc"])

            for ui, (j, b2, hp) in enumerate(units):
                k = cnt["st"] % 2
                cnt["st"] += 1
                stt = "@st%d" % k
                for gl in range(2):
                    g = 2 * b2 + gl
                    P.pe(lambda e: e.matmul(
                        st[k][:, gl * 256:(gl + 1) * 256],
                        lhsT=KZ[:, g * 2 + hp, j * 128:(j + 1) * 128],
                        rhs=qp[qb][:, 2 * g:2 * g + 2, :],
                        start=(gl == 0), stop=False, skip_group_check=True),
                        reads=["qp%d" % qb] + KZ_RD, writes=[stt])
                P.pe(lambda e: e.matmul(
                    st[k][:], lhsT=negm[sbi][:, j * 128:(j + 1) * 128], rhs=e4[:],
                    start=False, stop=True, skip_group_check=True),
                    reads=["negm%d" % sbi, "e4"], writes=[stt])
                pb_ = cnt["pt"] % NPTB
                cnt["pt"] += 1
                P.act(lambda e: e.activation(out=ptb[pb_][:], in_=st[k][:], func=AF.Exp, scale=0.125),
                      reads=[stt], writes=["ptb%d" % pb_])
                if pending is not None:
                    emit_pv(*pending)
                pending = ((j, b2, hp), pb_)
                if deferred is not None and ui == min(6, nun - 1):
                    deferred()
                    deferred = None
                tgt = (len(inter) * (ui + 1) + nun - 1) // nun
                while ii < tgt and ii < len(inter):
                    inter[ii]()
                    ii += 1
            emit_pv(*pending)
            while ii < len(inter):
                inter[ii]()
                ii += 1
            ab = m % 2
            for bank in range(3):
                ncw = 65 * (7 if bank < 2 else 2)
                P.act(lambda e: e.activation(out=accs[ab][:, bank, 0:ncw], in_=acc[:, bank, 0:ncw], func=AF.Copy),
                      reads=["@acc"], writes=["accs%d_%d" % (ab, bank)])
            for bank in range(3):
                nh = 7 if bank < 2 else 2
                v3 = accs[ab][:, bank, 0:nh * 65].rearrange("p (h d) -> p h d", d=65)
                P.dve(lambda e: e.reciprocal(out=rec[:, bank * 7:bank * 7 + nh], in_=v3[:, :, 64]),
                      reads=["accs%d_%d" % (ab, bank)], writes=["rec%d" % bank])
                P.dve(lambda e: e.tensor_tensor(
                    out=yst[ab][:, bank * 7:bank * 7 + nh, :], in0=v3[:, :, 0:64],
                    in1=rec[:, bank * 7:bank * 7 + nh].unsqueeze(2).to_broadcast([128, nh, 64]), op=ALU.mult),
                    reads=["accs%d_%d" % (ab, bank), "rec%d" % bank], writes=["yst%d_%d" % (ab, bank)])

            def finish():
                for kc in range(8):
                    P.pe(lambda e: e.transpose(out=ptr[:, kc * 128:(kc + 1) * 128],
                                               in_=yst[ab][:, 2 * kc:2 * kc + 2, :].rearrange("p h d -> p (h d)"),
                                               identity=ident[:]),
                         reads=["yst%d_%d" % (ab, b_) for b_ in range(3)] + ["ident"], writes=["@pmix0"])
                P.act(lambda e: e.activation(out=yT[ab][:], in_=ptr.rearrange("p (c t) -> p c t", t=128), func=AF.Copy),
                      reads=["@pmix0"], writes=["yT%d" % ab])
                P.dma(yaT_d[:, :, m * 128:(m + 1) * 128].rearrange("c p t -> p c t"), yT[ab][:],
                      reads=["yT%d" % ab], writes=["yaT_d%d" % (m // 4)])
            return finish

        for u in indexer_units(0):
            u()
        for u in threshold_steps(0):
            u()
        if NP > 1:
            for u in indexer_units(1):
                u()
        fin_prev = None
        for m in range(NP):
            idx_u = indexer_units(m + 2) if m + 2 < NP else []
            thr_u = threshold_steps(m + 1) if m + 1 < NP else []
            fin_prev = attention(m, merge(idx_u, thr_u), fin_prev)
        fin_prev()
    att.close()
    P.barrier(bscr[:, 2:3])

    if "ATT" in dbg:
        d_ = nc.dram_tensor("dbg_yaT", [8, 128, T], BF16, kind="ExternalOutput").ap()
        P.dma(d_, yaT_d, reads=["yaT_d%d" % i for i in range(NT)], writes=["dbg_yaT"])

    if stop == "ATT":
        P.emit()
        return nc, P
    with ExitStack() as pc:
        wor = sb("wor", [128, 8, D], BF16, pc)
        woa = sb("woa", [128, 8, D], BF16, pc)
        wout = sb("wout", [128, 8, D], BF16, pc)
        for nm, t_, d_ in (("wor", wor, wor_d), ("woa", woa, woa_d), ("wout", wout, wout_d)):
            for kc in range(8):
                P.dma(t_[:, kc, :], d_[kc * 128:(kc + 1) * 128, :], writes=["%s%d" % (nm, kc)], eng="gpsimd")
        WOR = ["wor%d" % k for k in range(8)]
        WOA = ["woa%d" % k for k in range(8)]
        WOUT = ["wout%d" % k for k in range(8)]
        yr = [sb("yr%d" % i, [128, 8, 512], BF16, pc) for i in range(2)]
        ya = [sb("ya%d" % i, [128, 8, 512], BF16, pc) for i in range(2)]
        gt = [sb("gt%d" % i, [128, 16, 512], BF16, pc) for i in range(2)]
        xt2 = [sb("xt2%d" % i, [128, 4, D], F32, pc) for i in range(2)]
        x1t = xt2
        mg = [sb("mg%d" % i, [128, 8, 512], BF16, pc) for i in range(2)]
        tmp = [sb("tmpc%d" % i, [128, 512], F32, pc) for i in range(2)]
        tmp2 = [sb("tmpd%d" % i, [128, 512], F32, pc) for i in range(2)]
        pA = [ps("@pA%d" % i, [128, 512], F32, pc) for i in range(2)]
        pB = [ps("@pB%d" % i, [128, 512], F32, pc) for i in range(2)]
        pO = [ps("@pO%d" % i, [128, 512], F32, pc) for i in range(2)]
        k = 0
        def c_loads(tt):
            b = tt % 2
            cs = slice(tt * 512, (tt + 1) * 512)
            P.dma(yr[b][:], yrT_d[:, :, cs].rearrange("c p t -> p c t"), reads=["yrT_d%d" % tt], writes=["yr%d" % b])
            P.dma(ya[b][:], yaT_d[:, :, cs].rearrange("c p t -> p c t"), reads=["yaT_d%d" % tt], writes=["ya%d" % b])
            P.dma(gt[b][:], gT_d[:, :, cs].rearrange("c p t -> p c t"), reads=["gT_d%d" % tt], writes=["gt%d" % b])
            P.dma(xt2[b][:], x_d[cs, :].rearrange("(j p) d -> p j d", p=128),
                  writes=["xt2%d" % b] + ["x1t%d_%d" % (b, s_) for s_ in range(4)])

        c_loads(0)
        for tt in range(NT):
            b = tt % 2
            cs = slice(tt * 512, (tt + 1) * 512)
            if tt + 1 < NT:
                c_loads(tt + 1)
            for mc in range(8):
                i = k % 2
                k += 1
                for kc in range(8):
                    P.pe(lambda e, kc=kc, mc=mc, i=i, b=b: e.matmul(
                        pA[i][:], lhsT=wor[:, kc, mc * 128:(mc + 1) * 128], rhs=yr[b][:, kc, :],
                        start=(kc == 0), stop=(kc == 7)),
                        reads=["wor%d" % kc, "yr%d" % b], writes=["@pA%d" % i])
                for kc in range(8):
                    P.pe(lambda e, kc=kc, mc=mc, i=i, b=b: e.matmul(
                        pB[i][:], lhsT=woa[:, kc, mc * 128:(mc + 1) * 128], rhs=ya[b][:, kc, :],
                        start=(kc == 0), stop=(kc == 7)),
                        reads=["woa%d" % kc, "ya%d" % b], writes=["@pB%d" % i])
                P.dve(lambda e, i=i, b=b, mc=mc: e.tensor_tensor(out=tmp[i][:], in0=pA[i][:], in1=gt[b][:, mc, :],
                                                                 op=ALU.mult),
                      reads=["@pA%d" % i, "gt%d" % b], writes=["tmpc%d" % i])
                P.dve(lambda e, i=i, b=b, mc=mc: e.tensor_tensor(out=tmp2[i][:], in0=pB[i][:], in1=gt[b][:, 8 + mc, :],
                                                                 op=ALU.mult),
                      reads=["@pB%d" % i, "gt%d" % b], writes=["tmpd%d" % i])
                P.pool(lambda e, i=i, b=b, mc=mc: e.tensor_tensor(out=mg[b][:, mc, :], in0=tmp[i][:], in1=tmp2[i][:],
                                                                  op=ALU.add),
                       reads=["tmpc%d" % i, "tmpd%d" % i], writes=["mg%d_%d" % (b, mc)])
            for sub in range(4):
                for half in range(2):
                    i = k % 2
                    k += 1
                    for mc in range(8):
                        P.pe(lambda e, mc=mc, sub=sub, half=half, i=i, b=b: e.matmul(
                            pO[i][:], lhsT=mg[b][:, mc, sub * 128:(sub + 1) * 128],
                            rhs=wout[:, mc, half * 512:(half + 1) * 512], start=(mc == 0), stop=(mc == 7)),
                            reads=["mg%d_%d" % (b, mc), "wout%d" % mc], writes=["@pO%d" % i])
                    P.dve(lambda e, sub=sub, half=half, i=i, b=b: e.tensor_tensor(
                        out=x1t[b][:, sub, half * 512:(half + 1) * 512], in0=pO[i][:],
                        in1=xt2[b][:, sub, half * 512:(half + 1) * 512], op=ALU.add),
                        reads=["@pO%d" % i, "xt2%d" % b], writes=["x1t%d_%d" % (b, sub)])
            P.dma(x1_d[cs, :].rearrange("(j p) d -> p j d", p=128), x1t[b][:],
                  reads=["x1t%d_%d" % (b, s_) for s_ in range(4)], writes=["x1_d%d" % tt])

    P.barrier(bscr[:, 3:4])
    if "C" in dbg:
        d_ = nc.dram_tensor("dbg_x1", [T, D], F32, kind="ExternalOutput").ap()
        P.dma(d_, x1_d, reads=["x1_d%d" % i for i in range(NT)], writes=["dbg_x1"])

    with ExitStack() as pd:
        w1 = sb("w1", [128, 8, 4 * D], BF16, pd)
        w2 = sb("w2", [128, 32, D], BF16, pd)
        for kc in range(8):
            for q4 in range(2):
                P.dma(w1[:, kc, q4 * 2048:(q4 + 1) * 2048], w1_d[kc * 128:(kc + 1) * 128, q4 * 2048:(q4 + 1) * 2048],
                      writes=["w1_%d" % kc], eng="gpsimd")
        for f in range(32):
            P.dma(w2[:, f, :], w2_d[f * 128:(f + 1) * 128, :], writes=["w2_%d" % f], eng="gpsimd")
        S2 = {
            "ssq": [sb("ssqD%d" % i, [128, 12], F32, pd) for i in range(2)],
            "junk": sb("junkD", [128, 1024], BF16, pd),
            "hb": [sb("hbD%d" % i, [128, 2, 1024], BF16, pd) for i in range(2)],
        }
        x1s = [sb("x1s%d" % i, [128, 2, D], F32, pd) for i in range(2)]
        h2T = [sb("h2T%d" % i, [128, 8, 256], BF16, pd) for i in range(2)]
        aT = sb("aT", [128, 32, 256], BF16, pd)
        rl = [sb("rl%d" % i, [128, 256], F32, pd) for i in range(2)]
        ot = x1s
        tpD = ps("tpD", [128, 3, 1024], BF16, pd)
        pF = [ps("@pF%d" % i, [128, 512], F32, pd) for i in range(3)]
        pG = [ps("@pG%d" % i, [128, 512], F32, pd) for i in range(2)]
        k = 0
        k2 = 0
        def d_prep(t2):
            b = t2 % 2
            rs_ = slice(t2 * 256, (t2 + 1) * 256)
            P.dma(x1s[b][:], x1_d[rs_, :].rearrange("(j p) d -> p j d", p=128), reads=["x1_d%d" % (t2 // 2)],
                  writes=["x1s%d" % b, "ot%d_0" % b, "ot%d_1" % b])
            norm_transpose(S2, x1s[b], "x1s%d" % b, 2,
                           lambda kc, b=b: (h2T[b][:, kc, :], "h2T%d_%d" % (b, kc)), V_N2, tpD, "D", t2)

        d_prep(0)
        for t2 in range(T // 256):
            b = t2 % 2
            rs_ = slice(t2 * 256, (t2 + 1) * 256)
            for f in range(32):
                i = k % 3
                k += 1
                for kc in range(8):
                    P.pe(lambda e, kc=kc, f=f, i=i, b=b: e.matmul(
                        pF[i][:, 0:256], lhsT=w1[:, kc, f * 128:(f + 1) * 128], rhs=h2T[b][:, kc, :],
                        start=(kc == 0), stop=(kc == 7)),
                        reads=["w1_%d" % kc, "h2T%d_%d" % (b, kc)], writes=["@pF%d" % i])
                r_ = k % 2
                P.act(lambda e, i=i, r_=r_: e.activation(out=rl[r_][:], in_=pF[i][:, 0:256], func=AF.Relu),
                      reads=["@pF%d" % i], writes=["rl%d" % r_])
                P.pool(lambda e, f=f, r_=r_: e.tensor_tensor(out=aT[:, f, :], in0=rl[r_][:], in1=rl[r_][:], op=ALU.mult),
                       reads=["rl%d" % r_], writes=["aT%d" % f])
            if t2 + 1 < T // 256:
                d_prep(t2 + 1)
            for sub in range(2):
                for half in range(2):
                    i = k2 % 2
                    k2 += 1
                    for f in range(32):
                        P.pe(lambda e, f=f, sub=sub, half=half, i=i: e.matmul(
                            pG[i][:], lhsT=aT[:, f, sub * 128:(sub + 1) * 128],
                            rhs=w2[:, f, half * 512:(half + 1) * 512], start=(f == 0), stop=(f == 31)),
                            reads=["aT%d" % f, "w2_%d" % f], writes=["@pG%d" % i])
                    P.dve(lambda e, sub=sub, half=half, i=i, b=b: e.tensor_tensor(
                        out=ot[b][:, sub, half * 512:(half + 1) * 512], in0=pG[i][:],
                        in1=x1s[b][:, sub, half * 512:(half + 1) * 512], op=ALU.add),
                        reads=["@pG%d" % i, "x1s%d" % b], writes=["ot%d_%d" % (b, sub)])
            P.dma(out_d[rs_, :].rearrange("(j p) d -> p j d", p=128), ot[b][:],
                  reads=["ot%d_0" % b, "ot%d_1" % b], writes=["out%d" % t2])

    P.emit()
    es.close()
    return nc, P


_CACHE = {}


def kernel(**inputs):
    x = np.asarray(inputs["x"], np.float32)
    B, T, _ = x.shape
    shared = prep_shared(inputs)
    if T not in _CACHE:
        _CACHE[T] = build(T)
    nc, _ = _CACHE[T]
    in_maps = []
    for b in range(B):
        m = dict(shared)
        m["x"] = np.ascontiguousarray(x[b])
        in_maps.append(m)
    res = run_bass_kernel_spmd(nc, in_maps, core_ids=list(range(B)))
    return np.stack([np.asarray(r["out"], np.float32) for r in res.results], axis=0)
```

```python
import types
import numpy as np
from contextlib import ExitStack
import concourse.bass as bass
import concourse.mybir as mybir
from concourse.bass_utils import run_bass_kernel_spmd

F32 = mybir.dt.float32
BF16 = mybir.dt.bfloat16
AF = mybir.ActivationFunctionType
ALU = mybir.AluOpType
AX = mybir.AxisListType

ENGS = ("tensor", "vector", "scalar", "gpsimd", "sync")
DMA_POOL = 20
D = 1024
NIT = 16
TOPK = 256
NEG = -1024.0
EPS = 1e-6


class Op:
    __slots__ = ("eng", "fn", "reads", "writes", "dma", "deps", "signal", "ev", "idx")

    def __init__(self, eng, fn, reads, writes, dma):
        self.eng = eng
        self.fn = fn
        self.reads = reads
        self.writes = writes
        self.dma = dma
        self.deps = []
        self.signal = False
        self.ev = None


def _freeze(fn):
    if fn.__closure__ is None:
        return fn
    cells = []
    for c in fn.__closure__:
        try:
            cells.append(types.CellType(c.cell_contents))
        except ValueError:
            cells.append(c)
    return types.FunctionType(fn.__code__, fn.__globals__, fn.__name__, fn.__defaults__, tuple(cells))


class Prog:
    def __init__(self, nc, stack):
        self.nc = nc
        self.stack = stack
        self.ops = []

    def op(self, eng, fn, reads=(), writes=(), dma=False):
        reads = tuple(reads)
        writes = tuple(writes) + tuple(r for r in reads if r.startswith("@"))
        reads = tuple(r for r in reads if not r.startswith("@"))
        o = Op(eng, _freeze(fn), reads + ("PHASE",), writes, dma)
        o.idx = len(self.ops)
        self.ops.append(o)
        return o

    def pe(self, fn, reads=(), writes=()):
        return self.op("tensor", fn, reads, writes)

    def dve(self, fn, reads=(), writes=()):
        return self.op("vector", fn, reads, writes)

    def act(self, fn, reads=(), writes=()):
        return self.op("scalar", fn, reads, writes)

    def pool(self, fn, reads=(), writes=()):
        return self.op("gpsimd", fn, reads, writes)

    def dma(self, out, in_, reads=(), writes=(), eng="sync"):
        return self.op(eng, lambda e: e.dma_start(out=out, in_=in_), reads, writes, dma=True)

    def barrier(self, scratch):
        o = Op("vector", lambda e: e.memset(scratch, 0.0), (), ("PHASE",), False)
        o.idx = len(self.ops)
        self.ops.append(o)

    def emit(self):
        nc = self.nc
        last_writer = {}
        readers = {}
        for o in self.ops:
            deps = set()
            for r in o.reads:
                w = last_writer.get(r)
                if w is not None:
                    deps.add(w.idx)
            for wtok in o.writes:
                w = last_writer.get(wtok)
                if w is not None:
                    deps.add(w.idx)
                for rd in readers.get(wtok, ()):
                    deps.add(rd.idx)
            deps.discard(o.idx)
            final = []
            for d in sorted(deps):
                p = self.ops[d]
                if (not p.dma) and (not o.dma) and p.eng == o.eng and o.eng == "tensor":
                    raw = any(last_writer.get(r) is p for r in o.reads)
                    if not raw:
                        continue
                final.append(p)
                p.signal = True
            o.deps = final
            for r in o.reads:
                lst = readers.setdefault(r, [])
                if not o.dma:
                    lst[:] = [x for x in lst if x.dma or x.eng != o.eng]
                lst.append(o)
            for wtok in o.writes:
                last_writer[wtok] = o
                readers[wtok] = []

        sems = {}
        counts = {}
        for e in ENGS:
            sems[e] = self.stack.enter_context(nc.semaphore("s_" + e))
            counts[e] = 0
        dpool = {}
        dcount = {}
        for q in ("sync", "gpsimd"):
            dpool[q] = [self.stack.enter_context(nc.semaphore("d_%s_%d" % (q, i))) for i in range(DMA_POOL)]
            dcount[q] = 0
        seen = {e: {} for e in ENGS}

        def wait(engname, sem, val):
            key = id(sem)
            if seen[engname].get(key, 0) >= val:
                return
            seen[engname][key] = val
            getattr(nc, engname).wait_ge(sem, val)

        for o in self.ops:
            eng = getattr(nc, o.eng)
            for p in o.deps:
                sem, val = p.ev
                wait(o.eng, sem, val)
            if o.dma:
                n = dcount[o.eng]
                dcount[o.eng] += 1
                sem = dpool[o.eng][n % DMA_POOL]
                prev = 16 * (n // DMA_POOL)
                if prev > 0:
                    wait(o.eng, sem, prev)
                inst = o.fn(eng)
                inst.then_inc(sem, 16)
                o.ev = (sem, prev + 16)
            else:
                inst = o.fn(eng)
                if o.signal:
                    counts[o.eng] += 1
                    inst.then_inc(sems[o.eng], 1)
                    o.ev = (sems[o.eng], counts[o.eng])
        for q in dpool:
            n = dcount[q]
            for i in range(min(n, DMA_POOL)):
                k = n - 1 - i
                sem = dpool[q][k % DMA_POOL]
                wait("sync", sem, 16 * (k // DMA_POOL + 1))
        self.counts = counts
        self.dcount = dcount


def chunk_list():
    ch = []
    for i in range(8):
        ch.append(("q", i))
    for i in range(2):
        ch.append(("k", i))
    for i in range(2):
        ch.append(("ks", i))
    for i in range(4):
        ch.append(("iq", i))
    for i in range(16):
        ch.append(("g", i))
    ch.append(("ik", 0))
    ch.append(("pad", 0))
    for c in range(8):
        ch.append(("rx", c))
        ch.append(("gate", c))
    return ch


V_N1, V_N2, V_CW, V_CB, V_BA, V_BX, V_LAM, V_QW, V_KW, V_IKW, V_GB = 0, 8, 16, 48, 56, 64, 72, 80, 81, 82, 83
NV = 100


def prep_shared(inp):
    w_in = np.asarray(inp["w_in"][0], np.float32)
    cols = []
    for kind, i in chunk_list():
        if kind == "q":
            cols.append(np.arange(2048 + 128 * i, 2048 + 128 * (i + 1)))
        elif kind == "k":
            cols.append(np.arange(3072 + 128 * i, 3072 + 128 * (i + 1)))
        elif kind == "ks":
            b = 3072 + 128 * i
            cols.append(np.concatenate([np.arange(b + 64, b + 128), np.arange(b, b + 64)]))
        elif kind in ("ik", "pad"):
            cols.append(np.concatenate([np.arange(4096, 4160), np.arange(4096, 4160)]))
        elif kind == "iq":
            cols.append(np.arange(3584 + 128 * i, 3584 + 128 * (i + 1)))
        elif kind == "g":
            cols.append(np.arange(4168 + 128 * i, 4168 + 128 * (i + 1)))
        elif kind == "rx":
            cols.append(np.arange(128 * i, 128 * (i + 1)))
        elif kind == "gate":
            cols.append(np.arange(1024 + 128 * i, 1024 + 128 * (i + 1)))
    cols = np.concatenate(cols)
    w_fm = np.ascontiguousarray(w_in[:, cols])
    w_tm = np.ascontiguousarray(np.concatenate([w_in[:, 3328:3584], w_in[:, 4160:4168]], axis=1))

    vec = np.zeros((128, NV), np.float32)

    def colmajor(v):
        return np.asarray(v, np.float32).reshape(8, 128).T

    vec[:, V_N1:V_N1 + 8] = colmajor(inp["norm1_w"][0])
    vec[:, V_N2:V_N2 + 8] = colmajor(inp["norm2_w"][0])
    for j in range(4):
        vec[:, V_CW + 8 * j:V_CW + 8 * j + 8] = colmajor(inp["conv_w"][0][j])
    vec[:, V_CB:V_CB + 8] = colmajor(inp["conv_b"][0])
    vec[:, V_BA:V_BA + 8] = colmajor(inp["rg_ba"][0])
    vec[:, V_BX:V_BX + 8] = colmajor(inp["rg_bx"][0])
    vec[:, V_LAM:V_LAM + 8] = colmajor(inp["rg_lambda"][0])
    vec[:, V_QW] = np.tile(np.asarray(inp["q_norm_w"][0], np.float32), 2)
    vec[:, V_KW] = np.tile(np.asarray(inp["k_norm_w"][0], np.float32), 2)
    vec[:, V_IKW] = np.tile(np.asarray(inp["idx_k_norm_w"][0], np.float32), 2)
    vec[:, V_GB:V_GB + 16] = np.asarray(inp["gate_b"][0], np.float32).reshape(16, 128).T

    def blockdiag(w):
        w = np.asarray(w, np.float32)
        o = np.zeros((128, 8, 128), np.float32)
        for c in range(8):
            o[0:64, c, 0:64] = w[2 * c]
            o[64:128, c, 64:128] = w[2 * c + 1]
        return o

    ident = np.eye(128, dtype=np.float32)
    onesblk = np.zeros((128, 128), np.float32)
    onesblk[0:64, 0:64] = 1.0
    onesblk[64:128, 64:128] = 1.0
    ptab = np.zeros((128, 2 * NIT), np.float32)
    for k in range(NIT):
        ptab[:, k] = 2.0 ** (-(k + 2))
        ptab[:, NIT + k] = 2.0 ** (-(k + 1))
    import ml_dtypes
    bf = ml_dtypes.bfloat16
    shared = {
        "w_fm": w_fm,
        "w_tm": w_tm,
        "vec": vec,
        "wa_bd": blockdiag(inp["rg_wa"][0]),
        "wx_bd": blockdiag(inp["rg_wx"][0]),
        "w_o_rnn": np.ascontiguousarray(np.asarray(inp["w_o_rnn"][0], np.float32)),
        "w_o_attn": np.ascontiguousarray(np.asarray(inp["w_o_attn"][0], np.float32)),
        "w_out": np.ascontiguousarray(np.asarray(inp["w_out"][0], np.float32)),
        "w_ff_in": np.ascontiguousarray(np.asarray(inp["w_ff_in"][0], np.float32)),
        "w_ff_out": np.ascontiguousarray(np.asarray(inp["w_ff_out"][0], np.float32)),
        "ident_bf": ident.astype(bf),
        "e4_bf": np.tile(ident, (1, 4)).astype(bf),
        "onesblk": onesblk,
        "ptab": ptab,
    }
    return shared


def build(T, dbg=(), stop=None):
    NT = T // 512
    NP = T // 128
    nc = bass.Bass("TRN2", target_bir_lowering=False)
    es = ExitStack()
    P = Prog(nc, es)

    def din(name, shape, dt=F32):
        return nc.dram_tensor(name, list(shape), dt, kind="ExternalInput").ap()

    x_d = din("x", [T, D])
    wfm_d = din("w_fm", [D, 6400])
    wtm_d = din("w_tm", [D, 264])
    vec_d = din("vec", [128, NV])
    wabd_d = din("wa_bd", [128, 8, 128])
    wxbd_d = din("wx_bd", [128, 8, 128])
    wor_d = din("w_o_rnn", [D, D])
    woa_d = din("w_o_attn", [D, D])
    wout_d = din("w_out", [D, D])
    w1_d = din("w_ff_in", [D, 4 * D])
    w2_d = din("w_ff_out", [4 * D, D])
    ident_d = din("ident_bf", [128, 128], BF16)
    e4_d = din("e4_bf", [128, 512], BF16)
    onesblk_d = din("onesblk", [128, 128])
    ptab_d = din("ptab", [128, 2 * NIT])
    out_d = nc.dram_tensor("out", [T, D], F32, kind="ExternalOutput").ap()

    qT_d = nc.dram_tensor("qT_s", [8, 128, T], BF16).ap()
    iqT_d = nc.dram_tensor("iqT_s", [4, 128, T], BF16).ap()
    gT_d = nc.dram_tensor("gT_s", [16, 128, T], BF16).ap()
    yrT_d = nc.dram_tensor("yrT_s", [8, 128, T], BF16).ap()
    yaT_d = nc.dram_tensor("yaT_s", [8, 128, T], BF16).ap()
    x1_d = nc.dram_tensor("x1_s", [T, D], F32).ap()
    dbg_out = {}

    def sb(name, shape, dt, stack=None, side=None):
        if side is not None:
            return (stack or es).enter_context(nc.sbuf_tensor("s_" + name, list(shape), dt, side=side))
        return (stack or es).enter_context(nc.sbuf_tensor("s_" + name, list(shape), dt))

    def ps(name, shape, dt, stack):
        return stack.enter_context(nc.psum_tensor("p_" + name.replace("@", ""), list(shape), dt))

    vec = sb("vec", [128, NV], F32)
    ident = sb("ident", [128, 128], BF16)
    e4 = sb("e4", [128, 512], BF16)
    onesblk = sb("onesblk", [128, 128], F32)
    ptab = sb("ptab", [128, 2 * NIT], F32)
    dvec = sb("dvec", [128, 32], F32)
    P.dma(vec[:], vec_d, writes=["vec"])
    P.dma(ident[:], ident_d, writes=["ident"])
    P.dma(e4[:], e4_d, writes=["e4"])
    P.dma(onesblk[:], onesblk_d, writes=["onesblk"])
    P.dma(ptab[:], ptab_d, writes=["ptab"])
    P.dve(lambda e: e.tensor_scalar(out=dvec[:, 0:8], in0=vec[:, V_BA:V_BA + 8], scalar1=0.5, scalar2=None,
                                    op0=ALU.mult), reads=["vec"], writes=["dv0"])
    P.dve(lambda e: e.tensor_scalar(out=dvec[:, 8:16], in0=vec[:, V_BX:V_BX + 8], scalar1=0.5, scalar2=None,
                                    op0=ALU.mult), reads=["vec"], writes=["dv1"])
    P.act(lambda e: e.activation(out=dvec[:, 24:32], in_=vec[:, V_LAM:V_LAM + 8], func=AF.Exp, scale=-1.0),
          reads=["vec"], writes=["dv3"])
    P.act(lambda e: e.activation(out=dvec[:, 24:32], in_=dvec[:, 24:32], func=AF.Ln, bias=1.0),
          reads=["dv3"], writes=["dv3"])
    P.dve(lambda e: e.tensor_scalar(out=dvec[:, 16:24], in0=dvec[:, 24:32], scalar1=-4.0, scalar2=None,
                                    op0=ALU.mult), reads=["dv3"], writes=["dv2"])
    DV = ["dv0", "dv1", "dv2"]
    bscr = sb("bscr", [128, 8], F32)

    def norm_transpose(S, src_tile, src_tok, nsub, dst_fn, wcol0, tp, tag, cnt):
        ssq = S["ssq"][cnt % 2]
        junk = S["junk"]
        hb = S["hb"][cnt % 2]
        tk = "%s%d" % (tag, cnt % 2)
        for j in range(nsub):
            P.act(lambda e, j=j: e.activation(out=junk[:], in_=src_tile[:, j, :], func=AF.Square,
                                              accum_out=ssq[:, j:j + 1]),
                  reads=[src_tok], writes=["junkA", tk + "ssq%d" % j])
        P.act(lambda e: e.activation(out=ssq[:, 4:4 + nsub], in_=ssq[:, 0:nsub], func=AF.Sqrt, scale=1.0 / D,
                                     bias=EPS),
              reads=[tk + "ssq%d" % j for j in range(nsub)], writes=[tk + "sd"])
        P.dve(lambda e: e.reciprocal(out=ssq[:, 8:8 + nsub], in_=ssq[:, 4:4 + nsub]),
              reads=[tk + "sd"], writes=[tk + "rs"])
        for j in range(nsub):
            P.dve(lambda e, j=j: e.tensor_scalar(out=hb[:, j, :], in0=src_tile[:, j, :],
                                                 scalar1=ssq[:, 8 + j:9 + j], scalar2=None, op0=ALU.mult),
                  reads=[src_tok, tk + "rs"], writes=[tk + "hb%d" % j])
        nb = tp.shape[1]
        for kc in range(8):
            bank = kc % nb
            ptok = "@tp%s%d" % (tag, bank)
            for j in range(nsub):
                P.pe(lambda e, j=j, kc=kc, bank=bank: e.transpose(
                    out=tp[:, bank, j * 128:(j + 1) * 128],
                    in_=hb[:, j, kc * 128:(kc + 1) * 128], identity=ident[:]),
                    reads=[tk + "hb%d" % j, "ident"], writes=[ptok])
            dst, dtok = dst_fn(kc)
            src = tp[:, bank, 0:nsub * 128]
            if kc % 2 == 0:
                P.act(lambda e, dst=dst, src=src, kc=kc: e.activation(
                    out=dst, in_=src, func=AF.Copy, scale=vec[:, wcol0 + kc: wcol0 + kc + 1]),
                    reads=[ptok, "vec"], writes=[dtok])
            else:
                P.dve(lambda e, dst=dst, src=src, kc=kc: e.tensor_scalar(
                    out=dst, in0=src, scalar1=vec[:, wcol0 + kc: wcol0 + kc + 1], scalar2=None, op0=ALU.mult),
                    reads=[ptok, "vec"], writes=[dtok])

    att = ExitStack()
    attR = ExitStack()
    KT = sb("KT", [128, 2, T], BF16, attR, side="right")
    KTs = sb("KTs", [128, 2, T], BF16, attR, side="right")
    Vp = sb("Vp", [128, NP, 4, 65], BF16, att)
    ikT = sb("ikT", [128, T], BF16, attR, side="right")
    iw = sb("iw", [128, NP, 8], F32, att)
    absw = sb("absw", [128, NP, 8], F32, att)
    sgnw = sb("sgnw", [128, NP, 8], F32, att)

    pab = ExitStack()
    hT = sb("hT", [128, 8, T], BF16, pab)
    with ExitStack() as pa:
        S = {
            "ssq": [sb("ssq%d" % i, [128, 12], F32, pa) for i in range(2)],
            "junk": sb("junkA", [128, 1024], BF16, pa),
            "hb": [sb("hb%d" % i, [128, 4, 1024], BF16, pa) for i in range(2)],
        }
        xs = [sb("xs%d" % i, [128, 4, 1024], F32, pa) for i in range(2)]
        tp = ps("tpA", [128, 4, 1024], BF16, pa)
        for g4 in range(NT):
            xt = xs[g4 % 2]
            xtok = "xs%d" % (g4 % 2)
            P.dma(xt[:], x_d[g4 * 512:(g4 + 1) * 512, :].rearrange("(j p) d -> p j d", p=128), writes=[xtok])
            norm_transpose(S, xt, xtok, 4,
                           lambda kc, g4=g4: (hT[:, kc, g4 * 512:(g4 + 1) * 512], "hT%d_%d" % (kc, g4)),
                           V_N1, tp, "A", g4)
    P.barrier(bscr[:, 0:1])
    if stop == "A":
        d_ = nc.dram_tensor("dbg_hT", [128, 8, T], BF16, kind="ExternalOutput").ap()
        P.dma(d_, hT[:], reads=["hT%d_%d" % (kc, g4) for kc in range(8) for g4 in range(NT)], writes=["dbg_hT"])
        P.emit()
        return nc, P
    HT_ALL = lambda tt: ["hT%d_%d" % (kc, tt) for kc in range(8)]

    chunks = chunk_list()
    NCH = len(chunks)
    with ExitStack() as pb:
        wbuf = [sb("wb%d" % i, [128, 8, 256], BF16, pb) for i in range(2)]
        wtm = sb("wtm", [128, 8, 264], BF16, pb)
        wabd = sb("wabd", [128, 8, 128], BF16, pb)
        wxbd = sb("wxbd", [128, 8, 128], BF16, pb)
        P.dma(wtm[:], wtm_d.rearrange("(kc p) c -> p kc c", p=128), writes=["wtm"], eng="gpsimd")
        P.dma(wabd[:], wabd_d, writes=["wabd"], eng="gpsimd")
        P.dma(wxbd[:], wxbd_d, writes=["wxbd"], eng="gpsimd")
        pacc = [ps("@pacc%d" % i, [128, 512], F32, pb) for i in range(3)]
        state = {"acc": 0, "hn": 0, "ba": 0, "g": 0}

        def load_w(grp):
            c0 = grp * 256
            wb = wbuf[grp % 2]
            P.dma(wb[:], wfm_d[:, c0:c0 + 256].rearrange("(kc p) c -> p kc c", p=128),
                  writes=["wb%d" % (grp % 2)], eng="gpsimd")

        def project(ci, tt):
            grp, loc = ci // 2, ci % 2
            wb = wbuf[grp % 2]
            a = state["acc"] % 3
            state["acc"] += 1
            pt = pacc[a]
            for kc in range(8):
                P.pe(lambda e: e.matmul(
                    pt[:], lhsT=wb[:, kc, loc * 128:(loc + 1) * 128], rhs=hT[:, kc, tt * 512:(tt + 1) * 512],
                    start=(kc == 0), stop=(kc == 7)),
                    reads=["wb%d" % (grp % 2), "hT%d_%d" % (kc, tt)], writes=["@pacc%d" % a])
            return pt, "@pacc%d" % a

        with ExitStack() as pb1:
            pst = [ps("@pst%d" % i, [128, 512], F32, pb1) for i in range(2)]
            hn0 = {k: sb("hn_%s" % k, [128, 512], F32, pb1) for k in ("sq", "qs", "sd")}
            hst = [sb("hst%d" % i, [128, 512], BF16, pb1) for i in range(2)]
            gst = [sb("gst%d" % i, [128, 512], BF16, pb1) for i in range(2)]

            def headnorm(pt, ptok, wcol, dst, dtok):
                i = state["hn"] % 2
                state["hn"] += 1
                B = hn0
                t = "hn0"
                P.act(lambda e: e.activation(out=B["sq"][:], in_=pt[:], func=AF.Square), reads=[ptok], writes=[t + "sq"])
                P.dve(lambda e: e.tensor_copy(out=B["qs"][:], in_=pt[:]), reads=[ptok], writes=[t + "qs"])
                P.pe(lambda e: e.matmul(pst[i][:], lhsT=onesblk[:], rhs=B["sq"][:], start=True, stop=True),
                     reads=[t + "sq", "onesblk"], writes=["@pst%d" % i])
                P.act(lambda e: e.activation(out=B["sd"][:], in_=pst[i][:], func=AF.Ln, scale=1.0 / 64, bias=EPS),
                      reads=["@pst%d" % i], writes=[t + "sd"])
                P.act(lambda e: e.activation(out=B["sd"][:], in_=B["sd"][:], func=AF.Exp, scale=-0.5),
                      reads=[t + "sd"], writes=[t + "sd"])
                P.dve(lambda e: e.scalar_tensor_tensor(out=dst, in0=B["qs"][:], scalar=vec[:, wcol:wcol + 1],
                                                       in1=B["sd"][:], op0=ALU.mult, op1=ALU.mult),
                      reads=[t + "qs", t + "sd", "vec"], writes=[dtok])

            for ci, (kind, idx) in enumerate(chunks):
                if kind in ("rx", "gate"):
                    break
                if ci % 2 == 0:
                    load_w(ci // 2)
                if kind == "pad":
                    continue
                for tt in range(NT):
                    cs = slice(tt * 512, (tt + 1) * 512)
                    pt, ptok = project(ci, tt)
                    if kind == "q":
                        i = state["hn"] % 2
                        headnorm(pt, ptok, V_QW, hst[i][:], "hst%d" % i)
                        P.dma(qT_d[idx, :, cs], hst[i][:], reads=["hst%d" % i], writes=["qT_d%d" % tt])
                    elif kind == "k":
                        headnorm(pt, ptok, V_KW, KT[:, idx, cs], "KT%d" % tt)
                    elif kind == "ks":
                        headnorm(pt, ptok, V_KW, KTs[:, idx, cs], "KTs%d" % tt)
                    elif kind == "ik":
                        headnorm(pt, ptok, V_IKW, ikT[:, cs], "ikT%d" % tt)
                    elif kind == "iq":
                        i = state["g"] % 2
                        state["g"] += 1
                        P.act(lambda e: e.activation(out=gst[i][:], in_=pt[:], func=AF.Copy),
                              reads=[ptok], writes=["gst%d" % i])
                        P.dma(iqT_d[idx, :, cs], gst[i][:], reads=["gst%d" % i], writes=["iqT_d%d" % tt])
                    elif kind == "g":
                        i = state["g"] % 2
                        state["g"] += 1
                        P.act(lambda e: e.activation(
                            out=gst[i][:], in_=pt[:], func=AF.Sigmoid, bias=vec[:, V_GB + idx:V_GB + idx + 1]),
                            reads=[ptok, "vec"], writes=["gst%d" % i])
                        P.dma(gT_d[idx, :, cs], gst[i][:], reads=["gst%d" % i], writes=["gT_d%d" % tt])
                if kind == "iq" and idx == 3:
                    for tk in range(NP):
                        a = state["acc"] % 3
                        state["acc"] += 1
                        pt = pacc[a]
                        for kc in range(8):
                            P.pe(lambda e: e.matmul(
                                pt[:, 0:264], lhsT=hT[:, kc, tk * 128:(tk + 1) * 128], rhs=wtm[:, kc, :],
                                start=(kc == 0), stop=(kc == 7)),
                                reads=["wtm", "hT%d_%d" % (kc, tk // 4)], writes=["@pacc%d" % a])
                        P.act(lambda e: e.activation(
                            out=Vp[:, tk, :, 0:64], in_=pt[:, 0:256].rearrange("p (g d) -> p g d", d=64), func=AF.Copy),
                            reads=["@pacc%d" % a], writes=["Vp%d" % tk])
                        P.dve(lambda e: e.tensor_copy(out=iw[:, tk, :], in_=pt[:, 256:264]),
                              reads=["@pacc%d" % a], writes=["iw%d" % tk])
                    P.dve(lambda e: e.memset(Vp[:, :, :, 64:65], 1.0), writes=["Vp1"])
                    P.act(lambda e: e.activation(out=absw[:], in_=iw[:], func=AF.Abs),
                          reads=["iw%d" % tk for tk in range(NP)], writes=["absw"])
                    P.act(lambda e: e.activation(out=sgnw[:], in_=iw[:], func=AF.Sign),
                          reads=["iw%d" % tk for tk in range(NP)], writes=["sgnw"])
        P.barrier(bscr[:, 4:5])

        with ExitStack() as pb2:
            pbd = [ps("@pbd%d" % i, [128, 512], F32, pb2) for i in range(4)]
            BA = []
            for i in range(4):
                d_ = {k: sb("ba_%s%d" % (k, i), [128, 512], F32, pb2) for k in ("xc", "tha", "thi", "aa", "hs", "gl")}
                d_["xr"] = sb("ba_xr%d" % i, [128, 515], F32, pb2)
                d_["xcb"] = sb("ba_xcb%d" % i, [128, 512], BF16, pb2)
                d_["yb"] = sb("ba_yb%d" % i, [128, 512], BF16, pb2)
                BA.append(d_)
            CI0 = [ci for ci, (kind, idx) in enumerate(chunks) if kind == "rx"][0]
            work = [(c, pr) for c in range(8) for pr in range(NT // 2)]
            prev = {"set": None, "hs": None}

            def front(wi):
                c, pr = work[wi]
                ci_rx = CI0 + 2 * c
                if pr == 0:
                    load_w(ci_rx // 2)
                for X in range(2):
                    tt = 2 * pr + X
                    si = 2 * (wi % 2) + X
                    B = BA[si]
                    t = "ba%d" % si
                    pt, ptok = project(ci_rx, tt)
                    P.act(lambda e: e.activation(out=B["xr"][:, 3:515], in_=pt[:], func=AF.Copy),
                          reads=[ptok], writes=[t + "xr"])
                    if tt == 0:
                        P.pool(lambda e: e.memset(B["xr"][:, 0:3], 0.0), writes=[t + "xh"])
                    else:
                        pB = BA[prev["set"]]
                        P.pool(lambda e: e.tensor_copy(out=B["xr"][:, 0:3], in_=pB["xr"][:, 512:515]),
                               reads=["ba%dxr" % prev["set"]], writes=[t + "xh"])
                    prev["set"] = si
                    rd = [t + "xr", t + "xh", "vec"]
                    P.dve(lambda e: e.tensor_scalar(
                        out=B["xc"][:], in0=B["xr"][:, 0:512], scalar1=vec[:, V_CW + c:V_CW + c + 1],
                        scalar2=vec[:, V_CB + c:V_CB + c + 1], op0=ALU.mult, op1=ALU.add),
                        reads=rd, writes=[t + "xc"])
                    for j in range(1, 4):
                        P.dve(lambda e: e.scalar_tensor_tensor(
                            out=B["xc"][:], in0=B["xr"][:, j:j + 512],
                            scalar=vec[:, V_CW + 8 * j + c:V_CW + 8 * j + c + 1], in1=B["xc"][:],
                            op0=ALU.mult, op1=ALU.add),
                            reads=rd + [t + "xc"], writes=[t + "xc"])
                    P.pool(lambda e: e.tensor_copy(out=B["xcb"][:], in_=B["xc"][:]), reads=[t + "xc"],
                           writes=[t + "xcb"])
                    ptg, ptokg = project(ci_rx + 1, tt)
                    P.act(lambda e: e.activation(out=B["gl"][:], in_=ptg[:], func=AF.Gelu_apprx_tanh),
                          reads=[ptokg], writes=[t + "gl"])

            def back(wi):
                c, pr = work[wi]
                sets = [2 * (wi % 2), 2 * (wi % 2) + 1]
                for X in range(2):
                    B = BA[sets[X]]
                    t = "ba%d" % sets[X]
                    P.pe(lambda e: e.matmul(pbd[2 * X][:], lhsT=wabd[:, c, :], rhs=B["xcb"][:], start=True, stop=True),
                         reads=[t + "xcb", "wabd"], writes=["@pbd%d" % (2 * X)])
                    P.pe(lambda e: e.matmul(pbd[2 * X + 1][:], lhsT=wxbd[:, c, :], rhs=B["xcb"][:], start=True, stop=True),
                         reads=[t + "xcb", "wxbd"], writes=["@pbd%d" % (2 * X + 1)])
                for X in range(2):
                    B = BA[sets[X]]
                    t = "ba%d" % sets[X]
                    P.act(lambda e: e.activation(out=B["tha"][:], in_=pbd[2 * X][:], func=AF.Tanh, scale=0.5,
                                                 bias=dvec[:, c:c + 1]),
                          reads=["@pbd%d" % (2 * X)] + DV, writes=[t + "tha"])
                    P.act(lambda e: e.activation(out=B["thi"][:], in_=pbd[2 * X + 1][:], func=AF.Tanh, scale=0.5,
                                                 bias=dvec[:, 8 + c:9 + c]),
                          reads=["@pbd%d" % (2 * X + 1)] + DV, writes=[t + "thi"])
                for X in range(2):
                    B = BA[sets[X]]
                    t = "ba%d" % sets[X]
                    P.act(lambda e: e.activation(out=B["aa"][:], in_=B["tha"][:], func=AF.Exp,
                                                 scale=dvec[:, 16 + c:17 + c], bias=dvec[:, 16 + c:17 + c]),
                          reads=[t + "tha"] + DV, writes=[t + "aa"])
                for X in range(2):
                    B = BA[sets[X]]
                    t = "ba%d" % sets[X]
                    P.act(lambda e: e.activation(out=B["tha"][:], in_=B["aa"][:], func=AF.Square),
                          reads=[t + "aa"], writes=[t + "tha"])
                for X in range(2):
                    B = BA[sets[X]]
                    t = "ba%d" % sets[X]
                    P.act(lambda e: e.activation(out=B["tha"][:], in_=B["tha"][:], func=AF.Sqrt, scale=-0.25, bias=0.25),
                          reads=[t + "tha"], writes=[t + "tha"])
                for X in range(2):
                    tt = 2 * pr + X
                    B = BA[sets[X]]
                    t = "ba%d" % sets[X]
                    P.dve(lambda e: e.scalar_tensor_tensor(out=B["thi"][:], in0=B["thi"][:], scalar=1.0, in1=B["xc"][:],
                                                           op0=ALU.add, op1=ALU.mult),
                          reads=[t + "thi", t + "xc"], writes=[t + "thi"])
                    P.dve(lambda e: e.tensor_tensor(out=B["thi"][:], in0=B["thi"][:], in1=B["tha"][:], op=ALU.mult),
                          reads=[t + "thi", t + "tha"], writes=[t + "thi"])
                    if tt == 0:
                        init, rdi = 0.0, []
                    else:
                        init, rdi = prev["hs"]
                    P.dve(lambda e: e.tensor_tensor_scan(out=B["hs"][:], data0=B["aa"][:], data1=B["thi"][:],
                                                         initial=init, op0=ALU.mult, op1=ALU.add),
                          reads=[t + "aa", t + "thi"] + rdi, writes=[t + "hs"])
                    prev["hs"] = (B["hs"][:, 511:512], [t + "hs"])
                    P.dve(lambda e: e.tensor_tensor(out=B["yb"][:], in0=B["hs"][:], in1=B["gl"][:], op=ALU.mult),
                          reads=[t + "hs", t + "gl"], writes=[t + "yb"])
                    P.dma(yrT_d[c, :, tt * 512:(tt + 1) * 512], B["yb"][:], reads=[t + "yb"], writes=["yrT_d%d" % tt])

            front(0)
            for wi in range(len(work)):
                if wi + 1 < len(work):
                    front(wi + 1)
                back(wi)
    pab.close()
    P.barrier(bscr[:, 1:2])

    if "B" in dbg:
        for name, t_, shp, dt in (("KT", KT, [128, 2, T], BF16), ("KTs", KTs, [128, 2, T], BF16),
                                  ("Vp", Vp, [128, NP, 4, 65], BF16), ("ikT", ikT, [128, T], BF16),
                                  ("iw", iw, [128, NP, 8], F32)):
            d_ = nc.dram_tensor("dbg_" + name, shp, dt, kind="ExternalOutput").ap()
            rd = {"KT": ["KT%d" % i for i in range(NT)], "KTs": ["KTs%d" % i for i in range(NT)],
                  "Vp": ["Vp%d" % i for i in range(NP)] + ["Vp1"], "ikT": ["ikT%d" % i for i in range(NT)],
                  "iw": ["iw%d" % i for i in range(NP)]}[name]
            P.dma(d_, t_[:], reads=rd, writes=["dbg_" + name])
        for name, src, n in (("qT", qT_d, 8), ("iqT", iqT_d, 4), ("gT", gT_d, 16), ("yrT", yrT_d, 8)):
            d_ = nc.dram_tensor("dbg_" + name, [n, 128, T], BF16, kind="ExternalOutput").ap()
            P.dma(d_, src, reads=["%s_d%d" % (name, i) for i in range(NT)], writes=["dbg_" + name])

    if stop == "B":
        P.emit()
        return nc, P
    QT_RD = ["qT_d%d" % i for i in range(NT)]
    IQ_RD = ["iqT_d%d" % i for i in range(NT)]
    LA = 2
    with ExitStack() as pt_:
        KZ = sb("KZ", [128, 8, T], BF16, pt_)
        ikZ = sb("ikZ", [128, 2, T], BF16, pt_)
        KALL = ["KT%d" % i for i in range(NT)] + ["KTs%d" % i for i in range(NT)]
        ci_ = 0
        for g in range(4):
            for hp in range(2):
                src_t = KT if (g % 2) == hp else KTs
                src = src_t[hp * 64:(hp + 1) * 64, g // 2, :]
                dst = KZ[hp * 64:(hp + 1) * 64, g * 2 + hp, :]
                zdst = KZ[(1 - hp) * 64:(2 - hp) * 64, g * 2 + hp, :]
                if ci_ % 2 == 0:
                    P.pool(lambda e: e.memset(zdst, 0.0), writes=["KZz_%d" % ci_])
                    P.dve(lambda e: e.tensor_copy(out=dst, in_=src), reads=KALL, writes=["KZ_%d" % ci_])
                else:
                    P.dve(lambda e: e.memset(zdst, 0.0), writes=["KZz_%d" % ci_])
                    P.act(lambda e: e.activation(out=dst, in_=src, func=AF.Copy), reads=KALL, writes=["KZ_%d" % ci_])
                ci_ += 1
        for hp in range(2):
            P.pool(lambda e: e.memset(ikZ[(1 - hp) * 64:(2 - hp) * 64, hp, :], 0.0), writes=["ikZz_%d" % hp])
            P.dve(lambda e: e.tensor_copy(out=ikZ[hp * 64:(hp + 1) * 64, hp, :], in_=ikT[hp * 64:(hp + 1) * 64, :]),
                  reads=["ikT%d" % i for i in range(NT)], writes=["ikZ_%d" % hp])
        KZ_RD = ["KZ_%d" % i for i in range(8)] + ["KZz_%d" % i for i in range(8)]
        IKZ_RD = ["ikZ_0", "ikZ_1", "ikZz_0", "ikZz_1"]
        P.barrier(bscr[:, 5:6])
        attR.close()
        NSC = 2
        sc = [sb("sc%d" % i, [128, T], F32, pt_) for i in range(NSC)]
        negm = [sb("negm%d" % i, [128, T], BF16, pt_) for i in range(NSC)]
        junkc = sb("junkc", [128, T], BF16, pt_)
        qp = [sb("qp%d" % i, [128, 8, 128], BF16, pt_) for i in range(2)]
        iqp = [sb("iqp%d" % i, [128, 4, 128], BF16, pt_) for i in range(NSC)]
        NRB = 10
        Rb = [sb("Rb%d" % i, [128, 512], BF16, pt_) for i in range(NRB)]
        dsg = [sb("dsg%d" % i, [128, 8, 128], BF16, pt_) for i in range(NSC)]
        NPTB = 6
        ptb = [sb("ptb%d" % i, [128, 512], BF16, pt_) for i in range(NPTB)]
        bs = [sb("bs%d" % i, [128, 8 + 2 * NIT], F32, pt_) for i in range(NSC)]
        rec = sb("rec", [128, 16], F32, pt_)
        yst = [sb("yst%d" % i, [128, 16, 64], BF16, pt_) for i in range(2)]
        accs = [sb("accs%d" % i, [128, 3, 512], F32, pt_) for i in range(2)]
        yT = [sb("yT%d" % i, [128, 8, 128], BF16, pt_) for i in range(2)]
        thc = sb("thc", [128, 1], F32, pt_)
        P.dve(lambda e: e.memset(thc[:], -1e29), writes=["thc"])
        st = [ps("@st%d" % i, [128, 512], F32, pt_) for i in range(2)]
        psc = ps("@psc", [128, 512], F32, pt_)
        acc = ps("acc", [128, 3, 512], F32, pt_)
        pmixs = [ps("pmix%d" % i, [128, 512], F32, pt_) for i in range(2)]
        pmix = pmixs[0]
        ptr = pmix[:].bitcast(BF16)

        cnt = {"R": 0, "pt": 0, "st": 0, "px": 0}

        def indexer_units(m):
            V = 128 * (m + 1)
            sbi = m % NSC
            units = []
            pend = {"f": None}

            def load():
                P.dma(iqp[sbi][:], iqT_d[:, :, m * 128:(m + 1) * 128].rearrange("c p t -> p c t"),
                      reads=IQ_RD, writes=["iqp%d" % sbi])
                for h in range(8):
                    P.pool(lambda e: e.tensor_scalar(out=dsg[sbi][:, h, :], in0=ident[:], scalar1=iw[:, m, h:h + 1],
                                                     scalar2=None, op0=ALU.mult),
                           reads=["ident", "iw%d" % m], writes=["dsg%d_%d" % (sbi, h)])
            units.append(load)
            nkb = (V + 511) // 512
            for kb in range(nkb):
                ncol = min(512, V - 512 * kb)
                for h in range(8):
                    def unit(kb=kb, ncol=ncol, h=h):
                        hp = h % 2
                        r = cnt["R"] % NRB
                        cnt["R"] += 1
                        px = cnt["px"] % 2
                        cnt["px"] += 1
                        pm = pmixs[px]
                        pmt = "@pmix%d" % px
                        P.pe(lambda e: e.matmul(pm[:, 0:ncol], lhsT=iqp[sbi][:, h // 2, :],
                                                rhs=ikZ[:, hp, kb * 512:kb * 512 + ncol],
                                                start=True, stop=True),
                             reads=["iqp%d" % sbi] + IKZ_RD, writes=[pmt])
                        P.act(lambda e: e.activation(out=Rb[r][:, 0:ncol], in_=pm[:, 0:ncol], func=AF.Relu),
                              reads=[pmt], writes=["Rb%d" % r])
                        if pend["f"] is not None:
                            pend["f"]()

                        def dsum():
                            P.pe(lambda e: e.matmul(psc[:, 0:ncol], lhsT=dsg[sbi][:, h, :], rhs=Rb[r][:, 0:ncol],
                                                    start=(h == 0), stop=(h == 7)),
                                 reads=["Rb%d" % r, "dsg%d_%d" % (sbi, h)], writes=["@psc"])
                            if h == 7:
                                P.act(lambda e: e.activation(out=sc[sbi][:, kb * 512:kb * 512 + ncol],
                                                             in_=psc[:, 0:ncol], func=AF.Copy),
                                      reads=["@psc"], writes=["sc%d_%d" % (sbi, kb)])
                        pend["f"] = dsum
                    units.append(unit)

            def flush():
                if pend["f"] is not None:
                    pend["f"]()
                    pend["f"] = None
            units.append(flush)
            return units

        def threshold_steps(m):
            V = 128 * (m + 1)
            sbi = m % NSC
            nkb = (V + 511) // 512
            SCT = ["sc%d_%d" % (sbi, kb) for kb in range(nkb)]
            b = bs[sbi]
            bt = "bs%d" % sbi
            s = sc[sbi]
            SCT2 = SCT + [bt + "ms"]
            steps = []

            def init():
                if m >= 2:
                    P.dve(lambda e: e.tensor_reduce(out=b[:, 0:1], in_=s[:, 0:V], axis=AX.X, op=ALU.max),
                          reads=SCT, writes=[bt + "mx"])
                    P.dve(lambda e: e.tensor_reduce(out=b[:, 1:2], in_=s[:, 0:V], axis=AX.X, op=ALU.min),
                          reads=SCT, writes=[bt + "mn"])
                P.dve(lambda e: e.memset(s[0:64, V - 64:V], -1e30), reads=SCT + [bt + "mx", bt + "mn"],
                      writes=[bt + "ms"])
                if m >= 2:
                    P.dve(lambda e: e.tensor_scalar(out=b[:, 2:3], in0=b[:, 0:1], scalar1=b[:, 1:2], scalar2=0.5,
                                                    op0=ALU.add, op1=ALU.mult),
                          reads=[bt + "mx", bt + "mn"], writes=[bt + "th"])
                    P.dve(lambda e: e.tensor_tensor(out=b[:, 3:4], in0=b[:, 0:1], in1=b[:, 1:2], op=ALU.subtract),
                          reads=[bt + "mx", bt + "mn"], writes=[bt + "rg"])
                    P.dve(lambda e: e.tensor_scalar(out=b[:, 8:8 + 2 * NIT], in0=ptab[:], scalar1=b[:, 3:4],
                                                    scalar2=None, op0=ALU.mult),
                          reads=[bt + "rg", "ptab"], writes=[bt + "tab"])
            steps.append(init)
            if m >= 2:
                for k in range(NIT):
                    def it(k=k):
                        P.dve(lambda e: e.tensor_scalar(out=junkc[:, 0:V], in0=s[:, 0:V], scalar1=b[:, 2:3],
                                                        scalar2=None, op0=ALU.is_ge, op1=ALU.add,
                                                        accum_out=b[:, 4:5]),
                              reads=SCT2 + [bt + "th"], writes=["junkc", bt + "cnt"])
                        P.dve(lambda e: e.tensor_scalar(out=b[:, 5:6], in0=b[:, 4:5], scalar1=TOPK - 0.5,
                                                        scalar2=b[:, 8 + NIT + k:9 + NIT + k],
                                                        op0=ALU.is_ge, op1=ALU.mult),
                              reads=[bt + "cnt", bt + "tab"], writes=[bt + "d"])
                        P.dve(lambda e: e.scalar_tensor_tensor(out=b[:, 2:3], in0=b[:, 2:3],
                                                               scalar=b[:, 8 + k:9 + k], in1=b[:, 5:6],
                                                               op0=ALU.subtract, op1=ALU.add),
                              reads=[bt + "th", bt + "d", bt + "tab"], writes=[bt + "th"])
                    steps.append(it)
                thap, thtok = b[:, 2:3], bt + "th"
            else:
                thap, thtok = thc[:], "thc"

            def fin():
                P.dve(lambda e: e.tensor_scalar(out=negm[sbi][:, 0:V], in0=s[:, 0:V], scalar1=thap, scalar2=NEG,
                                                op0=ALU.is_lt, op1=ALU.mult),
                      reads=SCT2 + [thtok], writes=["negm%d" % sbi])
            steps.append(fin)
            return steps

        def merge(a, b_):
            out_ = []
            ia = ib = 0
            na, nb_ = len(a), len(b_)
            while ia < na or ib < nb_:
                if ib >= nb_ or (ia < na and ia * nb_ <= ib * na):
                    out_.append(a[ia])
                    ia += 1
                else:
                    out_.append(b_[ib])
                    ib += 1
            return out_

        def attention(m, inter, deferred=None):
            sbi = m % NSC
            qb = m % 2
            if m == 0:
                P.dma(qp[0][:], qT_d[:, :, 0:128].rearrange("c p t -> p c t"), reads=QT_RD, writes=["qp0"])
            if m + 1 < NP:
                P.dma(qp[1 - qb][:], qT_d[:, :, (m + 1) * 128:(m + 2) * 128].rearrange("c p t -> p c t"),
                      reads=QT_RD, writes=["qp%d" % (1 - qb)])
            started = set()
            units = [(j, b2, hp) for j in range(m + 1) for b2 in range(2) for hp in range(2)]
            nun = len(units)
            ii = 0
            pending = None

            def emit_pv(u, pb_):
                j, b2, hp = u
                for gl in range(2):
                    for rr in range(2):
                        g = 2 * b2 + gl
                        head = 4 * g + hp + 2 * rr
                        bank = head // 7
                        off = (head % 7) * 65
                        first = (j == 0) and (bank not in started)
                        started.add(bank)
                        P.pe(lambda e: e.matmul(acc[:, bank, off:off + 65],
                                                lhsT=ptb[pb_][:, gl * 256 + rr * 128: gl * 256 + (rr + 1) * 128],
                                                rhs=Vp[:, j, g, :], start=first, stop=(j == m), skip_group_check=True),
                             reads=["ptb%d" % pb_, "Vp%d" % j, "Vp1"], writes=["@acc"])

            for ui, (j, b2, hp) in enumerate(units):
                k = cnt["st"] % 2
                cnt["st"] += 1
                stt = "@st%d" % k
                for gl in range(2):
                    g = 2 * b2 + gl
                    P.pe(lambda e: e.matmul(
                        st[k][:, gl * 256:(gl + 1) * 256],
                        lhsT=KZ[:, g * 2 + hp, j * 128:(j + 1) * 128],
                        rhs=qp[qb][:, 2 * g:2 * g + 2, :],
                        start=(gl == 0), stop=False, skip_group_check=True),
                        reads=["qp%d" % qb] + KZ_RD, writes=[stt])
                P.pe(lambda e: e.matmul(
                    st[k][:], lhsT=negm[sbi][:, j * 128:(j + 1) * 128], rhs=e4[:],
                    start=False, stop=True, skip_group_check=True),
                    reads=["negm%d" % sbi, "e4"], writes=[stt])
                pb_ = cnt["pt"] % NPTB
                cnt["pt"] += 1
                P.act(lambda e: e.activation(out=ptb[pb_][:], in_=st[k][:], func=AF.Exp, scale=0.125),
                      reads=[stt], writes=["ptb%d" % pb_])
                if pending is not None:
                    emit_pv(*pending)
                pending = ((j, b2, hp), pb_)
                if deferred is not None and ui == min(6, nun - 1):
                    deferred()
                    deferred = None
                tgt = (len(inter) * (ui + 1) + nun - 1) // nun
                while ii < tgt and ii < len(inter):
                    inter[ii]()
                    ii += 1
            emit_pv(*pending)
            while ii < len(inter):
                inter[ii]()
                ii += 1
            ab = m % 2
            for bank in range(3):
                ncw = 65 * (7 if bank < 2 else 2)
                P.act(lambda e: e.activation(out=accs[ab][:, bank, 0:ncw], in_=acc[:, bank, 0:ncw], func=AF.Copy),
                      reads=["@acc"], writes=["accs%d_%d" % (ab, bank)])
            for bank in range(3):
                nh = 7 if bank < 2 else 2
                v3 = accs[ab][:, bank, 0:nh * 65].rearrange("p (h d) -> p h d", d=65)
                P.dve(lambda e: e.reciprocal(out=rec[:, bank * 7:bank * 7 + nh], in_=v3[:, :, 64]),
                      reads=["accs%d_%d" % (ab, bank)], writes=["rec%d" % bank])
                P.dve(lambda e: e.tensor_tensor(
                    out=yst[ab][:, bank * 7:bank * 7 + nh, :], in0=v3[:, :, 0:64],
                    in1=rec[:, bank * 7:bank * 7 + nh].unsqueeze(2).to_broadcast([128, nh, 64]), op=ALU.mult),
                    reads=["accs%d_%d" % (ab, bank), "rec%d" % bank], writes=["yst%d_%d" % (ab, bank)])

            def finish():
                for kc in range(8):
                    P.pe(lambda e: e.transpose(out=ptr[:, kc * 128:(kc + 1) * 128],
                                               in_=yst[ab][:, 2 * kc:2 * kc + 2, :].rearrange("p h d -> p (h d)"),
                                               identity=ident[:]),
                         reads=["yst%d_%d" % (ab, b_) for b_ in range(3)] + ["ident"], writes=["@pmix0"])
                P.act(lambda e: e.activation(out=yT[ab][:], in_=ptr.rearrange("p (c t) -> p c t", t=128), func=AF.Copy),
                      reads=["@pmix0"], writes=["yT%d" % ab])
                P.dma(yaT_d[:, :, m * 128:(m + 1) * 128].rearrange("c p t -> p c t"), yT[ab][:],
                      reads=["yT%d" % ab], writes=["yaT_d%d" % (m // 4)])
            return finish

        for u in indexer_units(0):
            u()
        for u in threshold_steps(0):
            u()
        if NP > 1:
            for u in indexer_units(1):
                u()
        fin_prev = None
        for m in range(NP):
            idx_u = indexer_units(m + 2) if m + 2 < NP else []
            thr_u = threshold_steps(m + 1) if m + 1 < NP else []
            fin_prev = attention(m, merge(idx_u, thr_u), fin_prev)
        fin_prev()
    att.close()
    P.barrier(bscr[:, 2:3])

    if "ATT" in dbg:
        d_ = nc.dram_tensor("dbg_yaT", [8, 128, T], BF16, kind="ExternalOutput").ap()
        P.dma(d_, yaT_d, reads=["yaT_d%d" % i for i in range(NT)], writes=["dbg_yaT"])

    if stop == "ATT":
        P.emit()
        return nc, P
    with ExitStack() as pc:
        wor = sb("wor", [128, 8, D], BF16, pc)
        woa = sb("woa", [128, 8, D], BF16, pc)
        wout = sb("wout", [128, 8, D], BF16, pc)
        for nm, t_, d_ in (("wor", wor, wor_d), ("woa", woa, woa_d), ("wout", wout, wout_d)):
            for kc in range(8):
                P.dma(t_[:, kc, :], d_[kc * 128:(kc + 1) * 128, :], writes=["%s%d" % (nm, kc)], eng="gpsimd")
        WOR = ["wor%d" % k for k in range(8)]
        WOA = ["woa%d" % k for k in range(8)]
        WOUT = ["wout%d" % k for k in range(8)]
        yr = [sb("yr%d" % i, [128, 8, 512], BF16, pc) for i in range(2)]
        ya = [sb("ya%d" % i, [128, 8, 512], BF16, pc) for i in range(2)]
        gt = [sb("gt%d" % i, [128, 16, 512], BF16, pc) for i in range(2)]
        xt2 = [sb("xt2%d" % i, [128, 4, D], F32, pc) for i in range(2)]
        x1t = xt2
        mg = [sb("mg%d" % i, [128, 8, 512], BF16, pc) for i in range(2)]
        tmp = [sb("tmpc%d" % i, [128, 512], F32, pc) for i in range(2)]
        tmp2 = [sb("tmpd%d" % i, [128, 512], F32, pc) for i in range(2)]
        pA = [ps("@pA%d" % i, [128, 512], F32, pc) for i in range(2)]
        pB = [ps("@pB%d" % i, [128, 512], F32, pc) for i in range(2)]
        pO = [ps("@pO%d" % i, [128, 512], F32, pc) for i in range(2)]
        k = 0
        def c_loads(tt):
            b = tt % 2
            cs = slice(tt * 512, (tt + 1) * 512)
            P.dma(yr[b][:], yrT_d[:, :, cs].rearrange("c p t -> p c t"), reads=["yrT_d%d" % tt], writes=["yr%d" % b])
            P.dma(ya[b][:], yaT_d[:, :, cs].rearrange("c p t -> p c t"), reads=["yaT_d%d" % tt], writes=["ya%d" % b])
            P.dma(gt[b][:], gT_d[:, :, cs].rearrange("c p t -> p c t"), reads=["gT_d%d" % tt], writes=["gt%d" % b])
            P.dma(xt2[b][:], x_d[cs, :].rearrange("(j p) d -> p j d", p=128),
                  writes=["xt2%d" % b] + ["x1t%d_%d" % (b, s_) for s_ in range(4)])

        c_loads(0)
        for tt in range(NT):
            b = tt % 2
            cs = slice(tt * 512, (tt + 1) * 512)
            if tt + 1 < NT:
                c_loads(tt + 1)
            for mc in range(8):
                i = k % 2
                k += 1
                for kc in range(8):
                    P.pe(lambda e, kc=kc, mc=mc, i=i, b=b: e.matmul(
                        pA[i][:], lhsT=wor[:, kc, mc * 128:(mc + 1) * 128], rhs=yr[b][:, kc, :],
                        start=(kc == 0), stop=(kc == 7)),
                        reads=["wor%d" % kc, "yr%d" % b], writes=["@pA%d" % i])
                for kc in range(8):
                    P.pe(lambda e, kc=kc, mc=mc, i=i, b=b: e.matmul(
                        pB[i][:], lhsT=woa[:, kc, mc * 128:(mc + 1) * 128], rhs=ya[b][:, kc, :],
                        start=(kc == 0), stop=(kc == 7)),
                        reads=["woa%d" % kc, "ya%d" % b], writes=["@pB%d" % i])
                P.dve(lambda e, i=i, b=b, mc=mc: e.tensor_tensor(out=tmp[i][:], in0=pA[i][:], in1=gt[b][:, mc, :],
                                                                 op=ALU.mult),
                      reads=["@pA%d" % i, "gt%d" % b], writes=["tmpc%d" % i])
                P.dve(lambda e, i=i, b=b, mc=mc: e.tensor_tensor(out=tmp2[i][:], in0=pB[i][:], in1=gt[b][:, 8 + mc, :],
                                                                 op=ALU.mult),
                      reads=["@pB%d" % i, "gt%d" % b], writes=["tmpd%d" % i])
                P.pool(lambda e, i=i, b=b, mc=mc: e.tensor_tensor(out=mg[b][:, mc, :], in0=tmp[i][:], in1=tmp2[i][:],
                                                                  op=ALU.add),
                       reads=["tmpc%d" % i, "tmpd%d" % i], writes=["mg%d_%d" % (b, mc)])
            for sub in range(4):
                for half in range(2):
                    i = k % 2
                    k += 1
                    for mc in range(8):
                        P.pe(lambda e, mc=mc, sub=sub, half=half, i=i, b=b: e.matmul(
                            pO[i][:], lhsT=mg[b][:, mc, sub * 128:(sub + 1) * 128],
                            rhs=wout[:, mc, half * 512:(half + 1) * 512], start=(mc == 0), stop=(mc == 7)),
                            reads=["mg%d_%d" % (b, mc), "wout%d" % mc], writes=["@pO%d" % i])
                    P.dve(lambda e, sub=sub, half=half, i=i, b=b: e.tensor_tensor(
                        out=x1t[b][:, sub, half * 512:(half + 1) * 512], in0=pO[i][:],
                        in1=xt2[b][:, sub, half * 512:(half + 1) * 512], op=ALU.add),
                        reads=["@pO%d" % i, "xt2%d" % b], writes=["x1t%d_%d" % (b, sub)])
            P.dma(x1_d[cs, :].rearrange("(j p) d -> p j d", p=128), x1t[b][:],
                  reads=["x1t%d_%d" % (b, s_) for s_ in range(4)], writes=["x1_d%d" % tt])

    P.barrier(bscr[:, 3:4])
    if "C" in dbg:
        d_ = nc.dram_tensor("dbg_x1", [T, D], F32, kind="ExternalOutput").ap()
        P.dma(d_, x1_d, reads=["x1_d%d" % i for i in range(NT)], writes=["dbg_x1"])

    with ExitStack() as pd:
        w1 = sb("w1", [128, 8, 4 * D], BF16, pd)
        w2 = sb("w2", [128, 32, D], BF16, pd)
        for kc in range(8):
            for q4 in range(2):
                P.dma(w1[:, kc, q4 * 2048:(q4 + 1) * 2048], w1_d[kc * 128:(kc + 1) * 128, q4 * 2048:(q4 + 1) * 2048],
                      writes=["w1_%d" % kc], eng="gpsimd")
        for f in range(32):
            P.dma(w2[:, f, :], w2_d[f * 128:(f + 1) * 128, :], writes=["w2_%d" % f], eng="gpsimd")
        S2 = {
            "ssq": [sb("ssqD%d" % i, [128, 12], F32, pd) for i in range(2)],
            "junk": sb("junkD", [128, 1024], BF16, pd),
            "hb": [sb("hbD%d" % i, [128, 2, 1024], BF16, pd) for i in range(2)],
        }
        x1s = [sb("x1s%d" % i, [128, 2, D], F32, pd) for i in range(2)]
        h2T = [sb("h2T%d" % i, [128, 8, 256], BF16, pd) for i in range(2)]
        aT = sb("aT", [128, 32, 256], BF16, pd)
        rl = [sb("rl%d" % i, [128, 256], F32, pd) for i in range(2)]
        ot = x1s
        tpD = ps("tpD", [128, 3, 1024], BF16, pd)
        pF = [ps("@pF%d" % i, [128, 512], F32, pd) for i in range(3)]
        pG = [ps("@pG%d" % i, [128, 512], F32, pd) for i in range(2)]
        k = 0
        k2 = 0
        def d_prep(t2):
            b = t2 % 2
            rs_ = slice(t2 * 256, (t2 + 1) * 256)
            P.dma(x1s[b][:], x1_d[rs_, :].rearrange("(j p) d -> p j d", p=128), reads=["x1_d%d" % (t2 // 2)],
                  writes=["x1s%d" % b, "ot%d_0" % b, "ot%d_1" % b])
            norm_transpose(S2, x1s[b], "x1s%d" % b, 2,
                           lambda kc, b=b: (h2T[b][:, kc, :], "h2T%d_%d" % (b, kc)), V_N2, tpD, "D", t2)

        d_prep(0)
        for t2 in range(T // 256):
            b = t2 % 2
            rs_ = slice(t2 * 256, (t2 + 1) * 256)
            for f in range(32):
                i = k % 3
                k += 1
                for kc in range(8):
                    P.pe(lambda e, kc=kc, f=f, i=i, b=b: e.matmul(
                        pF[i][:, 0:256], lhsT=w1[:, kc, f * 128:(f + 1) * 128], rhs=h2T[b][:, kc, :],
                        start=(kc == 0), stop=(kc == 7)),
                        reads=["w1_%d" % kc, "h2T%d_%d" % (b, kc)], writes=["@pF%d" % i])
                r_ = k % 2
                P.act(lambda e, i=i, r_=r_: e.activation(out=rl[r_][:], in_=pF[i][:, 0:256], func=AF.Relu),
                      reads=["@pF%d" % i], writes=["rl%d" % r_])
                P.pool(lambda e, f=f, r_=r_: e.tensor_tensor(out=aT[:, f, :], in0=rl[r_][:], in1=rl[r_][:], op=ALU.mult),
                       reads=["rl%d" % r_], writes=["aT%d" % f])
            if t2 + 1 < T // 256:
                d_prep(t2 + 1)
            for sub in range(2):
                for half in range(2):
                    i = k2 % 2
                    k2 += 1
                    for f in range(32):
                        P.pe(lambda e, f=f, sub=sub, half=half, i=i: e.matmul(
                            pG[i][:], lhsT=aT[:, f, sub * 128:(sub + 1) * 128],
                            rhs=w2[:, f, half * 512:(half + 1) * 512], start=(f == 0), stop=(f == 31)),
                            reads=["aT%d" % f, "w2_%d" % f], writes=["@pG%d" % i])
                    P.dve(lambda e, sub=sub, half=half, i=i, b=b: e.tensor_tensor(
                        out=ot[b][:, sub, half * 512:(half + 1) * 512], in0=pG[i][:],
                        in1=x1s[b][:, sub, half * 512:(half + 1) * 512], op=ALU.add),
                        reads=["@pG%d" % i, "x1s%d" % b], writes=["ot%d_%d" % (b, sub)])
            P.dma(out_d[rs_, :].rearrange("(j p) d -> p j d", p=128), ot[b][:],
                  reads=["ot%d_0" % b, "ot%d_1" % b], writes=["out%d" % t2])

    P.emit()
    es.close()
    return nc, P


_CACHE = {}


def kernel(**inputs):
    x = np.asarray(inputs["x"], np.float32)
    B, T, _ = x.shape
    shared = prep_shared(inputs)
    if T not in _CACHE:
        _CACHE[T] = build(T)
    nc, _ = _CACHE[T]
    in_maps = []
    for b in range(B):
        m = dict(shared)
        m["x"] = np.ascontiguousarray(x[b])
        in_maps.append(m)
    res = run_bass_kernel_spmd(nc, in_maps, core_ids=list(range(B)))
    return np.stack([np.asarray(r["out"], np.float32) for r in res.results], axis=0)
```

```python
import types
import numpy as np
from contextlib import ExitStack
import concourse.bass as bass
import concourse.mybir as mybir
from concourse.bass_utils import run_bass_kernel_spmd

F32 = mybir.dt.float32
BF16 = mybir.dt.bfloat16
AF = mybir.ActivationFunctionType
ALU = mybir.AluOpType
AX = mybir.AxisListType

ENGS = ("tensor", "vector", "scalar", "gpsimd", "sync")
DMA_POOL = 20
D = 1024
NIT = 16
TOPK = 256
NEG = -1024.0
EPS = 1e-6


class Op:
    __slots__ = ("eng", "fn", "reads", "writes", "dma", "deps", "signal", "ev", "idx")

    def __init__(self, eng, fn, reads, writes, dma):
        self.eng = eng
        self.fn = fn
        self.reads = reads
        self.writes = writes
        self.dma = dma
        self.deps = []
        self.signal = False
        self.ev = None


def _freeze(fn):
    if fn.__closure__ is None:
        return fn
    cells = []
    for c in fn.__closure__:
        try:
            cells.append(types.CellType(c.cell_contents))
        except ValueError:
            cells.append(c)
    return types.FunctionType(fn.__code__, fn.__globals__, fn.__name__, fn.__defaults__, tuple(cells))


class Prog:
    def __init__(self, nc, stack):
        self.nc = nc
        self.stack = stack
        self.ops = []

    def op(self, eng, fn, reads=(), writes=(), dma=False):
        reads = tuple(reads)
        writes = tuple(writes) + tuple(r for r in reads if r.startswith("@"))
        reads = tuple(r for r in reads if not r.startswith("@"))
        o = Op(eng, _freeze(fn), reads + ("PHASE",), writes, dma)
        o.idx = len(self.ops)
        self.ops.append(o)
        return o

    def pe(self, fn, reads=(), writes=()):
        return self.op("tensor", fn, reads, writes)

    def dve(self, fn, reads=(), writes=()):
        return self.op("vector", fn, reads, writes)

    def act(self, fn, reads=(), writes=()):
        return self.op("scalar", fn, reads, writes)

    def pool(self, fn, reads=(), writes=()):
        return self.op("gpsimd", fn, reads, writes)

    def dma(self, out, in_, reads=(), writes=(), eng="sync"):
        return self.op(eng, lambda e: e.dma_start(out=out, in_=in_), reads, writes, dma=True)

    def barrier(self, scratch):
        o = Op("vector", lambda e: e.memset(scratch, 0.0), (), ("PHASE",), False)
        o.idx = len(self.ops)
        self.ops.append(o)

    def emit(self):
        nc = self.nc
        last_writer = {}
        readers = {}
        for o in self.ops:
            deps = set()
            for r in o.reads:
                w = last_writer.get(r)
                if w is not None:
                    deps.add(w.idx)
            for wtok in o.writes:
                w = last_writer.get(wtok)
                if w is not None:
                    deps.add(w.idx)
                for rd in readers.get(wtok, ()):
                    deps.add(rd.idx)
            deps.discard(o.idx)
            final = []
            for d in sorted(deps):
                p = self.ops[d]
                if (not p.dma) and (not o.dma) and p.eng == o.eng and o.eng == "tensor":
                    raw = any(last_writer.get(r) is p for r in o.reads)
                    if not raw:
                        continue
                final.append(p)
                p.signal = True
            o.deps = final
            for r in o.reads:
                lst = readers.setdefault(r, [])
                if not o.dma:
                    lst[:] = [x for x in lst if x.dma or x.eng != o.eng]
                lst.append(o)
            for wtok in o.writes:
                last_writer[wtok] = o
                readers[wtok] = []

        sems = {}
        counts = {}
        for e in ENGS:
            sems[e] = self.stack.enter_context(nc.semaphore("s_" + e))
            counts[e] = 0
        dpool = {}
        dcount = {}
        for q in ("sync", "gpsimd"):
            dpool[q] = [self.stack.enter_context(nc.semaphore("d_%s_%d" % (q, i))) for i in range(DMA_POOL)]
            dcount[q] = 0
        seen = {e: {} for e in ENGS}

        def wait(engname, sem, val):
            key = id(sem)
            if seen[engname].get(key, 0) >= val:
                return
            seen[engname][key] = val
            getattr(nc, engname).wait_ge(sem, val)

        for o in self.ops:
            eng = getattr(nc, o.eng)
            for p in o.deps:
                sem, val = p.ev
                wait(o.eng, sem, val)
            if o.dma:
                n = dcount[o.eng]
                dcount[o.eng] += 1
                sem = dpool[o.eng][n % DMA_POOL]
                prev = 16 * (n // DMA_POOL)
                if prev > 0:
                    wait(o.eng, sem, prev)
                inst = o.fn(eng)
                inst.then_inc(sem, 16)
                o.ev = (sem, prev + 16)
            else:
                inst = o.fn(eng)
                if o.signal:
                    counts[o.eng] += 1
                    inst.then_inc(sems[o.eng], 1)
                    o.ev = (sems[o.eng], counts[o.eng])
        for q in dpool:
            n = dcount[q]
            for i in range(min(n, DMA_POOL)):
                k = n - 1 - i
                sem = dpool[q][k % DMA_POOL]
                wait("sync", sem, 16 * (k // DMA_POOL + 1))
        self.counts = counts
        self.dcount = dcount


def chunk_list():
    ch = []
    for i in range(8):
        ch.append(("q", i))
    for i in range(2):
        ch.append(("k", i))
    for i in range(2):
        ch.append(("ks", i))
    for i in range(4):
        ch.append(("iq", i))
    for i in range(16):
        ch.append(("g", i))
    ch.append(("ik", 0))
    ch.append(("pad", 0))
    for c in range(8):
        ch.append(("rx", c))
        ch.append(("gate", c))
    return ch


V_N1, V_N2, V_CW, V_CB, V_BA, V_BX, V_LAM, V_QW, V_KW, V_IKW, V_GB = 0, 8, 16, 48, 56, 64, 72, 80, 81, 82, 83
NV = 100


def prep_shared(inp):
    w_in = np.asarray(inp["w_in"][0], np.float32)
    cols = []
    for kind, i in chunk_list():
        if kind == "q":
            cols.append(np.arange(2048 + 128 * i, 2048 + 128 * (i + 1)))
        elif kind == "k":
            cols.append(np.arange(3072 + 128 * i, 3072 + 128 * (i + 1)))
        elif kind == "ks":
            b = 3072 + 128 * i
            cols.append(np.concatenate([np.arange(b + 64, b + 128), np.arange(b, b + 64)]))
        elif kind in ("ik", "pad"):
            cols.append(np.concatenate([np.arange(4096, 4160), np.arange(4096, 4160)]))
        elif kind == "iq":
            cols.append(np.arange(3584 + 128 * i, 3584 + 128 * (i + 1)))
        elif kind == "g":
            cols.append(np.arange(4168 + 128 * i, 4168 + 128 * (i + 1)))
        elif kind == "rx":
            cols.append(np.arange(128 * i, 128 * (i + 1)))
        elif kind == "gate":
            cols.append(np.arange(1024 + 128 * i, 1024 + 128 * (i + 1)))
    cols = np.concatenate(cols)
    w_fm = np.ascontiguousarray(w_in[:, cols])
    w_tm = np.ascontiguousarray(np.concatenate([w_in[:, 3328:3584], w_in[:, 4160:4168]], axis=1))

    vec = np.zeros((128, NV), np.float32)

    def colmajor(v):
        return np.asarray(v, np.float32).reshape(8, 128).T

    vec[:, V_N1:V_N1 + 8] = colmajor(inp["norm1_w"][0])
    vec[:, V_N2:V_N2 + 8] = colmajor(inp["norm2_w"][0])
    for j in range(4):
        vec[:, V_CW + 8 * j:V_CW + 8 * j + 8] = colmajor(inp["conv_w"][0][j])
    vec[:, V_CB:V_CB + 8] = colmajor(inp["conv_b"][0])
    vec[:, V_BA:V_BA + 8] = colmajor(inp["rg_ba"][0])
    vec[:, V_BX:V_BX + 8] = colmajor(inp["rg_bx"][0])
    vec[:, V_LAM:V_LAM + 8] = colmajor(inp["rg_lambda"][0])
    vec[:, V_QW] = np.tile(np.asarray(inp["q_norm_w"][0], np.float32), 2)
    vec[:, V_KW] = np.tile(np.asarray(inp["k_norm_w"][0], np.float32), 2)
    vec[:, V_IKW] = np.tile(np.asarray(inp["idx_k_norm_w"][0], np.float32), 2)
    vec[:, V_GB:V_GB + 16] = np.asarray(inp["gate_b"][0], np.float32).reshape(16, 128).T

    def blockdiag(w):
        w = np.asarray(w, np.float32)
        o = np.zeros((128, 8, 128), np.float32)
        for c in range(8):
            o[0:64, c, 0:64] = w[2 * c]
            o[64:128, c, 64:128] = w[2 * c + 1]
        return o

    ident = np.eye(128, dtype=np.float32)
    onesblk = np.zeros((128, 128), np.float32)
    onesblk[0:64, 0:64] = 1.0
    onesblk[64:128, 64:128] = 1.0
    ptab = np.zeros((128, 2 * NIT), np.float32)
    for k in range(NIT):
        ptab[:, k] = 2.0 ** (-(k + 2))
        ptab[:, NIT + k] = 2.0 ** (-(k + 1))
    import ml_dtypes
    bf = ml_dtypes.bfloat16
    shared = {
        "w_fm": w_fm,
        "w_tm": w_tm,
        "vec": vec,
        "wa_bd": blockdiag(inp["rg_wa"][0]),
        "wx_bd": blockdiag(inp["rg_wx"][0]),
        "w_o_rnn": np.ascontiguousarray(np.asarray(inp["w_o_rnn"][0], np.float32)),
        "w_o_attn": np.ascontiguousarray(np.asarray(inp["w_o_attn"][0], np.float32)),
        "w_out": np.ascontiguousarray(np.asarray(inp["w_out"][0], np.float32)),
        "w_ff_in": np.ascontiguousarray(np.asarray(inp["w_ff_in"][0], np.float32)),
        "w_ff_out": np.ascontiguousarray(np.asarray(inp["w_ff_out"][0], np.float32)),
        "ident_bf": ident.astype(bf),
        "e4_bf": np.tile(ident, (1, 4)).astype(bf),
        "onesblk": onesblk,
        "ptab": ptab,
    }
    return shared


def build(T, dbg=(), stop=None):
    NT = T // 512
    NP = T // 128
    nc = bass.Bass("TRN2", target_bir_lowering=False)
    es = ExitStack()
    P = Prog(nc, es)

    def din(name, shape, dt=F32):
        return nc.dram_tensor(name, list(shape), dt, kind="ExternalInput").ap()

    x_d = din("x", [T, D])
    wfm_d = din("w_fm", [D, 6400])
    wtm_d = din("w_tm", [D, 264])
    vec_d = din("vec", [128, NV])
    wabd_d = din("wa_bd", [128, 8, 128])
    wxbd_d = din("wx_bd", [128, 8, 128])
    wor_d = din("w_o_rnn", [D, D])
    woa_d = din("w_o_attn", [D, D])
    wout_d = din("w_out", [D, D])
    w1_d = din("w_ff_in", [D, 4 * D])
    w2_d = din("w_ff_out", [4 * D, D])
    ident_d = din("ident_bf", [128, 128], BF16)
    e4_d = din("e4_bf", [128, 512], BF16)
    onesblk_d = din("onesblk", [128, 128])
    ptab_d = din("ptab", [128, 2 * NIT])
    out_d = nc.dram_tensor("out", [T, D], F32, kind="ExternalOutput").ap()

    qT_d = nc.dram_tensor("qT_s", [8, 128, T], BF16).ap()
    iqT_d = nc.dram_tensor("iqT_s", [4, 128, T], BF16).ap()
    gT_d = nc.dram_tensor("gT_s", [16, 128, T], BF16).ap()
    yrT_d = nc.dram_tensor("yrT_s", [8, 128, T], BF16).ap()
    yaT_d = nc.dram_tensor("yaT_s", [8, 128, T], BF16).ap()
    x1_d = nc.dram_tensor("x1_s", [T, D], F32).ap()
    dbg_out = {}

    def sb(name, shape, dt, stack=None, side=None):
        if side is not None:
            return (stack or es).enter_context(nc.sbuf_tensor("s_" + name, list(shape), dt, side=side))
        return (stack or es).enter_context(nc.sbuf_tensor("s_" + name, list(shape), dt))

    def ps(name, shape, dt, stack):
        return stack.enter_context(nc.psum_tensor("p_" + name.replace("@", ""), list(shape), dt))

    vec = sb("vec", [128, NV], F32)
    ident = sb("ident", [128, 128], BF16)
    e4 = sb("e4", [128, 512], BF16)
    onesblk = sb("onesblk", [128, 128], F32)
    ptab = sb("ptab", [128, 2 * NIT], F32)
    dvec = sb("dvec", [128, 32], F32)
    P.dma(vec[:], vec_d, writes=["vec"])
    P.dma(ident[:], ident_d, writes=["ident"])
    P.dma(e4[:], e4_d, writes=["e4"])
    P.dma(onesblk[:], onesblk_d, writes=["onesblk"])
    P.dma(ptab[:], ptab_d, writes=["ptab"])
    P.dve(lambda e: e.tensor_scalar(out=dvec[:, 0:8], in0=vec[:, V_BA:V_BA + 8], scalar1=0.5, scalar2=None,
                                    op0=ALU.mult), reads=["vec"], writes=["dv0"])
    P.dve(lambda e: e.tensor_scalar(out=dvec[:, 8:16], in0=vec[:, V_BX:V_BX + 8], scalar1=0.5, scalar2=None,
                                    op0=ALU.mult), reads=["vec"], writes=["dv1"])
    P.act(lambda e: e.activation(out=dvec[:, 24:32], in_=vec[:, V_LAM:V_LAM + 8], func=AF.Exp, scale=-1.0),
          reads=["vec"], writes=["dv3"])
    P.act(lambda e: e.activation(out=dvec[:, 24:32], in_=dvec[:, 24:32], func=AF.Ln, bias=1.0),
          reads=["dv3"], writes=["dv3"])
    P.dve(lambda e: e.tensor_scalar(out=dvec[:, 16:24], in0=dvec[:, 24:32], scalar1=-4.0, scalar2=None,
                                    op0=ALU.mult), reads=["dv3"], writes=["dv2"])
    DV = ["dv0", "dv1", "dv2"]
    bscr = sb("bscr", [128, 8], F32)

    def norm_transpose(S, src_tile, src_tok, nsub, dst_fn, wcol0, tp, tag, cnt):
        ssq = S["ssq"][cnt % 2]
        junk = S["junk"]
        hb = S["hb"][cnt % 2]
        tk = "%s%d" % (tag, cnt % 2)
        for j in range(nsub):
            P.act(lambda e, j=j: e.activation(out=junk[:], in_=src_tile[:, j, :], func=AF.Square,
                                              accum_out=ssq[:, j:j + 1]),
                  reads=[src_tok], writes=["junkA", tk + "ssq%d" % j])
        P.act(lambda e: e.activation(out=ssq[:, 4:4 + nsub], in_=ssq[:, 0:nsub], func=AF.Sqrt, scale=1.0 / D,
                                     bias=EPS),
              reads=[tk + "ssq%d" % j for j in range(nsub)], writes=[tk + "sd"])
        P.dve(lambda e: e.reciprocal(out=ssq[:, 8:8 + nsub], in_=ssq[:, 4:4 + nsub]),
              reads=[tk + "sd"], writes=[tk + "rs"])
        for j in range(nsub):
            P.dve(lambda e, j=j: e.tensor_scalar(out=hb[:, j, :], in0=src_tile[:, j, :],
                                                 scalar1=ssq[:, 8 + j:9 + j], scalar2=None, op0=ALU.mult),
                  reads=[src_tok, tk + "rs"], writes=[tk + "hb%d" % j])
        nb = tp.shape[1]
        for kc in range(8):
            bank = kc % nb
            ptok = "@tp%s%d" % (tag, bank)
            for j in range(nsub):
                P.pe(lambda e, j=j, kc=kc, bank=bank: e.transpose(
                    out=tp[:, bank, j * 128:(j + 1) * 128],
                    in_=hb[:, j, kc * 128:(kc + 1) * 128], identity=ident[:]),
                    reads=[tk + "hb%d" % j, "ident"], writes=[ptok])
            dst, dtok = dst_fn(kc)
            src = tp[:, bank, 0:nsub * 128]
            if kc % 2 == 0:
                P.act(lambda e, dst=dst, src=src, kc=kc: e.activation(
                    out=dst, in_=src, func=AF.Copy, scale=vec[:, wcol0 + kc: wcol0 + kc + 1]),
                    reads=[ptok, "vec"], writes=[dtok])
            else:
                P.dve(lambda e, dst=dst, src=src, kc=kc: e.tensor_scalar(
                    out=dst, in0=src, scalar1=vec[:, wcol0 + kc: wcol0 + kc + 1], scalar2=None, op0=ALU.mult),
                    reads=[ptok, "vec"], writes=[dtok])

    att = ExitStack()
    attR = ExitStack()
    KT = sb("KT", [128, 2, T], BF16, attR, side="right")
    KTs = sb("KTs", [128, 2, T], BF16, attR, side="right")
    Vp = sb("Vp", [128, NP, 4, 65], BF16, att)
    ikT = sb("ikT", [128, T], BF16, attR, side="right")
    iw = sb("iw", [128, NP, 8], F32, att)
    absw = sb("absw", [128, NP, 8], F32, att)
    sgnw = sb("sgnw", [128, NP, 8], F32, att)

    pab = ExitStack()
    hT = sb("hT", [128, 8, T], BF16, pab)
    with ExitStack() as pa:
        S = {
            "ssq": [sb("ssq%d" % i, [128, 12], F32, pa) for i in range(2)],
            "junk": sb("junkA", [128, 1024], BF16, pa),
            "hb": [sb("hb%d" % i, [128, 4, 1024], BF16, pa) for i in range(2)],
        }
        xs = [sb("xs%d" % i, [128, 4, 1024], F32, pa) for i in range(2)]
        tp = ps("tpA", [128, 4, 1024], BF16, pa)
        for g4 in range(NT):
            xt = xs[g4 % 2]
            xtok = "xs%d" % (g4 % 2)
            P.dma(xt[:], x_d[g4 * 512:(g4 + 1) * 512, :].rearrange("(j p) d -> p j d", p=128), writes=[xtok])
            norm_transpose(S, xt, xtok, 4,
                           lambda kc, g4=g4: (hT[:, kc, g4 * 512:(g4 + 1) * 512], "hT%d_%d" % (kc, g4)),
                           V_N1, tp, "A", g4)
    P.barrier(bscr[:, 0:1])
    if stop == "A":
        d_ = nc.dram_tensor("dbg_hT", [128, 8, T], BF16, kind="ExternalOutput").ap()
        P.dma(d_, hT[:], reads=["hT%d_%d" % (kc, g4) for kc in range(8) for g4 in range(NT)], writes=["dbg_hT"])
        P.emit()
        return nc, P
    HT_ALL = lambda tt: ["hT%d_%d" % (kc, tt) for kc in range(8)]

    chunks = chunk_list()
    NCH = len(chunks)
    with ExitStack() as pb:
        wbuf = [sb("wb%d" % i, [128, 8, 256], BF16, pb) for i in range(2)]
        wtm = sb("wtm", [128, 8, 264], BF16, pb)
        wabd = sb("wabd", [128, 8, 128], BF16, pb)
        wxbd = sb("wxbd", [128, 8, 128], BF16, pb)
        P.dma(wtm[:], wtm_d.rearrange("(kc p) c -> p kc c", p=128), writes=["wtm"], eng="gpsimd")
        P.dma(wabd[:], wabd_d, writes=["wabd"], eng="gpsimd")
        P.dma(wxbd[:], wxbd_d, writes=["wxbd"], eng="gpsimd")
        pacc = [ps("@pacc%d" % i, [128, 512], F32, pb) for i in range(3)]
        state = {"acc": 0, "hn": 0, "ba": 0, "g": 0}

        def load_w(grp):
            c0 = grp * 256
            wb = wbuf[grp % 2]
            P.dma(wb[:], wfm_d[:, c0:c0 + 256].rearrange("(kc p) c -> p kc c", p=128),
                  writes=["wb%d" % (grp % 2)], eng="gpsimd")

        def project(ci, tt):
            grp, loc = ci // 2, ci % 2
            wb = wbuf[grp % 2]
            a = state["acc"] % 3
            state["acc"] += 1
            pt = pacc[a]
            for kc in range(8):
                P.pe(lambda e: e.matmul(
                    pt[:], lhsT=wb[:, kc, loc * 128:(loc + 1) * 128], rhs=hT[:, kc, tt * 512:(tt + 1) * 512],
                    start=(kc == 0), stop=(kc == 7)),
                    reads=["wb%d" % (grp % 2), "hT%d_%d" % (kc, tt)], writes=["@pacc%d" % a])
            return pt, "@pacc%d" % a

        with ExitStack() as pb1:
            pst = [ps("@pst%d" % i, [128, 512], F32, pb1) for i in range(2)]
            hn0 = {k: sb("hn_%s" % k, [128, 512], F32, pb1) for k in ("sq", "qs", "sd")}
            hst = [sb("hst%d" % i, [128, 512], BF16, pb1) for i in range(2)]
            gst = [sb("gst%d" % i, [128, 512], BF16, pb1) for i in range(2)]

            def headnorm(pt, ptok, wcol, dst, dtok):
                i = state["hn"] % 2
                state["hn"] += 1
                B = hn0
                t = "hn0"
                P.act(lambda e: e.activation(out=B["sq"][:], in_=pt[:], func=AF.Square), reads=[ptok], writes=[t + "sq"])
                P.dve(lambda e: e.tensor_copy(out=B["qs"][:], in_=pt[:]), reads=[ptok], writes=[t + "qs"])
                P.pe(lambda e: e.matmul(pst[i][:], lhsT=onesblk[:], rhs=B["sq"][:], start=True, stop=True),
                     reads=[t + "sq", "onesblk"], writes=["@pst%d" % i])
                P.act(lambda e: e.activation(out=B["sd"][:], in_=pst[i][:], func=AF.Ln, scale=1.0 / 64, bias=EPS),
                      reads=["@pst%d" % i], writes=[t + "sd"])
                P.act(lambda e: e.activation(out=B["sd"][:], in_=B["sd"][:], func=AF.Exp, scale=-0.5),
                      reads=[t + "sd"], writes=[t + "sd"])
                P.dve(lambda e: e.scalar_tensor_tensor(out=dst, in0=B["qs"][:], scalar=vec[:, wcol:wcol + 1],
                                                       in1=B["sd"][:], op0=ALU.mult, op1=ALU.mult),
                      reads=[t + "qs", t + "sd", "vec"], writes=[dtok])

            for ci, (kind, idx) in enumerate(chunks):
                if kind in ("rx", "gate"):
                    break
                if ci % 2 == 0:
                    load_w(ci // 2)
                if kind == "pad":
                    continue
                for tt in range(NT):
                    cs = slice(tt * 512, (tt + 1) * 512)
                    pt, ptok = project(ci, tt)
                    if kind == "q":
                        i = state["hn"] % 2
                        headnorm(pt, ptok, V_QW, hst[i][:], "hst%d" % i)
                        P.dma(qT_d[idx, :, cs], hst[i][:], reads=["hst%d" % i], writes=["qT_d%d" % tt])
                    elif kind == "k":
                        headnorm(pt, ptok, V_KW, KT[:, idx, cs], "KT%d" % tt)
                    elif kind == "ks":
                        headnorm(pt, ptok, V_KW, KTs[:, idx, cs], "KTs%d" % tt)
                    elif kind == "ik":
                        headnorm(pt, ptok, V_IKW, ikT[:, cs], "ikT%d" % tt)
                    elif kind == "iq":
                        i = state["g"] % 2
                        state["g"] += 1
                        P.act(lambda e: e.activation(out=gst[i][:], in_=pt[:], func=AF.Copy),
                              reads=[ptok], writes=["gst%d" % i])
                        P.dma(iqT_d[idx, :, cs], gst[i][:], reads=["gst%d" % i], writes=["iqT_d%d" % tt])
                    elif kind == "g":
                        i = state["g"] % 2
                        state["g"] += 1
                        P.act(lambda e: e.activation(
                            out=gst[i][:], in_=pt[:], func=AF.Sigmoid, bias=vec[:, V_GB + idx:V_GB + idx + 1]),
                            reads=[ptok, "vec"], writes=["gst%d" % i])
                        P.dma(gT_d[idx, :, cs], gst[i][:], reads=["gst%d" % i], writes=["gT_d%d" % tt])
                if kind == "iq" and idx == 3:
                    for tk in range(NP):
                        a = state["acc"] % 3
                        state["acc"] += 1
                        pt = pacc[a]
                        for kc in range(8):
                            P.pe(lambda e: e.matmul(
                                pt[:, 0:264], lhsT=hT[:, kc, tk * 128:(tk + 1) * 128], rhs=wtm[:, kc, :],
                                start=(kc == 0), stop=(kc == 7)),
                                reads=["wtm", "hT%d_%d" % (kc, tk // 4)], writes=["@pacc%d" % a])
                        P.act(lambda e: e.activation(
                            out=Vp[:, tk, :, 0:64], in_=pt[:, 0:256].rearrange("p (g d) -> p g d", d=64), func=AF.Copy),
                            reads=["@pacc%d" % a], writes=["Vp%d" % tk])
                        P.dve(lambda e: e.tensor_copy(out=iw[:, tk, :], in_=pt[:, 256:264]),
                              reads=["@pacc%d" % a], writes=["iw%d" % tk])
                    P.dve(lambda e: e.memset(Vp[:, :, :, 64:65], 1.0), writes=["Vp1"])
                    P.act(lambda e: e.activation(out=absw[:], in_=iw[:], func=AF.Abs),
                          reads=["iw%d" % tk for tk in range(NP)], writes=["absw"])
                    P.act(lambda e: e.activation(out=sgnw[:], in_=iw[:], func=AF.Sign),
                          reads=["iw%d" % tk for tk in range(NP)], writes=["sgnw"])
        P.barrier(bscr[:, 4:5])

        with ExitStack() as pb2:
            pbd = [ps("@pbd%d" % i, [128, 512], F32, pb2) for i in range(4)]
            BA = []
            for i in range(4):
                d_ = {k: sb("ba_%s%d" % (k, i), [128, 512], F32, pb2) for k in ("xc", "tha", "thi", "aa", "hs", "gl")}
                d_["xr"] = sb("ba_xr%d" % i, [128, 515], F32, pb2)
                d_["xcb"] = sb("ba_xcb%d" % i, [128, 512], BF16, pb2)
                d_["yb"] = sb("ba_yb%d" % i, [128, 512], BF16, pb2)
                BA.append(d_)
            CI0 = [ci for ci, (kind, idx) in enumerate(chunks) if kind == "rx"][0]
            work = [(c, pr) for c in range(8) for pr in range(NT // 2)]
            prev = {"set": None, "hs": None}

            def front(wi):
                c, pr = work[wi]
                ci_rx = CI0 + 2 * c
                if pr == 0:
                    load_w(ci_rx // 2)
                for X in range(2):
                    tt = 2 * pr + X
                    si = 2 * (wi % 2) + X
                    B = BA[si]
                    t = "ba%d" % si
                    pt, ptok = project(ci_rx, tt)
                    P.act(lambda e: e.activation(out=B["xr"][:, 3:515], in_=pt[:], func=AF.Copy),
                          reads=[ptok], writes=[t + "xr"])
                    if tt == 0:
                        P.pool(lambda e: e.memset(B["xr"][:, 0:3], 0.0), writes=[t + "xh"])
                    else:
                        pB = BA[prev["set"]]
                        P.pool(lambda e: e.tensor_copy(out=B["xr"][:, 0:3], in_=pB["xr"][:, 512:515]),
                               reads=["ba%dxr" % prev["set"]], writes=[t + "xh"])
                    prev["set"] = si
                    rd = [t + "xr", t + "xh", "vec"]
                    P.dve(lambda e: e.tensor_scalar(
                        out=B["xc"][:], in0=B["xr"][:, 0:512], scalar1=vec[:, V_CW + c:V_CW + c + 1],
                        scalar2=vec[:, V_CB + c:V_CB + c + 1], op0=ALU.mult, op1=ALU.add),
                        reads=rd, writes=[t + "xc"])
                    for j in range(1, 4):
                        P.dve(lambda e: e.scalar_tensor_tensor(
                            out=B["xc"][:], in0=B["xr"][:, j:j + 512],
                            scalar=vec[:, V_CW + 8 * j + c:V_CW + 8 * j + c + 1], in1=B["xc"][:],
                            op0=ALU.mult, op1=ALU.add),
                            reads=rd + [t + "xc"], writes=[t + "xc"])
                    P.pool(lambda e: e.tensor_copy(out=B["xcb"][:], in_=B["xc"][:]), reads=[t + "xc"],
                           writes=[t + "xcb"])
                    ptg, ptokg = project(ci_rx + 1, tt)
                    P.act(lambda e: e.activation(out=B["gl"][:], in_=ptg[:], func=AF.Gelu_apprx_tanh),
                          reads=[ptokg], writes=[t + "gl"])

            def back(wi):
                c, pr = work[wi]
                sets = [2 * (wi % 2), 2 * (wi % 2) + 1]
                for X in range(2):
                    B = BA[sets[X]]
                    t = "ba%d" % sets[X]
                    P.pe(lambda e: e.matmul(pbd[2 * X][:], lhsT=wabd[:, c, :], rhs=B["xcb"][:], start=True, stop=True),
                         reads=[t + "xcb", "wabd"], writes=["@pbd%d" % (2 * X)])
                    P.pe(lambda e: e.matmul(pbd[2 * X + 1][:], lhsT=wxbd[:, c, :], rhs=B["xcb"][:], start=True, stop=True),
                         reads=[t + "xcb", "wxbd"], writes=["@pbd%d" % (2 * X + 1)])
                for X in range(2):
                    B = BA[sets[X]]
                    t = "ba%d" % sets[X]
                    P.act(lambda e: e.activation(out=B["tha"][:], in_=pbd[2 * X][:], func=AF.Tanh, scale=0.5,
                                                 bias=dvec[:, c:c + 1]),
                          reads=["@pbd%d" % (2 * X)] + DV, writes=[t + "tha"])
                    P.act(lambda e: e.activation(out=B["thi"][:], in_=pbd[2 * X + 1][:], func=AF.Tanh, scale=0.5,
                                                 bias=dvec[:, 8 + c:9 + c]),
                          reads=["@pbd%d" % (2 * X + 1)] + DV, writes=[t + "thi"])
                for X in range(2):
                    B = BA[sets[X]]
                    t = "ba%d" % sets[X]
                    P.act(lambda e: e.activation(out=B["aa"][:], in_=B["tha"][:], func=AF.Exp,
                                                 scale=dvec[:, 16 + c:17 + c], bias=dvec[:, 16 + c:17 + c]),
                          reads=[t + "tha"] + DV, writes=[t + "aa"])
                for X in range(2):
                    B = BA[sets[X]]
                    t = "ba%d" % sets[X]
                    P.act(lambda e: e.activation(out=B["tha"][:], in_=B["aa"][:], func=AF.Square),
                          reads=[t + "aa"], writes=[t + "tha"])
                for X in range(2):
                    B = BA[sets[X]]
                    t = "ba%d" % sets[X]
                    P.act(lambda e: e.activation(out=B["tha"][:], in_=B["tha"][:], func=AF.Sqrt, scale=-0.25, bias=0.25),
                          reads=[t + "tha"], writes=[t + "tha"])
                for X in range(2):
                    tt = 2 * pr + X
                    B = BA[sets[X]]
                    t = "ba%d" % sets[X]
                    P.dve(lambda e: e.scalar_tensor_tensor(out=B["thi"][:], in0=B["thi"][:], scalar=1.0, in1=B["xc"][:],
                                                           op0=ALU.add, op1=ALU.mult),
                          reads=[t + "thi", t + "xc"], writes=[t + "thi"])
                    P.dve(lambda e: e.tensor_tensor(out=B["thi"][:], in0=B["thi"][:], in1=B["tha"][:], op=ALU.mult),
                          reads=[t + "thi", t + "tha"], writes=[t + "thi"])
                    if tt == 0:
                        init, rdi = 0.0, []
                    else:
                        init, rdi = prev["hs"]
                    P.dve(lambda e: e.tensor_tensor_scan(out=B["hs"][:], data0=B["aa"][:], data1=B["thi"][:],
                                                         initial=init, op0=ALU.mult, op1=ALU.add),
                          reads=[t + "aa", t + "thi"] + rdi, writes=[t + "hs"])
                    prev["hs"] = (B["hs"][:, 511:512], [t + "hs"])
                    P.dve(lambda e: e.tensor_tensor(out=B["yb"][:], in0=B["hs"][:], in1=B["gl"][:], op=ALU.mult),
                          reads=[t + "hs", t + "gl"], writes=[t + "yb"])
                    P.dma(yrT_d[c, :, tt * 512:(tt + 1) * 512], B["yb"][:], reads=[t + "yb"], writes=["yrT_d%d" % tt])

            front(0)
            for wi in range(len(work)):
                if wi + 1 < len(work):
                    front(wi + 1)
                back(wi)
    pab.close()
    P.barrier(bscr[:, 1:2])

    if "B" in dbg:
        for name, t_, shp, dt in (("KT", KT, [128, 2, T], BF16), ("KTs", KTs, [128, 2, T], BF16),
                                  ("Vp", Vp, [128, NP, 4, 65], BF16), ("ikT", ikT, [128, T], BF16),
                                  ("iw", iw, [128, NP, 8], F32)):
            d_ = nc.dram_tensor("dbg_" + name, shp, dt, kind="ExternalOutput").ap()
            rd = {"KT": ["KT%d" % i for i in range(NT)], "KTs": ["KTs%d" % i for i in range(NT)],
                  "Vp": ["Vp%d" % i for i in range(NP)] + ["Vp1"], "ikT": ["ikT%d" % i for i in range(NT)],
                  "iw": ["iw%d" % i for i in range(NP)]}[name]
            P.dma(d_, t_[:], reads=rd, writes=["dbg_" + name])
        for name, src, n in (("qT", qT_d, 8), ("iqT", iqT_d, 4), ("gT", gT_d, 16), ("yrT", yrT_d, 8)):
            d_ = nc.dram_tensor("dbg_" + name, [n, 128, T], BF16, kind="ExternalOutput").ap()
            P.dma(d_, src, reads=["%s_d%d" % (name, i) for i in range(NT)], writes=["dbg_" + name])

    if stop == "B":
        P.emit()
        return nc, P
    QT_RD = ["qT_d%d" % i for i in range(NT)]
    IQ_RD = ["iqT_d%d" % i for i in range(NT)]
    LA = 2
    with ExitStack() as pt_:
        KZ = sb("KZ", [128, 8, T], BF16, pt_)
        ikZ = sb("ikZ", [128, 2, T], BF16, pt_)
        KALL = ["KT%d" % i for i in range(NT)] + ["KTs%d" % i for i in range(NT)]
        ci_ = 0
        for g in range(4):
            for hp in range(2):
                src_t = KT if (g % 2) == hp else KTs
                src = src_t[hp * 64:(hp + 1) * 64, g // 2, :]
                dst = KZ[hp * 64:(hp + 1) * 64, g * 2 + hp, :]
                zdst = KZ[(1 - hp) * 64:(2 - hp) * 64, g * 2 + hp, :]
                if ci_ % 2 == 0:
                    P.pool(lambda e: e.memset(zdst, 0.0), writes=["KZz_%d" % ci_])
                    P.dve(lambda e: e.tensor_copy(out=dst, in_=src), reads=KALL, writes=["KZ_%d" % ci_])
                else:
                    P.dve(lambda e: e.memset(zdst, 0.0), writes=["KZz_%d" % ci_])
                    P.act(lambda e: e.activation(out=dst, in_=src, func=AF.Copy), reads=KALL, writes=["KZ_%d" % ci_])
                ci_ += 1
        for hp in range(2):
            P.pool(lambda e: e.memset(ikZ[(1 - hp) * 64:(2 - hp) * 64, hp, :], 0.0), writes=["ikZz_%d" % hp])
            P.dve(lambda e: e.tensor_copy(out=ikZ[hp * 64:(hp + 1) * 64, hp, :], in_=ikT[hp * 64:(hp + 1) * 64, :]),
                  reads=["ikT%d" % i for i in range(NT)], writes=["ikZ_%d" % hp])
        KZ_RD = ["KZ_%d" % i for i in range(8)] + ["KZz_%d" % i for i in range(8)]
        IKZ_RD = ["ikZ_0", "ikZ_1", "ikZz_0", "ikZz_1"]
        P.barrier(bscr[:, 5:6])
        attR.close()
        NSC = 2
        sc = [sb("sc%d" % i, [128, T], F32, pt_) for i in range(NSC)]
        negm = [sb("negm%d" % i, [128, T], BF16, pt_) for i in range(NSC)]
        junkc = sb("junkc", [128, T], BF16, pt_)
        qp = [sb("qp%d" % i, [128, 8, 128], BF16, pt_) for i in range(2)]
        iqp = [sb("iqp%d" % i, [128, 4, 128], BF16, pt_) for i in range(NSC)]
        NRB = 10
        Rb = [sb("Rb%d" % i, [128, 512], BF16, pt_) for i in range(NRB)]
        dsg = [sb("dsg%d" % i, [128, 8, 128], BF16, pt_) for i in range(NSC)]
        NPTB = 6
        ptb = [sb("ptb%d" % i, [128, 512], BF16, pt_) for i in range(NPTB)]
        bs = [sb("bs%d" % i, [128, 8 + 2 * NIT], F32, pt_) for i in range(NSC)]
        rec = sb("rec", [128, 16], F32, pt_)
        yst = [sb("yst%d" % i, [128, 16, 64], BF16, pt_) for i in range(2)]
        accs = [sb("accs%d" % i, [128, 3, 512], F32, pt_) for i in range(2)]
        yT = [sb("yT%d" % i, [128, 8, 128], BF16, pt_) for i in range(2)]
        thc = sb("thc", [128, 1], F32, pt_)
        P.dve(lambda e: e.memset(thc[:], -1e29), writes=["thc"])
        st = [ps("@st%d" % i, [128, 512], F32, pt_) for i in range(2)]
        psc = ps("@psc", [128, 512], F32, pt_)
        acc = ps("acc", [128, 3, 512], F32, pt_)
        pmixs = [ps("pmix%d" % i, [128, 512], F32, pt_) for i in range(2)]
        pmix = pmixs[0]
        ptr = pmix[:].bitcast(BF16)

        cnt = {"R": 0, "pt": 0, "st": 0, "px": 0}

        def indexer_units(m):
            V = 128 * (m + 1)
            sbi = m % NSC
            units = []
            pend = {"f": None}

            def load():
                P.dma(iqp[sbi][:], iqT_d[:, :, m * 128:(m + 1) * 128].rearrange("c p t -> p c t"),
                      reads=IQ_RD, writes=["iqp%d" % sbi])
                for h in range(8):
                    P.pool(lambda e: e.tensor_scalar(out=dsg[sbi][:, h, :], in0=ident[:], scalar1=iw[:, m, h:h + 1],
                                                     scalar2=None, op0=ALU.mult),
                           reads=["ident", "iw%d" % m], writes=["dsg%d_%d" % (sbi, h)])
            units.append(load)
            nkb = (V + 511) // 512
            for kb in range(nkb):
                ncol = min(512, V - 512 * kb)
                for h in range(8):
                    def unit(kb=kb, ncol=ncol, h=h):
                        hp = h % 2
                        r = cnt["R"] % NRB
                        cnt["R"] += 1
                        px = cnt["px"] % 2
                        cnt["px"] += 1
                        pm = pmixs[px]
                        pmt = "@pmix%d" % px
                        P.pe(lambda e: e.matmul(pm[:, 0:ncol], lhsT=iqp[sbi][:, h // 2, :],
                                                rhs=ikZ[:, hp, kb * 512:kb * 512 + ncol],
                                                start=True, stop=True),
                             reads=["iqp%d" % sbi] + IKZ_RD, writes=[pmt])
                        P.act(lambda e: e.activation(out=Rb[r][:, 0:ncol], in_=pm[:, 0:ncol], func=AF.Relu),
                              reads=[pmt], writes=["Rb%d" % r])
                        if pend["f"] is not None:
                            pend["f"]()

                        def dsum():
                            P.pe(lambda e: e.matmul(psc[:, 0:ncol], lhsT=dsg[sbi][:, h, :], rhs=Rb[r][:, 0:ncol],
                                                    start=(h == 0), stop=(h == 7)),
                                 reads=["Rb%d" % r, "dsg%d_%d" % (sbi, h)], writes=["@psc"])
                            if h == 7:
                                P.act(lambda e: e.activation(out=sc[sbi][:, kb * 512:kb * 512 + ncol],
                                                             in_=psc[:, 0:ncol], func=AF.Copy),
                                      reads=["@psc"], writes=["sc%d_%d" % (sbi, kb)])
                        pend["f"] = dsum
                    units.append(unit)

            def flush():
                if pend["f"] is not None:
                    pend["f"]()
                    pend["f"] = None
            units.append(flush)
            return units

        def threshold_steps(m):
            V = 128 * (m + 1)
            sbi = m % NSC
            nkb = (V + 511) // 512
            SCT = ["sc%d_%d" % (sbi, kb) for kb in range(nkb)]
            b = bs[sbi]
            bt = "bs%d" % sbi
            s = sc[sbi]
            SCT2 = SCT + [bt + "ms"]
            steps = []

            def init():
                if m >= 2:
                    P.dve(lambda e: e.tensor_reduce(out=b[:, 0:1], in_=s[:, 0:V], axis=AX.X, op=ALU.max),
                          reads=SCT, writes=[bt + "mx"])
                    P.dve(lambda e: e.tensor_reduce(out=b[:, 1:2], in_=s[:, 0:V], axis=AX.X, op=ALU.min),
                          reads=SCT, writes=[bt + "mn"])
                P.dve(lambda e: e.memset(s[0:64, V - 64:V], -1e30), reads=SCT + [bt + "mx", bt + "mn"],
                      writes=[bt + "ms"])
                if m >= 2:
                    P.dve(lambda e: e.tensor_scalar(out=b[:, 2:3], in0=b[:, 0:1], scalar1=b[:, 1:2], scalar2=0.5,
                                                    op0=ALU.add, op1=ALU.mult),
                          reads=[bt + "mx", bt + "mn"], writes=[bt + "th"])
                    P.dve(lambda e: e.tensor_tensor(out=b[:, 3:4], in0=b[:, 0:1], in1=b[:, 1:2], op=ALU.subtract),
                          reads=[bt + "mx", bt + "mn"], writes=[bt + "rg"])
                    P.dve(lambda e: e.tensor_scalar(out=b[:, 8:8 + 2 * NIT], in0=ptab[:], scalar1=b[:, 3:4],
                                                    scalar2=None, op0=ALU.mult),
                          reads=[bt + "rg", "ptab"], writes=[bt + "tab"])
            steps.append(init)
            if m >= 2:
                for k in range(NIT):
                    def it(k=k):
                        P.dve(lambda e: e.tensor_scalar(out=junkc[:, 0:V], in0=s[:, 0:V], scalar1=b[:, 2:3],
                                                        scalar2=None, op0=ALU.is_ge, op1=ALU.add,
                                                        accum_out=b[:, 4:5]),
                              reads=SCT2 + [bt + "th"], writes=["junkc", bt + "cnt"])
                        P.dve(lambda e: e.tensor_scalar(out=b[:, 5:6], in0=b[:, 4:5], scalar1=TOPK - 0.5,
                                                        scalar2=b[:, 8 + NIT + k:9 + NIT + k],
                                                        op0=ALU.is_ge, op1=ALU.mult),
                              reads=[bt + "cnt", bt + "tab"], writes=[bt + "d"])
                        P.dve(lambda e: e.scalar_tensor_tensor(out=b[:, 2:3], in0=b[:, 2:3],
                                                               scalar=b[:, 8 + k:9 + k], in1=b[:, 5:6],
                                                               op0=ALU.subtract, op1=ALU.add),
                              reads=[bt + "th", bt + "d", bt + "tab"], writes=[bt + "th"])
                    steps.append(it)
                thap, thtok = b[:, 2:3], bt + "th"
            else:
                thap, thtok = thc[:], "thc"

            def fin():
                P.dve(lambda e: e.tensor_scalar(out=negm[sbi][:, 0:V], in0=s[:, 0:V], scalar1=thap, scalar2=NEG,
                                                op0=ALU.is_lt, op1=ALU.mult),
                      reads=SCT2 + [thtok], writes=["negm%d" % sbi])
            steps.append(fin)
            return steps

        def merge(a, b_):
            out_ = []
            ia = ib = 0
            na, nb_ = len(a), len(b_)
            while ia < na or ib < nb_:
                if ib >= nb_ or (ia < na and ia * nb_ <= ib * na):
                    out_.append(a[ia])
                    ia += 1
                else:
                    out_.append(b_[ib])
                    ib += 1
            return out_

        def attention(m, inter, deferred=None):
            sbi = m % NSC
            qb = m % 2
            if m == 0:
                P.dma(qp[0][:], qT_d[:, :, 0:128].rearrange("c p t -> p c t"), reads=QT_RD, writes=["qp0"])
            if m + 1 < NP:
                P.dma(qp[1 - qb][:], qT_d[:, :, (m + 1) * 128:(m + 2) * 128].rearrange("c p t -> p c t"),
                      reads=QT_RD, writes=["qp%d" % (1 - qb)])
            started = set()
            units = [(j, b2, hp) for j in range(m + 1) for b2 in range(2) for hp in range(2)]
            nun = len(units)
            ii = 0
            pending = None

            def emit_pv(u, pb_):
                j, b2, hp = u
                for gl in range(2):
                    for rr in range(2):
                        g = 2 * b2 + gl
                        head = 4 * g + hp + 2 * rr
                        bank = head // 7
                        off = (head % 7) * 65
                        first = (j == 0) and (bank not in started)
                        started.add(bank)
                        P.pe(lambda e: e.matmul(acc[:, bank, off:off + 65],
                                                lhsT=ptb[pb_][:, gl * 256 + rr * 128: gl * 256 + (rr + 1) * 128],
                                                rhs=Vp[:, j, g, :], start=first, stop=(j == m), skip_group_check=True),
                             reads=["ptb%d" % pb_, "Vp%d" % j, "Vp1"], writes=["@acc"])

            for ui, (j, b2, hp) in enumerate(units):
                k = cnt["st"] % 2
                cnt["st"] += 1
                stt = "@st%d" % k
                for gl in range(2):
                    g = 2 * b2 + gl
                    P.pe(lambda e: e.matmul(
                        st[k][:, gl * 256:(gl + 1) * 256],
                        lhsT=KZ[:, g * 2 + hp, j * 128:(j + 1) * 128],
                        rhs=qp[qb][:, 2 * g:2 * g + 2, :],
                        start=(gl == 0), stop=False, skip_group_check=True),
                        reads=["qp%d" % qb] + KZ_RD, writes=[stt])
                P.pe(lambda e: e.matmul(
                    st[k][:], lhsT=negm[sbi][:, j * 128:(j + 1) * 128], rhs=e4[:],
                    start=False, stop=True, skip_group_check=True),
                    reads=["negm%d" % sbi, "e4"], writes=[stt])
                pb_ = cnt["pt"] % NPTB
                cnt["pt"] += 1
                P.act(lambda e: e.activation(out=ptb[pb_][:], in_=st[k][:], func=AF.Exp, scale=0.125),
                      reads=[stt], writes=["ptb%d" % pb_])
                if pending is not None:
                    emit_pv(*pending)
                pending = ((j, b2, hp), pb_)
                if deferred is not None and ui == min(6, nun - 1):
                    deferred()
                    deferred = None
                tgt = (len(inter) * (ui + 1) + nun - 1) // nun
                while ii < tgt and ii < len(inter):
                    inter[ii]()
                    ii += 1
            emit_pv(*pending)
            while ii < len(inter):
                inter[ii]()
                ii += 1
            ab = m % 2
            for bank in range(3):
                ncw = 65 * (7 if bank < 2 else 2)
                P.act(lambda e: e.activation(out=accs[ab][:, bank, 0:ncw], in_=acc[:, bank, 0:ncw], func=AF.Copy),
                      reads=["@acc"], writes=["accs%d_%d" % (ab, bank)])
            for bank in range(3):
                nh = 7 if bank < 2 else 2
                v3 = accs[ab][:, bank, 0:nh * 65].rearrange("p (h d) -> p h d", d=65)
                P.dve(lambda e: e.reciprocal(out=rec[:, bank * 7:bank * 7 + nh], in_=v3[:, :, 64]),
                      reads=["accs%d_%d" % (ab, bank)], writes=["rec%d" % bank])
                P.dve(lambda e: e.tensor_tensor(
                    out=yst[ab][:, bank * 7:bank * 7 + nh, :], in0=v3[:, :, 0:64],
                    in1=rec[:, bank * 7:bank * 7 + nh].unsqueeze(2).to_broadcast([128, nh, 64]), op=ALU.mult),
                    reads=["accs%d_%d" % (ab, bank), "rec%d" % bank], writes=["yst%d_%d" % (ab, bank)])

            def finish():
                for kc in range(8):
                    P.pe(lambda e: e.transpose(out=ptr[:, kc * 128:(kc + 1) * 128],
                                               in_=yst[ab][:, 2 * kc:2 * kc + 2, :].rearrange("p h d -> p (h d)"),
                                               identity=ident[:]),
                         reads=["yst%d_%d" % (ab, b_) for b_ in range(3)] + ["ident"], writes=["@pmix0"])
                P.act(lambda e: e.activation(out=yT[ab][:], in_=ptr.rearrange("p (c t) -> p c t", t=128), func=AF.Copy),
                      reads=["@pmix0"], writes=["yT%d" % ab])
                P.dma(yaT_d[:, :, m * 128:(m + 1) * 128].rearrange("c p t -> p c t"), yT[ab][:],
                      reads=["yT%d" % ab], writes=["yaT_d%d" % (m // 4)])
            return finish

        for u in indexer_units(0):
            u()
        for u in threshold_steps(0):
            u()
        if NP > 1:
            for u in indexer_units(1):
                u()
        fin_prev = None
        for m in range(NP):
            idx_u = indexer_units(m + 2) if m + 2 < NP else []
            thr_u = threshold_steps(m + 1) if m + 1 < NP else []
            fin_prev = attention(m, merge(idx_u, thr_u), fin_prev)
        fin_prev()
    att.close()
    P.barrier(bscr[:, 2:3])

    if "ATT" in dbg:
        d_ = nc.dram_tensor("dbg_yaT", [8, 128, T], BF16, kind="ExternalOutput").ap()
        P.dma(d_, yaT_d, reads=["yaT_d%d" % i for i in range(NT)], writes=["dbg_yaT"])

    if stop == "ATT":
        P.emit()
        return nc, P
    with ExitStack() as pc:
        wor = sb("wor", [128, 8, D], BF16, pc)
        woa = sb("woa", [128, 8, D], BF16, pc)
        wout = sb("wout", [128, 8, D], BF16, pc)
        for nm, t_, d_ in (("wor", wor, wor_d), ("woa", woa, woa_d), ("wout", wout, wout_d)):
            for kc in range(8):
                P.dma(t_[:, kc, :], d_[kc * 128:(kc + 1) * 128, :], writes=["%s%d" % (nm, kc)], eng="gpsimd")
        WOR = ["wor%d" % k for k in range(8)]
        WOA = ["woa%d" % k for k in range(8)]
        WOUT = ["wout%d" % k for k in range(8)]
        yr = [sb("yr%d" % i, [128, 8, 512], BF16, pc) for i in range(2)]
        ya = [sb("ya%d" % i, [128, 8, 512], BF16, pc) for i in range(2)]
        gt = [sb("gt%d" % i, [128, 16, 512], BF16, pc) for i in range(2)]
        xt2 = [sb("xt2%d" % i, [128, 4, D], F32, pc) for i in range(2)]
        x1t = xt2
        mg = [sb("mg%d" % i, [128, 8, 512], BF16, pc) for i in range(2)]
        tmp = [sb("tmpc%d" % i, [128, 512], F32, pc) for i in range(2)]
        tmp2 = [sb("tmpd%d" % i, [128, 512], F32, pc) for i in range(2)]
        pA = [ps("@pA%d" % i, [128, 512], F32, pc) for i in range(2)]
        pB = [ps("@pB%d" % i, [128, 512], F32, pc) for i in range(2)]
        pO = [ps("@pO%d" % i, [128, 512], F32, pc) for i in range(2)]
        k = 0
        def c_loads(tt):
            b = tt % 2
            cs = slice(tt * 512, (tt + 1) * 512)
            P.dma(yr[b][:], yrT_d[:, :, cs].rearrange("c p t -> p c t"), reads=["yrT_d%d" % tt], writes=["yr%d" % b])
            P.dma(ya[b][:], yaT_d[:, :, cs].rearrange("c p t -> p c t"), reads=["yaT_d%d" % tt], writes=["ya%d" % b])
            P.dma(gt[b][:], gT_d[:, :, cs].rearrange("c p t -> p c t"), reads=["gT_d%d" % tt], writes=["gt%d" % b])
            P.dma(xt2[b][:], x_d[cs, :].rearrange("(j p) d -> p j d", p=128),
                  writes=["xt2%d" % b] + ["x1t%d_%d" % (b, s_) for s_ in range(4)])

        c_loads(0)
        for tt in range(NT):
            b = tt % 2
            cs = slice(tt * 512, (tt + 1) * 512)
            if tt + 1 < NT:
                c_loads(tt + 1)
            for mc in range(8):
                i = k % 2
                k += 1
                for kc in range(8):
                    P.pe(lambda e, kc=kc, mc=mc, i=i, b=b: e.matmul(
                        pA[i][:], lhsT=wor[:, kc, mc * 128:(mc + 1) * 128], rhs=yr[b][:, kc, :],
                        start=(kc == 0), stop=(kc == 7)),
                        reads=["wor%d" % kc, "yr%d" % b], writes=["@pA%d" % i])
                for kc in range(8):
                    P.pe(lambda e, kc=kc, mc=mc, i=i, b=b: e.matmul(
                        pB[i][:], lhsT=woa[:, kc, mc * 128:(mc + 1) * 128], rhs=ya[b][:, kc, :],
                        start=(kc == 0), stop=(kc == 7)),
                        reads=["woa%d" % kc, "ya%d" % b], writes=["@pB%d" % i])
                P.dve(lambda e, i=i, b=b, mc=mc: e.tensor_tensor(out=tmp[i][:], in0=pA[i][:], in1=gt[b][:, mc, :],
                                                                 op=ALU.mult),
                      reads=["@pA%d" % i, "gt%d" % b], writes=["tmpc%d" % i])
                P.dve(lambda e, i=i, b=b, mc=mc: e.tensor_tensor(out=tmp2[i][:], in0=pB[i][:], in1=gt[b][:, 8 + mc, :],
                                                                 op=ALU.mult),
                      reads=["@pB%d" % i, "gt%d" % b], writes=["tmpd%d" % i])
                P.pool(lambda e, i=i, b=b, mc=mc: e.tensor_tensor(out=mg[b][:, mc, :], in0=tmp[i][:], in1=tmp2[i][:],
                                                                  op=ALU.add),
                       reads=["tmpc%d" % i, "tmpd%d" % i], writes=["mg%d_%d" % (b, mc)])
            for sub in range(4):
                for half in range(2):
                    i = k % 2
                    k += 1
                    for mc in range(8):
                        P.pe(lambda e, mc=mc, sub=sub, half=half, i=i, b=b: e.matmul(
                            pO[i][:], lhsT=mg[b][:, mc, sub * 128:(sub + 1) * 128],
                            rhs=wout[:, mc, half * 512:(half + 1) * 512], start=(mc == 0), stop=(mc == 7)),
                            reads=["mg%d_%d" % (b, mc), "wout%d" % mc], writes=["@pO%d" % i])
                    P.dve(lambda e, sub=sub, half=half, i=i, b=b: e.tensor_tensor(
                        out=x1t[b][:, sub, half * 512:(half + 1) * 512], in0=pO[i][:],
                        in1=xt2[b][:, sub, half * 512:(half + 1) * 512], op=ALU.add),
                        reads=["@pO%d" % i, "xt2%d" % b], writes=["x1t%d_%d" % (b, sub)])
            P.dma(x1_d[cs, :].rearrange("(j p) d -> p j d", p=128), x1t[b][:],
                  reads=["x1t%d_%d" % (b, s_) for s_ in range(4)], writes=["x1_d%d" % tt])

    P.barrier(bscr[:, 3:4])
    if "C" in dbg:
        d_ = nc.dram_tensor("dbg_x1", [T, D], F32, kind="ExternalOutput").ap()
        P.dma(d_, x1_d, reads=["x1_d%d" % i for i in range(NT)], writes=["dbg_x1"])

    with ExitStack() as pd:
        w1 = sb("w1", [128, 8, 4 * D], BF16, pd)
        w2 = sb("w2", [128, 32, D], BF16, pd)
        for q4 in range(2):
            for kc in range(8):
                P.dma(w1[:, kc, q4 * 2048:(q4 + 1) * 2048], w1_d[kc * 128:(kc + 1) * 128, q4 * 2048:(q4 + 1) * 2048],
                      writes=["w1_%d_%d" % (kc, q4)], eng="gpsimd")
        for f in range(32):
            P.dma(w2[:, f, :], w2_d[f * 128:(f + 1) * 128, :], writes=["w2_%d" % f], eng="gpsimd")
        S2 = {
            "ssq": [sb("ssqD%d" % i, [128, 12], F32, pd) for i in range(2)],
            "junk": sb("junkD", [128, 1024], BF16, pd),
            "hb": [sb("hbD%d" % i, [128, 2, 1024], BF16, pd) for i in range(2)],
        }
        x1s = [sb("x1s%d" % i, [128, 2, D], F32, pd) for i in range(2)]
        h2T = [sb("h2T%d" % i, [128, 8, 256], BF16, pd) for i in range(2)]
        aT = sb("aT", [128, 32, 256], BF16, pd)
        rl = [sb("rl%d" % i, [128, 256], F32, pd) for i in range(2)]
        ot = x1s
        tpD = ps("tpD", [128, 3, 1024], BF16, pd)
        pF = [ps("@pF%d" % i, [128, 512], F32, pd) for i in range(3)]
        pG = [ps("@pG%d" % i, [128, 512], F32, pd) for i in range(2)]
        k = 0
        k2 = 0
        def d_prep(t2):
            b = t2 % 2
            rs_ = slice(t2 * 256, (t2 + 1) * 256)
            P.dma(x1s[b][:], x1_d[rs_, :].rearrange("(j p) d -> p j d", p=128), reads=["x1_d%d" % (t2 // 2)],
                  writes=["x1s%d" % b, "ot%d_0" % b, "ot%d_1" % b])
            norm_transpose(S2, x1s[b], "x1s%d" % b, 2,
                           lambda kc, b=b: (h2T[b][:, kc, :], "h2T%d_%d" % (b, kc)), V_N2, tpD, "D", t2)

        d_prep(0)
        for t2 in range(T // 256):
            b = t2 % 2
            rs_ = slice(t2 * 256, (t2 + 1) * 256)
            for f in range(32):
                i = k % 3
                k += 1
                for kc in range(8):
                    P.pe(lambda e, kc=kc, f=f, i=i, b=b: e.matmul(
                        pF[i][:, 0:256], lhsT=w1[:, kc, f * 128:(f + 1) * 128], rhs=h2T[b][:, kc, :],
                        start=(kc == 0), stop=(kc == 7)),
                        reads=["w1_%d_%d" % (kc, f // 16), "h2T%d_%d" % (b, kc)], writes=["@pF%d" % i])
                r_ = k % 2
                P.act(lambda e, i=i, r_=r_: e.activation(out=rl[r_][:], in_=pF[i][:, 0:256], func=AF.Relu),
                      reads=["@pF%d" % i], writes=["rl%d" % r_])
                P.pool(lambda e, f=f, r_=r_: e.tensor_tensor(out=aT[:, f, :], in0=rl[r_][:], in1=rl[r_][:], op=ALU.mult),
                       reads=["rl%d" % r_], writes=["aT%d" % f])
            if t2 + 1 < T // 256:
                d_prep(t2 + 1)
            for sub in range(2):
                for half in range(2):
                    i = k2 % 2
                    k2 += 1
                    for f in range(32):
                        P.pe(lambda e, f=f, sub=sub, half=half, i=i: e.matmul(
                            pG[i][:], lhsT=aT[:, f, sub * 128:(sub + 1) * 128],
                            rhs=w2[:, f, half * 512:(half + 1) * 512], start=(f == 0), stop=(f == 31)),
                            reads=["aT%d" % f, "w2_%d" % f], writes=["@pG%d" % i])
                    P.dve(lambda e, sub=sub, half=half, i=i, b=b: e.tensor_tensor(
                        out=ot[b][:, sub, half * 512:(half + 1) * 512], in0=pG[i][:],
                        in1=x1s[b][:, sub, half * 512:(half + 1) * 512], op=ALU.add),
                        reads=["@pG%d" % i, "x1s%d" % b], writes=["ot%d_%d" % (b, sub)])
            P.dma(out_d[rs_, :].rearrange("(j p) d -> p j d", p=128), ot[b][:],
                  reads=["ot%d_0" % b, "ot%d_1" % b], writes=["out%d" % t2])

    P.emit()
    es.close()
    return nc, P


_CACHE = {}


def kernel(**inputs):
    x = np.asarray(inputs["x"], np.float32)
    B, T, _ = x.shape
    shared = prep_shared(inputs)
    if T not in _CACHE:
        _CACHE[T] = build(T)
    nc, _ = _CACHE[T]
    in_maps = []
    for b in range(B):
        m = dict(shared)
        m["x"] = np.ascontiguousarray(x[b])
        in_maps.append(m)
    res = run_bass_kernel_spmd(nc, in_maps, core_ids=list(range(B)))
    return np.stack([np.asarray(r["out"], np.float32) for r in res.results], axis=0)
```
